# Optimizing a Trainium2 kernel written in Bass

```python
import math
import jax, jax.numpy as jnp
from jax import lax
import numpy as np

D_MODEL = 1024
BATCH = 32
SEQ = 256
DEPTH = 4
DEC_BATCH = 4
DEC_SEQ = 2048
PAST_LEN = 512

GRID_W = 64
NA_HEADS = 8
NA_DH = 64
NA_ROWS = 8
NA_COLS = 16
NA_QCOL_BLOCK = 16
NA_KCOL_BLOCK = 32
DIFF_HEADS = 4
DIFF_DH = 64
RET_HEADS = 8
RET_DH = 64
RET_CHUNK = 128
LRU_WIDTH = 512
LRU_BLOCKS = 8
LRU_CONV = 4
LRU_C = 8.0
N_BRANCH = 4
BRANCH_W = 512
D_FF = 4 * D_MODEL
ROPE_BASE = 10000.0
QUERY_BLOCK = 128
EPS = 1e-6
NEG_INF = -1e30

NA_W = NA_HEADS * NA_DH
DIFF_QK_W = 2 * DIFF_HEADS * DIFF_DH
DIFF_V_W = DIFF_HEADS * 2 * DIFF_DH
RET_W = RET_HEADS * RET_DH
IN_WIDTHS = (NA_W, NA_W, NA_W, DIFF_QK_W, DIFF_QK_W, DIFF_V_W, RET_W, RET_W, RET_W, RET_W, LRU_WIDTH, LRU_WIDTH, N_BRANCH * D_MODEL)
IN_WIDTH = sum(IN_WIDTHS)

kernel_name = "hybrid_flow_backbone_ctx_prefix_step"


def rmsnorm(x, g):
    xf = x.astype(jnp.float32)
    y = xf * lax.rsqrt(jnp.mean(jnp.square(xf), axis=-1, keepdims=True) + EPS)
    return (y * g.astype(jnp.float32)).astype(x.dtype)


def modulation(cond, ada_w, ada_b):
    m = jax.nn.silu(cond) @ ada_w + ada_b
    return [part[:, None, :] for part in jnp.split(m, 6, axis=-1)]


def split_cols(z):
    outs = []
    start = 0
    for w in IN_WIDTHS:
        outs.append(z[..., start:start + w])
        start += w
    return outs


def split_heads(z, n_heads):
    b, t, _ = z.shape
    return z.reshape(b, t, n_heads, -1).transpose(0, 2, 1, 3)


def merge_heads(z):
    b, h, t, d = z.shape
    return z.transpose(0, 2, 1, 3).reshape(b, t, h * d)


def diff_qk_heads(z):
    b, t, _ = z.shape
    return z.reshape(b, t, 2, DIFF_HEADS, DIFF_DH).transpose(0, 2, 3, 1, 4)


def axial_angles(n, dim):
    t = jnp.arange(n)
    quarter = dim // 4
    inv = ROPE_BASE ** (-jnp.arange(quarter, dtype=jnp.float32) / quarter)
    ang_r = (t // GRID_W).astype(jnp.float32)[:, None] * inv
    ang_c = (t % GRID_W).astype(jnp.float32)[:, None] * inv
    return ang_r, ang_c


def rotate(x, ang):
    x1, x2 = jnp.split(x, 2, axis=-1)
    cos = jnp.cos(ang).astype(x.dtype)
    sin = jnp.sin(ang).astype(x.dtype)
    return jnp.concatenate([x1 * cos - x2 * sin, x2 * cos + x1 * sin], axis=-1)


def axial_rope(x, ang_r, ang_c):
    xr, xc = jnp.split(x, 2, axis=-1)
    return jnp.concatenate([rotate(xr, ang_r), rotate(xc, ang_c)], axis=-1)


def dense_attention(q, k, v):
    b, h, nq, d = q.shape
    nb = nq // QUERY_BLOCK
    scale = d ** -0.5
    qb = q.reshape(b, h, nb, QUERY_BLOCK, d).transpose(2, 0, 1, 3, 4)

    def one_block(qi):
        s = jnp.einsum('bhqd,bhkd->bhqk', qi, k).astype(jnp.float32) * scale
        p = jax.nn.softmax(s, axis=-1).astype(v.dtype)
        return jnp.einsum('bhqk,bhkd->bhqd', p, v)

    o = lax.map(one_block, qb)
    return o.transpose(1, 2, 0, 3, 4).reshape(b, h, nq, d)


def diff_attention(q, k, v, lam):
    b, _, h, nq, d = q.shape
    nb = nq // QUERY_BLOCK
    scale = d ** -0.5
    qb = q.reshape(b, 2, h, nb, QUERY_BLOCK, d).transpose(3, 0, 1, 2, 4, 5)

    def one_block(qi):
        s = jnp.einsum('bchqd,bchkd->bchqk', qi, k).astype(jnp.float32) * scale
        p = jax.nn.softmax(s, axis=-1)
        a = (p[:, 0] - lam * p[:, 1]).astype(v.dtype)
        return jnp.einsum('bhqk,bhkd->bhqd', a, v)

    o = lax.map(one_block, qb)
    return o.transpose(1, 2, 0, 3, 4).reshape(b, h, nq, v.shape[-1])


def neighbourhood_attention(q, k, v, k_ctx, v_ctx, rpb):
    b, h, n, d = q.shape
    rows = n // GRID_W
    kr = min(NA_ROWS, rows)
    ncb = GRID_W // NA_QCOL_BLOCK
    nloc = kr * NA_KCOL_BLOCK
    scale = d ** -0.5
    qcol = np.arange(GRID_W).reshape(ncb, NA_QCOL_BLOCK)
    kc0 = np.clip(np.arange(ncb) * NA_QCOL_BLOCK - NA_COLS // 2, 0, GRID_W - NA_KCOL_BLOCK)
    kcol = kc0[:, None] + np.arange(NA_KCOL_BLOCK)[None, :]
    cstart = np.clip(qcol - NA_COLS // 2, 0, GRID_W - NA_COLS)
    col_ok = jnp.asarray((kcol[:, None, :] >= cstart[:, :, None]) & (kcol[:, None, :] < cstart[:, :, None] + NA_COLS))
    dc_idx = np.clip(kcol[:, None, :] - qcol[:, :, None] + NA_COLS - 1, 0, 2 * NA_COLS - 2)
    bias_cols = rpb[:, :, dc_idx]
    kg = k.reshape(b, h, rows, GRID_W, d)
    vg = v.reshape(b, h, rows, GRID_W, d)
    qg = q.reshape(b, h, rows, ncb, NA_QCOL_BLOCK, d).transpose(2, 0, 1, 3, 4, 5)

    def one_row(args):
        r, qr = args
        rs = jnp.clip(r - kr // 2, 0, rows - kr)
        k_win = lax.dynamic_slice_in_dim(kg, rs, kr, axis=2)[:, :, :, kcol].transpose(0, 1, 3, 2, 4, 5)
        v_win = lax.dynamic_slice_in_dim(vg, rs, kr, axis=2)[:, :, :, kcol].transpose(0, 1, 3, 2, 4, 5)
        dr = rs + jnp.arange(kr) - r + NA_ROWS - 1
        bias = bias_cols[:, dr].transpose(0, 2, 3, 1, 4).astype(jnp.float32)
        s_loc = jnp.einsum('bhjqd,bhjikd->bhjqik', qr, k_win).astype(jnp.float32) * scale + bias
        s_loc = jnp.where(col_ok[:, :, None, :], s_loc, NEG_INF)
        s_ctx = jnp.einsum('bhjqd,bhpd->bhjqp', qr, k_ctx).astype(jnp.float32) * scale
        s = jnp.concatenate([s_loc.reshape(b, h, ncb, NA_QCOL_BLOCK, nloc), s_ctx], axis=-1)
        p = jax.nn.softmax(s, axis=-1).astype(v.dtype)
        o_loc = jnp.einsum('bhjqn,bhjnd->bhjqd', p[..., :nloc], v_win.reshape(b, h, ncb, nloc, d))
        o_ctx = jnp.einsum('bhjqp,bhpd->bhjqd', p[..., nloc:], v_ctx)
        return o_loc + o_ctx

    o = lax.map(one_row, (jnp.arange(rows), qg))
    return o.transpose(1, 2, 0, 3, 4, 5).reshape(b, h, n, d)


def retention_chunkwise(q, k, v, log_g, s0, include_diag):
    b, h, t, _ = q.shape
    dv = v.shape[-1]
    nc = t // RET_CHUNK
    f32 = jnp.float32
    pos = jnp.arange(RET_CHUNK, dtype=f32)
    rel = pos[:, None] - pos[None, :]
    mask = rel >= 0 if include_diag else rel > 0
    lg = log_g.astype(f32)
    dmat = jnp.where(mask, jnp.exp(jnp.where(mask, rel, 0.0) * lg[:, None, None]), 0.0)
    xi = jnp.exp((pos + 1.0) * lg[:, None])[..., None]
    zeta = jnp.exp((RET_CHUNK - 1.0 - pos) * lg[:, None])[..., None]
    g_chunk = jnp.exp(RET_CHUNK * lg)[:, None, None]

    def chunks(z):
        return z.astype(f32).reshape(b, h, nc, RET_CHUNK, -1).transpose(2, 0, 1, 3, 4)

    def step(s, qkv):
        qc, kc, vc = qkv
        inner = jnp.einsum('bhnd,bhmd->bhnm', qc, kc) * dmat
        o = jnp.einsum('bhnm,bhme->bhne', inner, vc) + jnp.einsum('bhnd,bhde->bhne', qc, s) * xi
        s = g_chunk * s + jnp.einsum('bhmd,bhme->bhde', kc * zeta, vc)
        return s, o

    s_fin, o = lax.scan(step, s0.astype(f32), (chunks(q), chunks(k), chunks(v)))
    return o.transpose(1, 2, 0, 3, 4).reshape(b, h, t, dv), s_fin


def retention_bidir(q, k, v, lg_f, lg_b, s_f0, s_b0):
    o_f, s_f = retention_chunkwise(q, k, v, lg_f, s_f0, True)
    o_b, s_b = retention_chunkwise(q[:, :, ::-1], k[:, :, ::-1], v[:, :, ::-1], lg_b, s_b0, False)
    return o_f + o_b[:, :, ::-1], s_f, s_b


def centred_dwconv(x, w, bias):
    ch = x.shape[-1]
    y = lax.conv_general_dilated(x, w[:, None, :].astype(x.dtype), window_strides=(1,),
                                 padding=[((LRU_CONV - 1) // 2, LRU_CONV // 2)],
                                 dimension_numbers=('NWC', 'WIO', 'NWC'), feature_group_count=ch)
    return y + bias


def rglru_gates(x, wa, ba, wx, bx, lam):
    b, t, w = x.shape
    xf = x.astype(jnp.float32)
    xb = xf.reshape(b, t, LRU_BLOCKS, -1)
    r = jax.nn.sigmoid(jnp.einsum('btnd,nde->btne', xb, wa.astype(jnp.float32)).reshape(b, t, w) + ba.astype(jnp.float32))
    i = jax.nn.sigmoid(jnp.einsum('btnd,nde->btne', xb, wx.astype(jnp.float32)).reshape(b, t, w) + bx.astype(jnp.float32))
    log_a = -LRU_C * r * jax.nn.softplus(-lam.astype(jnp.float32))
    a = jnp.exp(log_a)
    u = jnp.sqrt(-jnp.expm1(2.0 * log_a)) * (i * xf)
    return a, u


def linear_scan(a, u, h0):
    u = u.at[:, 0].add(a[:, 0] * h0.astype(jnp.float32))

    def combine(left, right):
        a_l, u_l = left
        a_r, u_r = right
        return a_l * a_r, a_r * u_l + u_r

    _, hs = lax.associative_scan(combine, (a, u), axis=1)
    return hs


def rglru_bidir(x, p_fwd, p_bwd, h_f0, h_b0):
    a, u = rglru_gates(x, *p_fwd)
    h_f = linear_scan(a, u, h_f0)
    a, u = rglru_gates(x[:, ::-1], *p_bwd)
    h_b = linear_scan(a, u, h_b0)
    return h_f + h_b[:, ::-1], h_f[:, -1], h_b[:, -1]


def merge_branches(ys, gate_logits, w_branch, w_out):
    y = jnp.stack(ys, axis=-2)
    proj = jnp.einsum('btkw,kwd->btkd', y, w_branch)
    g = jax.nn.sigmoid(gate_logits.reshape(gate_logits.shape[:-1] + (N_BRANCH, D_MODEL)))
    return jnp.sum(g * proj, axis=-2) @ w_out


def mixer(h, P, l, ctx):
    b, t, _ = h.shape
    (na_q, na_k, na_v, df_q, df_k, df_v, rt_q, rt_k, rt_v, rt_g, lr_x, lr_g, gate_logits) = split_cols(h @ P['w_in'])
    qa, ka, va = split_heads(na_q, NA_HEADS), split_heads(na_k, NA_HEADS), split_heads(na_v, NA_HEADS)
    qb, kb, vb = diff_qk_heads(df_q), diff_qk_heads(df_k), split_heads(df_v, DIFF_HEADS)
    lam_init = 0.8 - 0.6 * math.exp(-0.3 * l)
    lam = (jnp.exp(jnp.sum(P['diff_lq1'].astype(jnp.float32) * P['diff_lk1'].astype(jnp.float32)))
           - jnp.exp(jnp.sum(P['diff_lq2'].astype(jnp.float32) * P['diff_lk2'].astype(jnp.float32))) + lam_init)
    qc = split_heads(rt_q, RET_HEADS)
    kc = split_heads(rt_k, RET_HEADS) * (RET_DH ** -0.5)
    vc = split_heads(rt_v, RET_HEADS)
    lg_f = jnp.log1p(-jnp.exp(P['ret_theta_fwd'].astype(jnp.float32)))
    lg_b = jnp.log1p(-jnp.exp(P['ret_theta_bwd'].astype(jnp.float32)))
    xd = centred_dwconv(lr_x, P['lru_conv_w'], P['lru_conv_b'])
    if ctx is None:
        ya_h = dense_attention(qa, ka, va)
        yb_h = diff_attention(qb, kb, vb, lam)
        s_rf0 = jnp.zeros((b, RET_HEADS, RET_DH, RET_DH), jnp.float32)
        s_rb0 = jnp.zeros((b, RET_HEADS, RET_DH, RET_DH), jnp.float32)
        h_lf0 = jnp.zeros((b, LRU_WIDTH), jnp.float32)
        h_lb0 = jnp.zeros((b, LRU_WIDTH), jnp.float32)
    else:
        (c_na_k, c_na_v, c_df_k, c_df_v, s_rf0, s_rb0, h_lf0, h_lb0) = ctx
        ya_h = neighbourhood_attention(qa, ka, va, c_na_k, c_na_v, P['na_rpb'])
        ang_r, ang_c = axial_angles(t, DIFF_DH)
        qb_rot = axial_rope(qb, ang_r, ang_c)
        kb_rot = axial_rope(kb, ang_r, ang_c)
        yb_h = diff_attention(qb_rot, jnp.concatenate([kb_rot, c_df_k], axis=3),
                              jnp.concatenate([vb, c_df_v], axis=2), lam)
    oc, s_rf, s_rb = retention_bidir(qc, kc, vc, lg_f, lg_b, s_rf0, s_rb0)
    hd, h_lf, h_lb = rglru_bidir(xd, P['lru_fwd'], P['lru_bwd'], h_lf0, h_lb0)
    ya = merge_heads(ya_h)
    yb = merge_heads(rmsnorm(yb_h, P['diff_norm']) * (1.0 - lam_init))
    yc = merge_heads(rmsnorm(oc, P['ret_norm'].reshape(RET_HEADS, 1, RET_DH)).astype(h.dtype)) * jax.nn.silu(rt_g)
    yd = hd.astype(h.dtype) * jax.nn.gelu(lr_g)
    out = merge_branches((ya, yb, yc, yd), gate_logits, P['w_branch'], P['w_out'])
    return out, (ka, va, kb, vb, s_rf, s_rb, h_lf, h_lb)


def trunk_layer(x, cond, P, l, ctx):
    sh1, sc1, g1, sh2, sc2, g2 = modulation(cond, P['ada_w'], P['ada_b'])
    h = rmsnorm(x, P['norm_mix_pre']) * (1.0 + sc1) + sh1
    y, ctx_out = mixer(h, P, l, ctx)
    x = x + g1 * rmsnorm(y, P['norm_mix_post'])
    h = rmsnorm(x, P['norm_ffn_pre']) * (1.0 + sc2) + sh2
    y = jnp.square(jax.nn.relu(h @ P['mlp_w1'])) @ P['mlp_w2']
    x = x + g2 * rmsnorm(y, P['norm_ffn_post'])
    return x, ctx_out


def setup_inputs(seed: int = 0) -> dict:
    key = jax.random.key(seed)
    keys = iter(jax.random.split(key, 64))
    D = D_MODEL

    def nrm(shape, scale=1.0):
        return jax.random.normal(next(keys), shape, jnp.float32) * scale

    def gain(shape):
        return 1.0 + nrm(shape, 0.05)

    def lru_lambda():
        u = jax.random.uniform(next(keys), (DEPTH, LRU_WIDTH), jnp.float32, 0.9, 0.999)
        s = u ** (1.0 / LRU_C)
        return jnp.log(s) - jnp.log1p(-s)

    theta0 = jnp.linspace(math.log(1.0 / 32.0), math.log(1.0 / 512.0), RET_HEADS, dtype=jnp.float32)
    bw = LRU_WIDTH // LRU_BLOCKS
    return {
        'x_prompt': nrm((BATCH, SEQ, D)),
        'x_sample': nrm((DEC_BATCH, DEC_SEQ, D)),
        'cache_na_k': nrm((DEC_BATCH, DEPTH, NA_HEADS, PAST_LEN, NA_DH)),
        'cache_na_v': nrm((DEC_BATCH, DEPTH, NA_HEADS, PAST_LEN, NA_DH)),
        'cache_diff_k': nrm((DEC_BATCH, DEPTH, 2, DIFF_HEADS, PAST_LEN, DIFF_DH)),
        'cache_diff_v': nrm((DEC_BATCH, DEPTH, DIFF_HEADS, PAST_LEN, 2 * DIFF_DH)),
        'state_ret_fwd': nrm((DEC_BATCH, DEPTH, RET_HEADS, RET_DH, RET_DH), 0.5),
        'state_ret_bwd': nrm((DEC_BATCH, DEPTH, RET_HEADS, RET_DH, RET_DH), 0.5),
        'state_lru_fwd': nrm((DEC_BATCH, DEPTH, LRU_WIDTH), 0.5),
        'state_lru_bwd': nrm((DEC_BATCH, DEPTH, LRU_WIDTH), 0.5),
        'c': nrm((DEC_BATCH, D)),
        'c_ctx': nrm((D,)),
        'ada_w': nrm((DEPTH, D, 6 * D), 0.5 * D ** -0.5),
        'ada_b': nrm((DEPTH, 6 * D), 0.02),
        'norm_mix_pre': gain((DEPTH, D)),
        'norm_mix_post': gain((DEPTH, D)),
        'norm_ffn_pre': gain((DEPTH, D)),
        'norm_ffn_post': gain((DEPTH, D)),
        'w_in': nrm((DEPTH, D, IN_WIDTH), D ** -0.5),
        'na_rpb': nrm((DEPTH, NA_HEADS, 2 * NA_ROWS - 1, 2 * NA_COLS - 1), 0.5),
        'diff_lq1': nrm((DEPTH, DIFF_DH), 0.1),
        'diff_lk1': nrm((DEPTH, DIFF_DH), 0.1),
        'diff_lq2': nrm((DEPTH, DIFF_DH), 0.1),
        'diff_lk2': nrm((DEPTH, DIFF_DH), 0.1),
        'diff_norm': gain((DEPTH, 2 * DIFF_DH)),
        'ret_theta_fwd': theta0 + nrm((DEPTH, RET_HEADS), 0.05),
        'ret_theta_bwd': theta0 + nrm((DEPTH, RET_HEADS), 0.05),
        'ret_norm': gain((DEPTH, RET_W)),
        'lru_conv_w': nrm((DEPTH, LRU_CONV, LRU_WIDTH), LRU_CONV ** -0.5),
        'lru_conv_b': nrm((DEPTH, LRU_WIDTH), 0.02),
        'lru_wa_fwd': nrm((DEPTH, LRU_BLOCKS, bw, bw), bw ** -0.5),
        'lru_ba_fwd': nrm((DEPTH, LRU_WIDTH), 0.02),
        'lru_wx_fwd': nrm((DEPTH, LRU_BLOCKS, bw, bw), bw ** -0.5),
        'lru_bx_fwd': nrm((DEPTH, LRU_WIDTH), 0.02),
        'lru_lam_fwd': lru_lambda(),
        'lru_wa_bwd': nrm((DEPTH, LRU_BLOCKS, bw, bw), bw ** -0.5),
        'lru_ba_bwd': nrm((DEPTH, LRU_WIDTH), 0.02),
        'lru_wx_bwd': nrm((DEPTH, LRU_BLOCKS, bw, bw), bw ** -0.5),
        'lru_bx_bwd': nrm((DEPTH, LRU_WIDTH), 0.02),
        'lru_lam_bwd': lru_lambda(),
        'w_branch': nrm((DEPTH, N_BRANCH, BRANCH_W, D), BRANCH_W ** -0.5),
        'w_out': nrm((DEPTH, D, D), D ** -0.5),
        'mlp_w1': nrm((DEPTH, D, D_FF), D ** -0.5),
        'mlp_w2': nrm((DEPTH, D_FF, D), D_FF ** -0.5),
    }


def reference(x_prompt, x_sample, cache_na_k, cache_na_v, cache_diff_k, cache_diff_v,
              state_ret_fwd, state_ret_bwd, state_lru_fwd, state_lru_bwd, c, c_ctx,
              ada_w, ada_b, norm_mix_pre, norm_mix_post, norm_ffn_pre, norm_ffn_post, w_in, na_rpb,
              diff_lq1, diff_lk1, diff_lq2, diff_lk2, diff_norm, ret_theta_fwd, ret_theta_bwd, ret_norm,
              lru_conv_w, lru_conv_b, lru_wa_fwd, lru_ba_fwd, lru_wx_fwd, lru_bx_fwd, lru_lam_fwd,
              lru_wa_bwd, lru_ba_bwd, lru_wx_bwd, lru_bx_bwd, lru_lam_bwd,
              w_branch, w_out, mlp_w1, mlp_w2):
    y_p = x_prompt
    y_s = x_sample
    ctx_outs = []
    for l in range(DEPTH):
        P = {
            'ada_w': ada_w[l], 'ada_b': ada_b[l],
            'norm_mix_pre': norm_mix_pre[l], 'norm_mix_post': norm_mix_post[l],
            'norm_ffn_pre': norm_ffn_pre[l], 'norm_ffn_post': norm_ffn_post[l],
            'w_in': w_in[l], 'na_rpb': na_rpb[l],
            'diff_lq1': diff_lq1[l], 'diff_lk1': diff_lk1[l], 'diff_lq2': diff_lq2[l], 'diff_lk2': diff_lk2[l],
            'diff_norm': diff_norm[l],
            'ret_theta_fwd': ret_theta_fwd[l], 'ret_theta_bwd': ret_theta_bwd[l], 'ret_norm': ret_norm[l],
            'lru_conv_w': lru_conv_w[l], 'lru_conv_b': lru_conv_b[l],
            'lru_fwd': (lru_wa_fwd[l], lru_ba_fwd[l], lru_wx_fwd[l], lru_bx_fwd[l], lru_lam_fwd[l]),
            'lru_bwd': (lru_wa_bwd[l], lru_ba_bwd[l], lru_wx_bwd[l], lru_bx_bwd[l], lru_lam_bwd[l]),
            'w_branch': w_branch[l], 'w_out': w_out[l],
            'mlp_w1': mlp_w1[l], 'mlp_w2': mlp_w2[l],
        }
        y_p, ctx_l = trunk_layer(y_p, c_ctx[None, :], P, l, None)
        ctx_outs.append(ctx_l)
        cached = (cache_na_k[:, l], cache_na_v[:, l], cache_diff_k[:, l], cache_diff_v[:, l],
                  state_ret_fwd[:, l], state_ret_bwd[:, l], state_lru_fwd[:, l], state_lru_bwd[:, l])
        y_s, _ = trunk_layer(y_s, c, P, l, cached)
    new_na_k = jnp.stack([o[0] for o in ctx_outs], axis=1)
    new_na_v = jnp.stack([o[1] for o in ctx_outs], axis=1)
    new_diff_k = jnp.stack([o[2] for o in ctx_outs], axis=1)
    new_diff_v = jnp.stack([o[3] for o in ctx_outs], axis=1)
    new_ret_fwd = jnp.stack([o[4] for o in ctx_outs], axis=1)
    new_ret_bwd = jnp.stack([o[5] for o in ctx_outs], axis=1)
    new_lru_fwd = jnp.stack([o[6] for o in ctx_outs], axis=1)
    new_lru_bwd = jnp.stack([o[7] for o in ctx_outs], axis=1)
    return (y_p, y_s, new_na_k, new_na_v, new_diff_k, new_diff_v, new_ret_fwd, new_ret_bwd, new_lru_fwd, new_lru_bwd)
```

```python
import math
import os
import numpy as np
import concourse.bass as bass
import concourse.mybir as mybir
from concourse.bass_utils import run_bass_kernel_spmd
from contextlib import ExitStack

F32 = mybir.dt.float32
BF16 = mybir.dt.bfloat16
AF = mybir.ActivationFunctionType
ALU = mybir.AluOpType
_DT_SIZE = {F32: 4, BF16: 2}
ENGS = ("pe", "act", "dve", "pool", "sp")
N_DMA_SEMS = 8

DEPTH = 4
D = 1024
T = 2048
NCH = 8
TB = 512
NTB = 4
NTT = 16
EPS = 1e-6
KIB = 256


def _box(ap, tracked_dram):
    if not hasattr(ap, "tensor"):
        ap = ap[:]
    t = ap.tensor
    pat = ap.ap
    off = ap.offset
    sz = _DT_SIZE[ap.dtype]
    if str(ap.space) == "DRAM":
        if t.name not in tracked_dram:
            return None
        lo = hi = off
        for st, cn in pat:
            if st < 0:
                lo += st * (cn - 1)
            else:
                hi += st * (cn - 1)
        return (t.name, 0, 1, lo * sz, (hi + 1) * sz)
    if str(ap.space) == "PSUM":
        return (t.name, 0, 128, 0, 2048)
    pstep, pcnt = pat[0]
    if pstep <= 0:
        pstep = 1 << 40
    p0 = off // pstep
    f = off % pstep
    lo = hi = f
    for st, cn in pat[1:]:
        if st < 0:
            lo += st * (cn - 1)
        else:
            hi += st * (cn - 1)
    return (t.name, p0, p0 + pcnt, lo * sz, (hi + 1) * sz)


def _overlap(a, b):
    return a[1] < b[2] and b[1] < a[2] and a[3] < b[4] and b[3] < a[4]


def _covers(a, b):
    return a[1] <= b[1] and a[2] >= b[2] and a[3] <= b[3] and a[4] >= b[4]


class Op:
    __slots__ = ("eng", "fn", "deps", "idx", "lidx", "is_dma", "sig", "sigval", "dsem", "dval")

    def __init__(self, eng, fn, is_dma):
        self.eng = eng
        self.fn = fn
        self.is_dma = is_dma
        self.deps = set()
        self.sig = False
        self.sigval = 0
        self.dsem = None
        self.dval = 0


class Sched:
    def __init__(self):
        self.nc = bass.Bass("TRN2", target_bir_lowering=False)
        self.ops = []
        self.streams = {e: [] for e in ENGS}
        self.recs = {}
        self.frozen = set()
        self.tracked_dram = set()
        self.stack = ExitStack()
        self.dma_count = {e: 0 for e in ENGS}
        self.dma_last = {}

    def dram_in(self, name, shape, dtype=F32):
        return self.nc.dram_tensor(name, list(shape), dtype, kind="ExternalInput").ap()

    def dram_out(self, name, shape, dtype=F32):
        return self.nc.dram_tensor(name, list(shape), dtype, kind="ExternalOutput").ap()

    def sb(self, name, shape, dtype=F32):
        return self.stack.enter_context(self.nc.sbuf_tensor(name, list(shape), dtype))

    def ps(self, name, shape, dtype=F32):
        return self.stack.enter_context(self.nc.psum_tensor(name, list(shape), dtype))

    def add(self, eng, fn, reads=(), writes=(), dma=False):
        op = Op(eng, fn, dma)
        op.idx = len(self.ops)
        op.lidx = len(self.streams[eng])
        self.ops.append(op)
        self.streams[eng].append(op)
        td = self.tracked_dram
        rboxes = [b for b in (_box(ap, td) for ap in reads) if b is not None]
        wboxes = [b for b in (_box(ap, td) for ap in writes) if b is not None]
        for b in rboxes:
            is_psum = b[0].startswith("ps") and b[4] - b[3] == 2048 and b[2] - b[1] == 128
            for r in self.recs.get(b[0], ()):
                if r[2]:
                    if _overlap(r[0], b):
                        op.deps.add(r[1])
                elif is_psum and r[3] != eng:
                    op.deps.add(r[1])
        for b in wboxes:
            for r in self.recs.get(b[0], ()):
                if _overlap(r[0], b):
                    op.deps.add(r[1])
        for b in wboxes:
            lst = self.recs.setdefault(b[0], [])
            keep = [r for r in lst if not (_overlap(r[0], b) and _covers(b, r[0]))]
            keep.append([b, op.idx, True, eng])
            self.recs[b[0]] = keep
        for b in rboxes:
            if b[0] in self.frozen:
                continue
            lst = self.recs.setdefault(b[0], [])
            if not dma:
                for r in lst:
                    if (not r[2]) and r[3] == eng and r[0] == b and not self.ops[r[1]].is_dma:
                        r[1] = op.idx
                        break
                else:
                    lst.append([b, op.idx, False, eng])
            else:
                lst.append([b, op.idx, False, eng])
        if dma:
            n = self.dma_count[eng]
            self.dma_count[eng] = n + 1
            slot = n % N_DMA_SEMS
            op.dsem = (eng, slot)
            op.dval = 16 * (n // N_DMA_SEMS + 1)
            prev = self.dma_last.get((eng, slot))
            if prev is not None:
                op.deps.add(prev.idx)
            self.dma_last[(eng, slot)] = op
        return op

    def emit(self):
        nc = self.nc
        ops = self.ops
        for op in ops:
            for d in list(op.deps):
                dop = ops[d]
                if dop.is_dma:
                    continue
                if dop.eng == op.eng and not op.is_dma:
                    if dop.eng == "pe":
                        op.deps.discard(d)
                        continue
                dop.sig = True
        for e in ENGS:
            c = 0
            for op in self.streams[e]:
                if (not op.is_dma) and op.sig:
                    c += 1
                    op.sigval = c
        st = self.stack
        esem = {e: st.enter_context(nc.semaphore("s_" + e)) for e in ENGS}
        dsem = {}
        for e in ENGS:
            if self.dma_count[e]:
                for s in range(N_DMA_SEMS):
                    dsem[(e, s)] = st.enter_context(nc.semaphore("d_%s%d" % (e, s)))
        block = st.enter_context(nc.Block())
        handles = {"pe": block.tensor, "act": block.scalar, "dve": block.vector, "pool": block.gpsimd, "sp": block.sync}
        self.n_waits = 0

        def make(e):
            stream = self.streams[e]

            def body(eng):
                waited = {}
                for op in stream:
                    need = {}
                    for d in op.deps:
                        dop = ops[d]
                        if dop.is_dma:
                            key = ("d",) + dop.dsem
                            val = dop.dval
                            sem = dsem[dop.dsem]
                        else:
                            key = ("e", dop.eng)
                            val = dop.sigval
                            sem = esem[dop.eng]
                        if val > need.get(key, (0, None))[0]:
                            need[key] = (val, sem)
                    for key, (val, sem) in need.items():
                        if waited.get(key, 0) >= val:
                            continue
                        waited[key] = val
                        eng.wait_ge(sem, val)
                        self.n_waits += 1
                    ins = op.fn(eng)
                    if op.is_dma:
                        ins.then_inc(dsem[op.dsem], 16)
                    elif op.sig:
                        ins.then_inc(esem[e], 1)
                for (ee, s), last in self.dma_last.items():
                    if ee == e:
                        eng.wait_ge(dsem[(ee, s)], last.dval)
            return body

        for e in ENGS:
            if self.streams[e]:
                handles[e](make(e))
        st.close()
        return nc


LAM_INIT = [0.8 - 0.6 * math.exp(-0.3 * l) for l in range(DEPTH)]
CF_RELF, CF_RELB, CF_POS1, CF_POSB, CF_SPERM, CF_MPOSF, CF_MPOSB, CF_N = 0, 128, 256, 384, 512, 640, 648, 656
CB_UM, CB_LM, CB_BD, CB_COL, CB_ONES, CB_N = 0, 128, 256, 384, 448, 576
VON = 8192
CVON = 2048
NSTG = 6


def build(depth=DEPTH, dbg=False, parts="ABCD", stop=None):
    S = Sched()
    nc = S.nc
    di, do = S.dram_in, S.dram_out
    xin = di("xin", [NTB, 128, 8, TB])
    yres = do("yres", [NTB, 128, 8, TB])
    S.tracked_dram.add("yres")
    cond = di("cond", [128, 8])
    flags_d = di("flags", [128, 2])
    rope_d = di("rope", [128, NTB, 2, TB])
    rfl_d = di("rfl", [128, 2, 16])
    ada_w = di("ada_w", [DEPTH, 12, 128, 8, 512])
    w_in = di("w_in", [DEPTH, 20, 128, 8, 512])
    w_br = di("w_br", [DEPTH, 4, 2, 128, 4, 512])
    w_out = di("w_out", [DEPTH, 2, 128, 8, 512])
    w1 = di("w1", [DEPTH, 8, 128, 8, 512])
    w2 = di("w2", [DEPTH, 4, 2, 128, 8, 512])
    adab_d = di("adab", [128, DEPTH, 48])
    gains_d = di("gains", [128, DEPTH, 4, 8])
    rpbp = di("rpbp", [DEPTH, 8, 16, 128])
    dlam_d = di("dlam", [128, DEPTH, 4, 64])
    dnorm_d = di("dnorm", [128, DEPTH])
    rtheta_d = di("rtheta", [128, DEPTH, 2, 8])
    rnorm_d = di("rnorm", [128, DEPTH, 4])
    convw_d = di("convw", [128, DEPTH, 4, 4])
    convb_d = di("convb", [128, DEPTH, 4])
    lruw_d = di("lruw", [DEPTH, 128, 4, 4, 128])
    lrub_d = di("lrub", [128, DEPTH, 6, 4])
    c_nakT = di("c_nakT", [DEPTH, 128, 4, 512])
    c_nav = di("c_nav", [DEPTH, 128, 4, 512])
    c_dfkT = di("c_dfkT", [DEPTH, 128, 4, 512])
    c_dfv = di("c_dfv", [DEPTH, 128, 4, 512])
    s_rf0 = di("s_rf0", [DEPTH, 128, 256])
    s_rb0 = di("s_rb0", [DEPTH, 128, 256])
    h_l0_d = di("h_l0", [128, DEPTH, 2, 4])
    cf_d = di("cf", [128, CF_N])
    cb_d = di("cb", [128, CB_N])

    o_nakT = do("o_nakT", [DEPTH, 128, 4, T])
    o_nav = do("o_nav", [DEPTH, NTT, 128, 512])
    o_dfkT = do("o_dfkT", [DEPTH, 128, 4, T])
    o_dfv = do("o_dfv", [DEPTH, NTT, 128, 512])
    o_srf = do("o_srf", [DEPTH, 8, 128, 256])
    o_srb = do("o_srb", [DEPTH, 8, 128, 256])
    o_hl = do("o_hl", [DEPTH, 128, 2, 4, 8])
    if dbg:
        dbg_y = do("dbg_y", [4, 128, 4, T])
        dbg_h = do("dbg_h", [128, 8, T])
        dbg_mod = do("dbg_mod", [128, 96])
        dbg_mod2 = do("dbg_mod2", [128, 96])

    ARENA_K = 160
    arena = S.sb("arena", [128, ARENA_K * KIB], F32)

    def av(off_k, size_k, dtype, pattern=None, **kw):
        a = arena[:, int(off_k * KIB):int((off_k + size_k) * KIB)]
        if dtype == BF16:
            a = a.bitcast(BF16)
        if pattern:
            a = a.rearrange(pattern, **kw)
        return a

    hT = av(0, 32, BF16, "p (c t) -> p c t", c=8)
    WR = [av(32 + 8 * i, 8, BF16, "p (k n) -> p k n", k=8) for i in range(3)]
    YM = [av(56 + 16 * m, 16, BF16, "p (c t) -> p c t", c=4) for m in range(4)]
    ymlp = av(56, 64, F32, "p (c t) -> p c t", c=8)
    vbuf_t = S.sb("vbuf", [128, VON + 64], BF16)
    cvbuf_t = S.sb("cvbuf", [128, CVON + 64], BF16)
    vbuf = vbuf_t[:, :]
    cvbuf = cvbuf_t[:, :]
    stg = S.sb("stg", [128, NSTG, 512], F32)
    cf = S.sb("cf_sb", [128, CF_N], F32)
    cb = S.sb("cb_sb", [128, CB_N], BF16)
    cst2 = S.sb("cst2", [128, 4, 256], F32)
    sm = S.sb("sm", [128, 512], F32)
    modraw = S.sb("modraw", [128, DEPTH, 48], F32)
    modc = S.sb("modc", [128, DEPTH + 1, 6, 8], F32)
    adab = S.sb("adab_sb", [128, DEPTH, 48], F32)
    gains = S.sb("gains_sb", [128, DEPTH, 4, 8], F32)
    flags = S.sb("flags_sb", [128, 2], F32)
    rfl = S.sb("rfl_sb", [128, 2, 16], F32)
    dnorm = S.sb("dnorm_sb", [128, DEPTH], F32)
    rtheta = S.sb("rtheta_sb", [128, DEPTH, 2, 8], F32)
    rnorm = S.sb("rnorm_sb", [128, DEPTH, 4], F32)
    convw = S.sb("convw_sb", [128, DEPTH, 4, 4], F32)
    convb = S.sb("convb_sb", [128, DEPTH, 4], F32)
    lrub = S.sb("lrub_sb", [128, DEPTH, 6, 4], F32)
    h_l0 = S.sb("h_l0_sb", [128, DEPTH, 2, 4], F32)
    hl_sb = S.sb("hl_sb", [128, 2, 4, 8], F32)
    scond = S.sb("scond", [128, 8], BF16)
    condsb = S.sb("condsb", [128, 8], F32)
    PS = [S.ps("ps%d" % i, [128, 512], F32) for i in range(8)]

    st = {"ps": 0, "stg": 0, "wr": 0, "stgL": 0}

    def stageL():
        i_ = st["stgL"] % 2
        s_ = cst2[:, 2 * i_:2 * i_ + 2, :].rearrange("p a b -> p (a b)")
        st["stgL"] += 1
        return s_

    def psb():
        b = PS[st["ps"] % 6][:, :]
        st["ps"] += 1
        return b

    def stage():
        s = stg[:, st["stg"] % NSTG, :]
        st["stg"] += 1
        return s

    def stage_bf():
        return stage().bitcast(BF16)[:, 0:512]

    def mm(out, lhsT, rhs, start=True, stop=True):
        S.add("pe", lambda e: e.matmul(out, lhsT=lhsT, rhs=rhs, start=start, stop=stop), reads=[lhsT, rhs], writes=[out])

    def act(out, in_, func, bias=None, scale=None):
        kw = {}
        rd = [in_]
        if bias is not None:
            kw["bias"] = bias
            if not isinstance(bias, (int, float)):
                rd.append(bias)
        if scale is not None:
            kw["scale"] = scale
            if not isinstance(scale, (int, float)):
                rd.append(scale)
        S.add("act", lambda e: e.activation(out=out, in_=in_, func=func, **kw), reads=rd, writes=[out])

    def tt(out, in0, in1, op, eng="dve"):
        S.add(eng, lambda e: e.tensor_tensor(out=out, in0=in0, in1=in1, op=op), reads=[in0, in1], writes=[out])

    def ts(out, in0, s1, s2, op0, op1=None, eng="dve"):
        rd = [in0]
        for s in (s1, s2):
            if s is not None and not isinstance(s, (int, float)):
                rd.append(s)
        if op1 is None:
            S.add(eng, lambda e: e.tensor_scalar(out=out, in0=in0, scalar1=s1, scalar2=None, op0=op0), reads=rd, writes=[out])
        else:
            S.add(eng, lambda e: e.tensor_scalar(out=out, in0=in0, scalar1=s1, scalar2=s2, op0=op0, op1=op1), reads=rd, writes=[out])

    def stt(out, in0, scalar, in1, op0, op1):
        rd = [in0, in1]
        if not isinstance(scalar, (int, float)):
            rd.append(scalar)
        S.add("dve", lambda e: e.scalar_tensor_tensor(out=out, in0=in0, scalar=scalar, in1=in1, op0=op0, op1=op1), reads=rd, writes=[out])

    def cp(out, in_, eng="dve"):
        if eng == "act":
            act(out, in_, AF.Copy)
        else:
            S.add(eng, lambda e: e.tensor_copy(out=out, in_=in_), reads=[in_], writes=[out])

    def recip(out, in_):
        S.add("dve", lambda e: e.reciprocal(out=out, in_=in_), reads=[in_], writes=[out])

    def memset(ap, val, eng="dve"):
        S.add(eng, lambda e: e.memset(ap, val), writes=[ap])

    def dma(eng, out, in_):
        S.add(eng, lambda e: e.dma_start(out=out, in_=in_), reads=[in_], writes=[out], dma=True)

    def load_w(src2d, nk, ncols):
        slot = WR[st["wr"] % 3]
        st["wr"] += 1
        dst = slot[:, 0:nk, 0:ncols]
        dma("pool", dst, src2d)
        return dst

    def proj_fm(wt, cc, tb, nk=8, src=None, n0=None, n=TB):
        src = hT if src is None else src
        n0 = tb * TB if n0 is None else n0
        ps = psb()
        for k in range(nk):
            mm(ps[:, 0:n], wt[:, k, cc * 128:(cc + 1) * 128], src[:, k, n0:n0 + n], k == 0, k == nk - 1)
        return ps

    def proj_tm(wt, ti):
        ps = psb()
        for k in range(8):
            mm(ps, hT[:, k, ti * 128:(ti + 1) * 128], wt[:, k, :], k == 0, k == 7)
        return ps

    ones_bf = cb[:, CB_ONES:CB_ONES + 128]
    bd_bf = cb[:, CB_BD:CB_BD + 128]
    epsc = sm[:, 0:1]
    onec = sm[:, 1:2]

    def vh(tile_i, h):
        off = tile_i * 512 + h * 64
        return vbuf[:, off:off + 64]

    def cvh(kc, h):
        off = kc * 512 + h * 64
        return cvbuf[:, off:off + 64]

    dma("sp", cf[:], cf_d)
    dma("pool", cb[:], cb_d)
    for dst, srcd in ((flags, flags_d), (rfl, rfl_d), (condsb, cond), (adab, adab_d), (gains, gains_d), (dnorm, dnorm_d),
                      (rtheta, rtheta_d), (rnorm, rnorm_d), (convw, convw_d), (convb, convb_d), (lrub, lrub_d), (h_l0, h_l0_d)):
        dma("sp", dst[:], srcd)
    memset(sm[:, 0:1], EPS)
    memset(sm[:, 1:2], 1.0)
    memset(vbuf[:, VON:VON + 64], 1.0)
    memset(cvbuf[:, CVON:CVON + 64], 1.0)
    act(scond[:], condsb[:], AF.Silu)

    def modulation(l):
        for j in range(12):
            wt = load_w(ada_w[l, j], 8, 512)
            ps = psb()
            for k in range(8):
                mm(ps[0:1, :], scond[:, k:k + 1], wt[:, k, :], k == 0, k == 7)
            row = stage()
            act(row[0:1, :], ps[0:1, :], AF.Copy)
            ps2 = psb()
            for q in range(4):
                mm(ps2[:, q:q + 1], row[0:1, q * 128:(q + 1) * 128], onec[0:1, 0:1])
            tt(modraw[:, l, j * 4:(j + 1) * 4], ps2[:, 0:4], adab[:, l, j * 4:(j + 1) * 4], ALU.add)
        mr = modraw[:, l, :]
        stt(modc[:, l, 0, :], mr[:, 8:16], 1.0, gains[:, l, 0, :], ALU.add, ALU.mult)
        cp(modc[:, l, 1, :], mr[:, 0:8])
        tt(modc[:, l, 2, :], mr[:, 16:24], gains[:, l, 1, :], ALU.mult)
        stt(modc[:, l, 3, :], mr[:, 32:40], 1.0, gains[:, l, 2, :], ALU.add, ALU.mult)
        cp(modc[:, l, 4, :], mr[:, 24:32])
        tt(modc[:, l, 5, :], mr[:, 40:48], gains[:, l, 3, :], ALU.mult)

    modulation(0)

    def resid_norm(l, tb, src, ysrc, ares, xs, hspec, preloaded=False):
        if not preloaded:
            dma("sp", xs, src[tb])
        if ysrc is not None:
            ps = psb()
            for c in range(8):
                sq = stage_bf()
                act(sq, ysrc[:, c, :], AF.Square)
                mm(ps, ones_bf, sq, c == 0, c == 7)
            rs = stageL()
            act(rs, ps, AF.Ln, bias=epsc, scale=1.0 / D)
            act(rs, rs, AF.Exp, scale=-0.5)
            for c in range(8):
                tmp = stage()
                stt(tmp, ysrc[:, c, :], modc[:, l, ares, c:c + 1], rs, ALU.mult, ALU.mult)
                tt(xs[:, c, :], xs[:, c, :], tmp, ALU.add, eng=("pool" if c % 2 == 0 else "dve"))
            dma("sp", yres[tb], xs)
        if hspec is not None:
            ln, an, bn = hspec
            ps = psb()
            for c in range(8):
                sq = stage_bf()
                act(sq, xs[:, c, :], AF.Square)
                mm(ps, ones_bf, sq, c == 0, c == 7)
            rs = stageL()
            act(rs, ps, AF.Ln, bias=epsc, scale=1.0 / D)
            act(rs, rs, AF.Exp, scale=-0.5)
            for c in range(8):
                tmp = stage()
                stt(tmp, xs[:, c, :], modc[:, ln, an, c:c + 1], rs, ALU.mult, ALU.mult)
                act(hT[:, c, tb * TB:(tb + 1) * TB], tmp, AF.Identity, bias=modc[:, ln, bn, c:c + 1])

    def run_pipeline(items, look):
        toks = {}
        n = len(items)
        for i in range(min(look, n)):
            toks[i] = items[i][0]()
        for i in range(n):
            if i + look < n:
                toks[i + look] = items[i + look][0]()
            items[i][1](toks.pop(i))

    def mixerA(l):
        kT = av(120, 16, BF16, "p (c t) -> p c t", c=4)
        qT = av(136, 16, BF16, "p (c t) -> p c t", c=4)
        EB2 = av(152, 7.5, BF16, "p (h r q) -> p h r q", h=4, r=15)
        EBraw = av(72, 15, F32, "p (h r q) -> p h r q", h=4, r=15)
        ckT = av(87, 4, BF16, "p (c t) -> p c t", c=4)
        Eloc = [av(91 + 5 * i, 5, BF16, "p (j h q) -> p j h q", j=5, h=4) for i in range(2)]
        Ectx = [av(101 + 4 * i, 4, BF16, "p (h k) -> p h k", h=4) for i in range(2)]
        yap = av(109, 2, F32, "p (c q) -> p c q", c=2)
        yas = av(111, 2, F32, "p (c q) -> p c q", c=2)
        Ep = [av(113 + i, 1, BF16) for i in range(4)]
        epi = [0]
        ya = YM[0]
        wk = load_w(w_in[l, 1], 8, 512)
        for c in range(4):
            for tb in range(NTB):
                ps = proj_fm(wk, c, tb)
                sg = stage()
                cp(sg, ps)
                act(kT[:, c, tb * TB:(tb + 1) * TB], sg, AF.Copy)
                dma("sp", o_nakT[l][:, c, tb * TB:(tb + 1) * TB], sg)
        wv = load_w(w_in[l, 2], 8, 512)
        for ti in range(NTT):
            ps = proj_tm(wv, ti)
            sg = stage()
            cp(sg, ps)
            act(vbuf[:, ti * 512:(ti + 1) * 512], sg, AF.Copy)
            dma("sp", o_nav[l][ti], sg)
        wq = load_w(w_in[l, 0], 8, 512)
        for c in range(4):
            for tb in range(NTB):
                ps = proj_fm(wq, c, tb)
                act(qT[:, c, tb * TB:(tb + 1) * TB], ps, AF.Copy)
        dma("pool", ckT, c_nakT[l])
        dma("pool", cvbuf[:, 0:CVON].rearrange("p (k n) -> p k n", k=4), c_nav[l])
        colm = bass.AP(cb.tensor if hasattr(cb, "tensor") else cb, CB_COL, [[CB_N, 128], [0, 60], [1, 64]])
        for hg in range(2):
            for a in range(2):
                for hh in range(4):
                    srcap = bass.AP(rpbp.tensor, rpbp[l, 4 * hg + hh, a, 0:1].offset, [[1, 64], [128, 15], [1, 64]])
                    if os.environ.get("A_NOEB"):
                        memset(EBraw[a * 64:(a + 1) * 64, hh, :, :], 0.0)
                    else:
                        dma("sp", EBraw[a * 64:(a + 1) * 64, hh, :, :], srcap)
            if not os.environ.get("A_NOEB2"):
                act(EB2[:, :, :, :], EBraw[:, :, :, :], AF.Exp)
                e2f = EB2.rearrange("p h r q -> p (h r) q")
                if not os.environ.get("A_NOCOLM"):
                    tt(e2f, e2f, colm, ALU.mult)
            items = []
            for s in range(8):
                for cl in range(2):
                    def f1(s=s, cl=cl):
                        c = 2 * hg + cl
                        Es = []
                        for hs in range(2):
                            hp = hs * 64
                            psS = psb()
                            for kc in range(2):
                                mm(psS[:, kc * 256:(kc + 1) * 256], kT[hp:hp + 64, c, (2 * s + kc) * 128:(2 * s + kc + 1) * 128],
                                   qT[hp:hp + 64, c, s * 256:(s + 1) * 256])
                            E = Ep[epi[0] % 4]
                            epi[0] += 1
                            act(E, psS, AF.Exp, scale=0.125)
                            Es.append(E)
                        return Es

                    def f2(Es, s=s, cl=cl):
                        c = 2 * hg + cl
                        psO = psb()
                        for hs in range(2):
                            h = 2 * c + hs
                            E = Es[hs]
                            for kc in range(2):
                                mm(psO[0:64, hs * 256:(hs + 1) * 256], vh(2 * s + kc, h), E[:, kc * 256:(kc + 1) * 256], kc == 0, kc == 1)
                            for kc in range(2):
                                mm(psO[64:128, hs * 256:(hs + 1) * 256], ones_bf[:, 0:64], E[:, kc * 256:(kc + 1) * 256], kc == 0, kc == 1)
                        rd = stage()
                        recip(rd[64:128, :], psO[64:128, :])
                        for hs in range(2):
                            tt(yap[hs * 64:(hs + 1) * 64, cl, :], psO[0:64, hs * 256:(hs + 1) * 256], rd[64:128, hs * 256:(hs + 1) * 256], ALU.mult)
                    items.append((f1, f2))
                for i in (2 * s, 2 * s + 1):
                    if i < 2:
                        js = [0, 1, 2, 3]
                    elif i > 13:
                        js = [12, 13, 14, 15]
                    else:
                        js = [i - 2, i - 1, i, i + 1, i + 2]

                    def f1(i=i, js=js):
                        nj = len(js)
                        El = Eloc[i % 2]
                        Ec = Ectx[i % 2]
                        for hh in range(4):
                            h = 4 * hg + hh
                            c = h // 2
                            hp = (h % 2) * 64
                            q_ = qT[hp:hp + 64, c, i * 128:(i + 1) * 128]
                            psL = psb()
                            psL2 = psb() if nj == 5 else None
                            psC = psb()
                            for idx, j in enumerate(js):
                                tgt = psL[:, idx * 128:(idx + 1) * 128] if idx < 4 else psL2[:, 0:128]
                                mm(tgt, kT[hp:hp + 64, c, j * 128:(j + 1) * 128], q_)
                            for kc in range(4):
                                mm(psC[:, kc * 128:(kc + 1) * 128], ckT[hp:hp + 64, c, kc * 128:(kc + 1) * 128], q_)
                            act(El[:, 0:4, hh, :], psL.rearrange("p (j q) -> p j q", j=4), AF.Exp, scale=0.125)
                            if nj == 5:
                                act(El[:, 4, hh, :], psL2[:, 0:128], AF.Exp, scale=0.125)
                            act(Ec[:, hh, :], psC, AF.Exp, scale=0.125)
                        for idx, j in enumerate(js):
                            for b in range(2):
                                dr0 = 2 * (j - i) - b + 7
                                assert 0 <= dr0 <= 14
                                tt(El[:, idx, :, b * 64:(b + 1) * 64], El[:, idx, :, b * 64:(b + 1) * 64], EB2[:, :, dr0, ::-1], ALU.mult)
                        if nj == 5:
                            memset(El[0:64, 0, :, 64:128], 0.0)
                            memset(El[0:64, 4, :, 0:64], 0.0)
                            memset(El[64:128, 4, :, :], 0.0)
                        return (El, Ec)

                    def f2(tok, i=i, js=js, s=s):
                        El, Ec = tok
                        psO = psb()
                        for hh in range(4):
                            h = 4 * hg + hh
                            for idx, j in enumerate(js):
                                mm(psO[0:64, hh * 128:(hh + 1) * 128], vh(j, h), El[:, idx, hh, :], idx == 0, False)
                            for kc in range(4):
                                mm(psO[0:64, hh * 128:(hh + 1) * 128], cvh(kc, h), Ec[:, hh, kc * 128:(kc + 1) * 128], False, kc == 3)
                            for idx, j in enumerate(js):
                                mm(psO[64:128, hh * 128:(hh + 1) * 128], ones_bf[:, 0:64], El[:, idx, hh, :], idx == 0, False)
                            for kc in range(4):
                                mm(psO[64:128, hh * 128:(hh + 1) * 128], ones_bf[:, 0:64], Ec[:, hh, kc * 128:(kc + 1) * 128], False, kc == 3)
                        rd = stage()
                        recip(rd[64:128, :], psO[64:128, :])
                        for hh in range(4):
                            tt(yas[(hh % 2) * 64:(hh % 2) * 64 + 64, hh // 2, (i % 2) * 128:(i % 2) * 128 + 128],
                               psO[0:64, hh * 128:(hh + 1) * 128], rd[64:128, hh * 128:(hh + 1) * 128], ALU.mult)
                        if i % 2 == 1:
                            t_ = stage().rearrange("p (c q) -> p c q", c=2)
                            ts(t_, yas, flags[:, 1:2], None, ALU.mult)
                            stt(ya[:, 2 * hg:2 * hg + 2, s * 256:(s + 1) * 256], yap, flags[:, 0:1], t_, ALU.mult, ALU.add)
                    items.append((f1, f2))
            run_pipeline(items, 1)

    def mixerB(l):
        kT = av(120, 16, BF16, "p (c t) -> p c t", c=4)
        qT = av(136, 16, BF16, "p (c t) -> p c t", c=4)
        ckT = av(152, 4, BF16, "p (c t) -> p c t", c=4)
        ropes = av(88, 16, F32, "p (b s t) -> p b s t", b=4, s=2)
        Er = [av(104 + i, 1, BF16) for i in range(4)] + [av(116 + i, 1, BF16) for i in range(2)]
        Eacc = av(118, 2, F32)
        onesf = av(152 + 4, 0.5, F32)
        ybp = av(108, 2, F32)
        ybs = av(110, 2, F32)
        o1n = av(112, 2, F32)
        ybh = av(114, 2, F32)
        yb = YM[1]
        sperm = cf[:, CF_SPERM:CF_SPERM + 128]
        ones128 = ones_bf
        dma("sp", ropes, rope_d)
        dl = stage().rearrange("p (a d) -> p a d", a=8)
        dma("sp", dl[:, 0:4, :], dlam_d[:, l, :, :])
        tt(dl[:, 4, :], dl[:, 0, :], dl[:, 1, :], ALU.mult)
        tt(dl[:, 5, :], dl[:, 2, :], dl[:, 3, :], ALU.mult)
        S.add("dve", lambda e: e.tensor_reduce(out=sm[:, 8:10], in_=dl[:, 4:6, :], axis=mybir.AxisListType.X, op=ALU.add),
              reads=[dl[:, 4:6, :]], writes=[sm[:, 8:10]])
        act(sm[:, 10:12], sm[:, 8:10], AF.Exp)
        tt(sm[:, 12:13], sm[:, 11:12], sm[:, 10:11], ALU.subtract)
        ts(sm[:, 12:13], sm[:, 12:13], -LAM_INIT[l], None, ALU.add)
        neglam = sm[:, 12:13]
        ts(sm[:, 13:14], dnorm[:, l:l + 1], 1.0 - LAM_INIT[l], None, ALU.mult)
        dnc = sm[:, 13:14]

        def rot_proj(wt, dst, outd):
            for c in range(4):
                for tb in range(NTB):
                    ps = proj_fm(wt, c, tb)
                    zs = stage()
                    act(zs, ps, AF.Copy)
                    ps2 = psb()
                    mm(ps2, sperm, zs)
                    t1 = stage()
                    tt(t1, zs, ropes[:, tb, 0, :], ALU.mult)
                    t2 = stage()
                    tt(t2, ps2, ropes[:, tb, 1, :], ALU.mult)
                    tt(t1, t1, t2, ALU.add, eng="pool")
                    act(dst[:, c, tb * TB:(tb + 1) * TB], t1, AF.Copy)
                    if outd is not None:
                        dma("sp", outd[:, c, tb * TB:(tb + 1) * TB], t1)

        wk = load_w(w_in[l, 4], 8, 512)
        rot_proj(wk, kT, o_dfkT[l])
        wv = load_w(w_in[l, 5], 8, 512)
        for ti in range(NTT):
            ps = proj_tm(wv, ti)
            sg = stage()
            cp(sg, ps)
            act(vbuf[:, ti * 512:(ti + 1) * 512], sg, AF.Copy)
            dma("sp", o_dfv[l][ti], sg)
        wq = load_w(w_in[l, 3], 8, 512)
        rot_proj(wq, qT, None)
        dma("pool", ckT, c_dfkT[l])
        dma("pool", cvbuf[:, 0:CVON].rearrange("p (k n) -> p k n", k=4), c_dfv[l])
        psOa, psDa = PS[6][:, :], PS[7][:, :]
        memset(onesf, 1.0)

        def finish(dst, w):
            sq = stage_bf()
            act(sq[:, 0:w], ybh[:, 0:w], AF.Square)
            psN = psb()
            mm(psN[:, 0:w], ones128, sq[:, 0:w])
            rs = stage()
            act(rs[:, 0:w], psN[:, 0:w], AF.Ln, bias=epsc, scale=1.0 / 128)
            act(rs[:, 0:w], rs[:, 0:w], AF.Exp, scale=-0.5)
            stt(dst[:, 0:w], ybh[:, 0:w], dnc, rs[:, 0:w], ALU.mult, ALU.mult)

        ei = [0]

        def nextE():
            e_ = Er[ei[0] % 6]
            ei[0] += 1
            return e_

        qz = [av(88 + 8 * i, 8, BF16, "p (c s t) -> p c s t", c=4, s=2) for i in range(2)]
        for i in range(2):
            memset(qz[i], 0.0, eng="pool")
        built = set()

        def build_qz(tb):
            if tb in built:
                return
            built.add(tb)
            for cq in range(4):
                for hs in range(2):
                    cp(qz[tb % 2][hs * 64:(hs + 1) * 64, cq, hs, :], qT[hs * 64:(hs + 1) * 64, cq, tb * TB:(tb + 1) * TB],
                       eng=("pool" if hs == 0 else "act"))

        items = []
        for tb in range(NTB):
            for h in range(4):
                hp = (h % 2) * 64
                for s in (2 * tb, 2 * tb + 1):
                    for comp in range(2):
                        def f1(s=s, comp=comp, h=h, hp=hp, tb=tb):
                            build_qz(tb)
                            cq = comp * 2 + h // 2
                            psS = psb()
                            for kc in range(2):
                                mm(psS[:, kc * 256:(kc + 1) * 256], kT[:, cq, (2 * s + kc) * 128:(2 * s + kc + 1) * 128],
                                   qz[tb % 2][:, cq, h % 2, (s % 2) * 256:(s % 2) * 256 + 256])
                            E = nextE()
                            act(E, psS, AF.Exp, scale=0.125)
                            return E

                        def f2(E, s=s, comp=comp, h=h):
                            psO = psb()
                            for kc in range(2):
                                mm(psO[:, 0:256], vbuf[:, (2 * s + kc) * 512 + h * 128:(2 * s + kc) * 512 + (h + 1) * 128],
                                   E[:, kc * 256:(kc + 1) * 256], kc == 0, kc == 1)
                            for kc in range(2):
                                mm(psO[:, 256:512], ones128, E[:, kc * 256:(kc + 1) * 256], kc == 0, kc == 1)
                            rd = stage()
                            recip(rd[:, 0:256], psO[:, 256:512])
                            if comp == 0:
                                tt(o1n[:, 0:256], psO[:, 0:256], rd[:, 0:256], ALU.mult)
                            else:
                                t_ = stage()
                                tt(t_[:, 0:256], psO[:, 0:256], rd[:, 0:256], ALU.mult)
                                stt(ybh[:, 0:256], t_[:, 0:256], neglam, o1n[:, 0:256], ALU.mult, ALU.add)
                                finish(ybp[:, (s % 2) * 256:(s % 2) * 256 + 256], 256)
                        items.append((f1, f2))
                for comp in range(2):
                    for kk in range(20):
                        def f1(comp=comp, kk=kk, h=h, hp=hp, tb=tb):
                            build_qz(tb)
                            cq = comp * 2 + h // 2
                            q_ = qz[tb % 2][:, cq, h % 2, :]
                            if kk < 16:
                                k_ = kT[:, cq, kk * 128:(kk + 1) * 128]
                            else:
                                k_ = ckT[:, cq, (kk - 16) * 128:(kk - 15) * 128]
                            psS = psb()
                            mm(psS, k_, q_)
                            E = nextE()
                            act(E, psS, AF.Exp, scale=0.125)
                            return E

                        def f2(E, comp=comp, kk=kk, h=h, tb=tb):
                            if kk < 16:
                                v_ = vbuf[:, kk * 512 + h * 128:kk * 512 + (h + 1) * 128]
                            else:
                                v_ = cvbuf[:, (kk - 16) * 512 + h * 128:(kk - 16) * 512 + (h + 1) * 128]
                            mm(psOa, v_, E, kk == 0, kk == 19)
                            mm(psDa, ones128, E, kk == 0, kk == 19)
                            if kk == 19:
                                rd = stage()
                                recip(rd, psDa)
                                if comp == 0:
                                    tt(o1n, psOa, rd, ALU.mult)
                                else:
                                    t_ = stage()
                                    tt(t_, psOa, rd, ALU.mult)
                                    stt(ybh, t_, neglam, o1n, ALU.mult, ALU.add)
                                    finish(ybs, 512)
                                    t2_ = stage()
                                    ts(t2_, ybs, flags[:, 1:2], None, ALU.mult)
                                    stt(yb[:, h, tb * TB:(tb + 1) * TB], ybp, flags[:, 0:1], t2_, ALU.mult, ALU.add)
                        items.append((f1, f2))
        run_pipeline(items, 3)

    def mixerC(l):
        ktok = av(120, 16, BF16, "p (t n) -> p t n", t=16)
        qT = av(120, 16, BF16, "p (c t) -> p c t", c=4)
        kT = av(136, 16, BF16, "p (c t) -> p c t", c=4)
        Dc = av(152, 2, BF16, "p (h n) -> p h n", h=8)
        XIf = av(154, 1, BF16, "p (c n) -> p c n", c=4)
        XIb = av(155, 1, BF16, "p (c n) -> p c n", c=4)
        kz = [av(156 + i, 1, BF16) for i in range(2)]
        innerD = av(158, 2, BF16, "p (h n) -> p h n", h=8)
        SA = [av(104 + 8 * d, 8, BF16, "p (c n) -> p c n", c=16) for d in range(2)]
        yc = YM[2]
        Srun = [cst2[:, 0, :], cst2[:, 1, :]]
        Gt = [cst2[:, 2, :], cst2[:, 3, :]]
        relf = cf[:, CF_RELF:CF_RELF + 128]
        relb = cf[:, CF_RELB:CF_RELB + 128]
        pos1 = cf[:, CF_POS1:CF_POS1 + 128]
        posb = cf[:, CF_POSB:CF_POSB + 128]
        um = cb[:, CB_UM:CB_UM + 128]
        lmk = cb[:, CB_LM:CB_LM + 128]
        lg = sm[:, 16:32].rearrange("p (d h) -> p d h", d=2)
        lgp = sm[:, 32:40].rearrange("p (d c) -> p d c", d=2)
        act(lg, rtheta[:, l, :, :], AF.Exp)
        ts(lg, lg, -1.0, 1.0, ALU.mult, ALU.add)
        act(lg, lg, AF.Ln)
        for d in range(2):
            cp(lgp[0:64, d, :], lg[0:64, d, 0:8:2])
            cp(lgp[64:128, d, :], lg[64:128, d, 1:8:2])
        for c in range(4):
            act(XIf[:, c, :], pos1, AF.Exp, scale=lgp[:, 0, c:c + 1])
            act(XIb[:, c, :], posb, AF.Exp, scale=lgp[:, 1, c:c + 1])
        for h in range(8):
            a1 = stage()
            act(a1[:, 0:128], relf, AF.Exp, scale=lg[:, 0, h:h + 1])
            tt(a1[:, 0:128], a1[:, 0:128], um, ALU.mult)
            act(a1[:, 128:256], relb, AF.Exp, scale=lg[:, 1, h:h + 1])
            tt(a1[:, 128:256], a1[:, 128:256], lmk, ALU.mult)
            tt(Dc[:, h, :], a1[:, 0:128], a1[:, 128:256], ALU.add)
        zf = sm[:, 40:48]
        zb = sm[:, 48:56]
        tt(zf, cf[:, CF_MPOSF:CF_MPOSF + 8], lg[:, 0, :], ALU.mult)
        act(zf, zf, AF.Exp)
        ts(zf, zf, 0.125, None, ALU.mult)
        tt(zb, cf[:, CF_MPOSB:CF_MPOSB + 8], lg[:, 1, :], ALU.mult)
        act(zb, zb, AF.Exp)
        ts(zb, zb, 0.125, None, ALU.mult)
        g4 = sm[:, 56:64].rearrange("p (d c) -> p d c", d=2)
        act(g4, lgp, AF.Exp, scale=128.0)
        for d in range(2):
            src = bass.AP(sm, 56 + 4 * d, [[512, 128], [1, 4], [0, 64]])
            cp(Gt[d].rearrange("p (c e) -> p c e", c=4), src)
        wk = load_w(w_in[l, 7], 8, 512)
        for ti in range(NTT):
            ps = proj_tm(wk, ti)
            act(ktok[:, ti, :], ps, AF.Copy)
        wv = load_w(w_in[l, 8], 8, 512)
        for ti in range(NTT):
            ps = proj_tm(wv, ti)
            act(vbuf[:, ti * 512:(ti + 1) * 512], ps, AF.Copy)
        for d in range(2):
            z_ = zf if d == 0 else zb
            zbc = bass.AP(sm, 40 + 8 * d, [[512, 128], [1, 8], [0, 64]])
            order = list(range(16)) if d == 0 else list(range(15, -1, -1))
            s0 = s_rf0 if d == 0 else s_rb0
            od = o_srf if d == 0 else o_srb
            dma("sp", Srun[d], s0[l])
            cp(SA[d][:, order[0], :], Srun[d], eng="pool")
            for n_, c in enumerate(order):
                kzt = kz[n_ % 2]
                tt(kzt.rearrange("p (h e) -> p h e", h=8), ktok[:, c, :].rearrange("p (h e) -> p h e", h=8), zbc, ALU.mult)
                psP = psb()
                for h in range(8):
                    hp = (h % 2) * 64
                    mm(psP[hp:hp + 64, (h // 2) * 64:(h // 2) * 64 + 64], kzt[:, h * 64:(h + 1) * 64],
                       vbuf[:, c * 512 + h * 64:c * 512 + (h + 1) * 64])
                t_ = stage()
                tt(t_[:, 0:256], Srun[d], Gt[d], ALU.mult)
                se = stage()
                tt(se[:, 0:256], t_[:, 0:256], psP[:, 0:256], ALU.add)
                if (d == 0 and c % 2 == 1) or (d == 1 and c % 2 == 0):
                    dma("sp", od[l][c // 2], se[:, 0:256])
                if n_ < 15:
                    cn = order[n_ + 1]
                    ts(Srun[d], se[:, 0:256], rfl[:, d, cn:cn + 1], None, ALU.mult)
                    cp(SA[d][:, cn, :], Srun[d], eng="pool")
        wq = load_w(w_in[l, 6], 8, 512)
        for c in range(4):
            for tb in range(NTB):
                ps = proj_fm(wq, c, tb)
                act(qT[:, c, tb * TB:(tb + 1) * TB], ps, AF.Copy)
        wk2 = load_w(w_in[l, 7], 8, 512)
        for c in range(4):
            for tb in range(NTB):
                ps = proj_fm(wk2, c, tb)
                act(kT[:, c, tb * TB:(tb + 1) * TB], ps, AF.Copy)
        for c in range(16):
            cs = slice(c * 128, (c + 1) * 128)
            psI = [psb(), psb()]
            for h in range(8):
                hp = (h % 2) * 64
                mm(psI[h % 2][:, (h // 2) * 128:(h // 2 + 1) * 128], kT[hp:hp + 64, h // 2, cs], qT[hp:hp + 64, h // 2, cs])
            for g in range(2):
                tt(innerD[:, g:8:2, :], psI[g].rearrange("p (h n) -> p h n", h=4), Dc[:, g:8:2, :], ALU.mult)
            qx = stage_bf().rearrange("p (c n) -> p c n", c=4)
            tt(qx, qT[:, :, cs], XIf, ALU.mult, eng="pool")
            qy = stage_bf().rearrange("p (c n) -> p c n", c=4)
            tt(qy, qT[:, :, cs], XIb, ALU.mult, eng="pool")
            sbd = stage().bitcast(BF16).rearrange("p (d c n) -> p d c n", d=2, c=4)
            memset(sbd, 0.0, eng="pool")
            for d in range(2):
                for hs in range(2):
                    cp(sbd[hs * 64:(hs + 1) * 64, d, :, hs * 64:(hs + 1) * 64],
                       SA[d][hs * 64:(hs + 1) * 64, c, :].rearrange("p (c e) -> p c e", c=4), eng="pool")
            psO = psb()
            for ch in range(4):
                for hs in range(2):
                    h = 2 * ch + hs
                    mm(psO[hs * 64:(hs + 1) * 64, ch * 128:(ch + 1) * 128], vbuf[:, c * 512 + h * 64:c * 512 + (h + 1) * 64],
                       innerD[:, h, :], True, False)
                mm(psO[:, ch * 128:(ch + 1) * 128], sbd[:, 0, ch, :], qx[:, ch, :], False, False)
                mm(psO[:, ch * 128:(ch + 1) * 128], sbd[:, 1, ch, :], qy[:, ch, :], False, True)
            sq = stage_bf()
            act(sq, psO, AF.Square)
            psN = psb()
            mm(psN, bd_bf, sq)
            rs = stage()
            act(rs, psN, AF.Ln, bias=epsc, scale=1.0 / 64)
            act(rs, rs, AF.Exp, scale=-0.5)
            for ch in range(4):
                stt(yc[:, ch, cs], psO[:, ch * 128:(ch + 1) * 128], rnorm[:, l, ch:ch + 1], rs[:, ch * 128:(ch + 1) * 128], ALU.mult, ALU.mult)
        wg = load_w(w_in[l, 9], 8, 512)
        for ch in range(4):
            for tb in range(NTB):
                ps = proj_fm(wg, ch, tb)
                sg = stage()
                act(sg, ps, AF.Silu)
                tt(yc[:, ch, tb * TB:(tb + 1) * TB], yc[:, ch, tb * TB:(tb + 1) * TB], sg, ALU.mult)

    def mixerD(l):
        lrx = av(120, 8.25, F32)
        abuf = lrx
        xd = av(128.25, 8, F32)
        xdbf = av(136.25, 4, BF16)
        ubuf = av(140.25, 8, F32)
        hf = av(148.25, 8, F32)
        yd = YM[3]
        lw = cvbuf[:, 0:2048].rearrange("p (k c n) -> p k c n", k=4, c=4)
        dma("pool", lw, lruw_d[l])
        wx = load_w(w_in[l, 10], 8, 512)
        wg = load_w(w_in[l, 11], 8, 512)
        cl = sm[:, 64:72].rearrange("p (d c) -> p d c", d=2)
        cl2 = sm[:, 72:80].rearrange("p (d c) -> p d c", d=2)
        for d in range(2):
            act(cl[:, d, :], lrub[:, l, 2 + 3 * d, :], AF.Exp, scale=-1.0)
        act(cl, cl, AF.Ln, bias=onec)
        ts(cl2, cl, -16.0, None, ALU.mult)
        ts(cl, cl, -8.0, None, ALU.mult)
        nfw = sm[:, 80:96].rearrange("p (k c) -> p k c", k=4)
        ts(nfw, convw[:, l, :, :], flags[:, 0:1], -1.0, ALU.mult, ALU.mult)
        memset(lrx[:, 2049:2112], 0.0)
        for ch in range(4):
            memset(lrx[:, 0:1], 0.0)
            for tb in range(NTB):
                ps = proj_fm(wx, ch, tb)
                act(lrx[:, 1 + tb * TB:1 + (tb + 1) * TB], ps, AF.Copy)
            act(xd, lrx[:, 1:2049], AF.Identity, bias=convb[:, l, ch:ch + 1], scale=convw[:, l, 1, ch:ch + 1])
            stt(xd, lrx[:, 0:2048], convw[:, l, 0, ch:ch + 1], xd, ALU.mult, ALU.add)
            stt(xd, lrx[:, 2:2050], convw[:, l, 2, ch:ch + 1], xd, ALU.mult, ALU.add)
            stt(xd, lrx[:, 3:2051], convw[:, l, 3, ch:ch + 1], xd, ALU.mult, ALU.add)
            stt(xd[:, 256:2048:256], lrx[:, 256:2048:256], nfw[:, 0, ch:ch + 1], xd[:, 256:2048:256], ALU.mult, ALU.add)
            stt(xd[:, 255:1792:256], lrx[:, 257:1794:256], nfw[:, 2, ch:ch + 1], xd[:, 255:1792:256], ALU.mult, ALU.add)
            stt(xd[:, 255:1792:256], lrx[:, 258:1795:256], nfw[:, 3, ch:ch + 1], xd[:, 255:1792:256], ALU.mult, ALU.add)
            stt(xd[:, 254:1791:256], lrx[:, 257:1794:256], nfw[:, 3, ch:ch + 1], xd[:, 254:1791:256], ALU.mult, ALU.add)
            cp(xdbf, xd, eng="pool")
            for d in range(2):
                for tb in range(NTB):
                    ts_ = slice(tb * TB, (tb + 1) * TB)
                    psr = psb()
                    mm(psr, lw[:, 2 * d, ch, :], xdbf[:, ts_])
                    act(abuf[:, ts_], psr, AF.Sigmoid, bias=lrub[:, l, 3 * d, ch:ch + 1])
                    psi = psb()
                    mm(psi, lw[:, 2 * d + 1, ch, :], xdbf[:, ts_])
                    act(ubuf[:, ts_], psi, AF.Sigmoid, bias=lrub[:, l, 3 * d + 1, ch:ch + 1])
                act(abuf[:, 0:2048], abuf[:, 0:2048], AF.Exp, scale=cl[:, d, ch:ch + 1])
                tt(ubuf, ubuf, xd, ALU.mult, eng="pool")
                for tb in range(NTB):
                    ts_ = slice(tb * TB, (tb + 1) * TB)
                    e2 = stage()
                    act(e2, abuf[:, ts_], AF.Square)
                    ts(e2, e2, -1.0, 1.0, ALU.mult, ALU.add)
                    act(e2, e2, AF.Ln)
                    act(e2, e2, AF.Exp, scale=0.5)
                    tt(ubuf[:, ts_], ubuf[:, ts_], e2, ALU.mult)
                if d == 0:
                    ts(abuf[:, 256:2048:256], abuf[:, 256:2048:256], flags[:, 1:2], None, ALU.mult)
                    S.add("dve", lambda e, ch=ch: e.tensor_tensor_scan(out=hf, data0=abuf[:, 0:2048], data1=ubuf, initial=h_l0[:, l, 0, ch:ch + 1],
                                                                        op0=ALU.mult, op1=ALU.add),
                          reads=[abuf[:, 0:2048], ubuf, h_l0[:, l, 0, ch:ch + 1]], writes=[hf])
                    cp(hl_sb[:, 0, ch, :], hf[:, 255:2048:256])
                else:
                    ts(abuf[:, 255:1792:256], abuf[:, 255:1792:256], flags[:, 1:2], None, ALU.mult)
                    S.add("dve", lambda e, ch=ch: e.tensor_tensor_scan(out=ubuf[:, ::-1], data0=abuf[:, 2047::-1], data1=ubuf[:, ::-1],
                                                                        initial=h_l0[:, l, 1, ch:ch + 1], op0=ALU.mult, op1=ALU.add),
                          reads=[abuf[:, 0:2048], ubuf, h_l0[:, l, 1, ch:ch + 1]], writes=[ubuf])
                    cp(hl_sb[:, 1, ch, :], ubuf[:, 0:2048:256])
            tt(hf, hf, ubuf, ALU.add, eng="pool")
            for tb in range(NTB):
                ts_ = slice(tb * TB, (tb + 1) * TB)
                psg = proj_fm(wg, ch, tb)
                g = stage()
                act(g, psg, AF.Copy)
                t_ = stage()
                act(t_, psg, AF.Square)
                ts(t_, t_, 0.044715, 1.0, ALU.mult, ALU.add)
                tt(t_, t_, g, ALU.mult)
                act(t_, t_, AF.Sigmoid, scale=2.0 * math.sqrt(2.0 / math.pi))
                tt(t_, t_, g, ALU.mult)
                tt(yd[:, ch, ts_], t_, hf[:, ts_], ALU.mult)
        dma("sp", o_hl[l], hl_sb[:])

    def merge_out(l):
        acc = av(120, 32, F32, "p (c t) -> p c t", c=8)
        Y01 = av(56, 32, BF16, "p (c t) -> p c t", c=8)
        Y23 = av(88, 32, F32, "p (c t) -> p c t", c=8)
        src = xin if l == 0 else yres
        for th in range(2):
            merged = Y01[:, :, th * 1024:(th + 1) * 1024]
            xs = Y23[:, :, th * 512:(th + 1) * 512]
            for half in range(2):
                for m in range(4):
                    gt = load_w(w_in[l, 12 + 2 * m + half], 8, 512)
                    bt = load_w(w_br[l, m, half], 4, 512)
                    for cc in range(4):
                        c8 = half * 4 + cc
                        for t2 in range(2):
                            tb = 2 * th + t2
                            a_ = acc[:, c8, t2 * TB:(t2 + 1) * TB]
                            psG = proj_fm(gt, cc, tb)
                            G = stage()
                            act(G, psG, AF.Sigmoid)
                            psP = proj_fm(bt, cc, tb, nk=4, src=YM[m])
                            if m == 0:
                                tt(a_, psP, G, ALU.mult)
                            else:
                                t_ = stage()
                                tt(t_, psP, G, ALU.mult)
                                tt(a_, a_, t_, ALU.add, eng=("pool" if (cc + t2) % 2 == 0 else "dve"))
            for c8 in range(8):
                act(merged[:, c8, :], acc[:, c8, :], AF.Copy)
            for half in range(2):
                wt = load_w(w_out[l, half], 8, 512)
                for cc in range(4):
                    for t2 in range(2):
                        ps = proj_fm(wt, cc, 0, src=merged, n0=t2 * TB)
                        cp(acc[:, half * 4 + cc, t2 * TB:(t2 + 1) * TB], ps)
            for t2 in range(2):
                resid_norm(l, 2 * th + t2, src, acc[:, :, t2 * TB:(t2 + 1) * TB], 2, xs, (l, 3, 4))

    def mlp(l, last):
        hid = av(120, 32, BF16, "p (c t) -> p c t", c=8)
        xs = av(120, 16, F32, "p (c t) -> p c t", c=8)
        for jg in range(4):
            if jg == 2 and not last:
                modulation(l + 1)
            for jh in range(2):
                wt = load_w(w1[l, 2 * jg + jh], 8, 512)
                for jj in range(4):
                    for tb in range(NTB):
                        ps = proj_fm(wt, jj, tb)
                        r = stage()
                        act(r, ps, AF.Relu)
                        tt(hid[:, jh * 4 + jj, tb * TB:(tb + 1) * TB], r, r, ALU.mult, eng=("pool" if tb % 2 == 0 else "dve"))
            for chh in range(2):
                wt2 = load_w(w2[l, jg, chh], 8, 512)
                for cc in range(4):
                    for tb in range(NTB):
                        ps = proj_fm(wt2, cc, tb, src=hid)
                        dst = ymlp[:, chh * 4 + cc, tb * TB:(tb + 1) * TB]
                        if jg == 0:
                            act(dst, ps, AF.Copy)
                        else:
                            tt(dst, dst, ps, ALU.add)
        xsb = [xs, av(136, 16, F32, "p (c t) -> p c t", c=8)]
        dma("sp", xsb[0], yres[0])
        dma("sp", xsb[1], yres[1])
        for tb in range(NTB):
            resid_norm(l, tb, yres, ymlp[:, :, tb * TB:(tb + 1) * TB], 5, xsb[tb % 2], None if last else (l + 1, 0, 1), preloaded=True)
            if tb + 2 < NTB:
                dma("sp", xsb[tb % 2], yres[tb + 2])

    xs0 = av(120, 16, F32, "p (c t) -> p c t", c=8)
    for tb in range(NTB):
        resid_norm(0, tb, xin, None, 0, xs0, (0, 0, 1))
    for l in range(depth):
        if dbg and l == 0:
            dma("pool", dbg_h, hT)
        if stop == "mod":

            if os.environ.get("DBGMOD", "1") == "1":
                dma("sp", dbg_mod, modraw[:, 0:2, :])
            break
        if dbg and l == 0:
            pass
        S.marks = getattr(S, "marks", [])
        for nm_, fn_, m_ in (("A", mixerA, 0), ("B", mixerB, 1), ("C", mixerC, 2), ("D", mixerD, 3)):
            S.marks.append(("L%d %s" % (l, nm_), len(S.streams["pe"])))
            if nm_ in parts:
                fn_(l)
            else:
                memset(YM[m_], 0.0)
        S.marks.append(("L%d merge" % l, len(S.streams["pe"])))
        if dbg and l == 0:
            for m_ in range(4):
                dma("pool", dbg_y[m_], YM[m_])
        merge_out(l)
        S.marks.append(("L%d mlp" % l, len(S.streams["pe"])))
        mlp(l, l == depth - 1)
        S.marks.append(("L%d end" % l, len(S.streams["pe"])))
    S.emit()
    return S


def _consts():
    m = np.arange(128)[:, None].astype(np.float32)
    n = np.arange(128)[None, :].astype(np.float32)
    cf = np.zeros((128, CF_N), np.float32)
    cf[:, CF_RELF:CF_RELF + 128] = np.maximum(n - m, 0)
    cf[:, CF_RELB:CF_RELB + 128] = np.maximum(m - n, 0)
    cf[:, CF_POS1:CF_POS1 + 128] = n + 1
    cf[:, CF_POSB:CF_POSB + 128] = 128 - n
    sp = np.zeros((128, 128), np.float32)
    for d in range(128):
        dd = d % 64
        base = d - dd
        if dd % 32 < 16:
            sp[base + dd + 16, d] = -1.0
        else:
            sp[base + dd - 16, d] = 1.0
    cf[:, CF_SPERM:CF_SPERM + 128] = sp
    cf[:, CF_MPOSF:CF_MPOSF + 8] = 127 - m
    cf[:, CF_MPOSB:CF_MPOSB + 8] = m
    cb = np.zeros((128, CB_N), np.float32)
    cb[:, CB_UM:CB_UM + 128] = 0.125 * (n >= m)
    cb[:, CB_LM:CB_LM + 128] = 0.125 * (m > n)
    cb[:, CB_BD:CB_BD + 128] = ((np.arange(128)[:, None] // 64) == (np.arange(128)[None, :] // 64))
    kc = np.arange(128)[:, None] % 64
    qc = 63 - np.arange(64)[None, :]
    cs = np.clip(qc - 8, 0, 48)
    cb[:, CB_COL:CB_COL + 64] = (kc >= cs) & (kc < cs + 16)
    cb[:, CB_ONES:CB_ONES + 128] = 1.0
    return cf, cb


def _rope_tables():
    t = np.arange(T)
    inv = (np.float32(10000.0) ** (-np.arange(16, dtype=np.float32) / np.float32(16))).astype(np.float32)
    ang_r = (t // 64).astype(np.float32)[:, None] * inv
    ang_c = (t % 64).astype(np.float32)[:, None] * inv
    ang = np.zeros((128, T), np.float32)
    for p in range(128):
        d = p % 64
        ang[p] = (ang_r if d < 32 else ang_c)[:, d % 16]
    rope = np.zeros((128, NTB, 2, TB), np.float32)
    rope[:, :, 0, :] = np.cos(ang).astype(np.float32).reshape(128, NTB, TB)
    rope[:, :, 1, :] = np.sin(ang).astype(np.float32).reshape(128, NTB, TB)
    return rope


def _tile(w):
    L, K, N = w.shape
    return np.ascontiguousarray(w.reshape(L, K // 128, 128, N // 512, 512).transpose(0, 3, 2, 1, 4))


def _fm(v):
    k = v.shape[-1] // 128
    r = v.reshape(v.shape[:-1] + (k, 128))
    return np.ascontiguousarray(np.moveaxis(r, -1, 0))


def _prep(I):
    f32 = np.float32
    A = lambda x: np.ascontiguousarray(x, dtype=f32)
    cf, cb = _consts()
    shared = {
        "ada_w": _tile(A(I["ada_w"])), "w_in": _tile(A(I["w_in"])),
        "w_br": np.ascontiguousarray(_tile(A(I["w_branch"]).reshape(DEPTH * 4, 512, D)).reshape(DEPTH, 4, 2, 128, 4, 512)),
        "w_out": _tile(A(I["w_out"])), "w1": _tile(A(I["mlp_w1"])),
        "w2": np.ascontiguousarray(_tile(A(I["mlp_w2"]).reshape(DEPTH * 4, 1024, D)).reshape(DEPTH, 4, 2, 128, 8, 512)),
        "adab": _fm(A(I["ada_b"]).reshape(DEPTH, 6 * D)).reshape(128, DEPTH, 48),
        "gains": np.ascontiguousarray(np.stack([_fm(A(I[k])) for k in ("norm_mix_pre", "norm_mix_post", "norm_ffn_pre", "norm_ffn_post")], axis=2)),
        "dlam": np.ascontiguousarray(np.broadcast_to(np.stack([A(I[k]) for k in ("diff_lq1", "diff_lk1", "diff_lq2", "diff_lk2")], axis=1)[None], (128, DEPTH, 4, 64))),
        "dnorm": np.ascontiguousarray(A(I["diff_norm"]).T),
        "rtheta": np.ascontiguousarray(np.broadcast_to(np.stack([A(I["ret_theta_fwd"]), A(I["ret_theta_bwd"])], axis=1)[None], (128, DEPTH, 2, 8))),
        "rnorm": _fm(A(I["ret_norm"])),
        "convw": _fm(A(I["lru_conv_w"])),
        "convb": _fm(A(I["lru_conv_b"])),
        "lrub": np.ascontiguousarray(np.stack([_fm(A(I[k])) for k in ("lru_ba_fwd", "lru_bx_fwd", "lru_lam_fwd", "lru_ba_bwd", "lru_bx_bwd", "lru_lam_bwd")], axis=2)),
        "cf": cf, "cb": cb,
    }
    rp = np.zeros((DEPTH, 8, 16, 128), f32)
    rp[:, :, 0:15, 48:79] = A(I["na_rpb"])
    shared["rpbp"] = rp
    lw = np.zeros((DEPTH, 128, 4, 4, 128), f32)
    for ki, k in enumerate(("lru_wa_fwd", "lru_wx_fwd", "lru_wa_bwd", "lru_wx_bwd")):
        W = A(I[k])
        for ch in range(4):
            for nb in range(2):
                lw[:, nb * 64:(nb + 1) * 64, ki, ch, nb * 64:(nb + 1) * 64] = W[:, 2 * ch + nb]
    shared["lruw"] = lw
    rope_s = _rope_tables()
    rope_p = np.zeros_like(rope_s)
    rope_p[:, :, 0, :] = 1.0
    maps = []
    for i in range(8):
        d = dict(shared)
        if i < 4:
            b = i
            x = A(I["x_sample"][b])
            cond = A(I["c"][b])
            d["flags"] = np.tile(np.array([[0.0, 1.0]], f32), (128, 1))
            d["rope"] = rope_s
            d["rfl"] = np.ones((128, 2, 16), f32)
            ck = A(I["cache_na_k"][b])
            d["c_nakT"] = np.ascontiguousarray(ck.reshape(DEPTH, 4, 2, 512, 64).transpose(0, 2, 4, 1, 3)).reshape(DEPTH, 128, 4, 512)
            cv = A(I["cache_na_v"][b])
            d["c_nav"] = np.ascontiguousarray(cv.reshape(DEPTH, 8, 4, 128, 64).transpose(0, 3, 2, 1, 4)).reshape(DEPTH, 128, 4, 512)
            dk = A(I["cache_diff_k"][b])
            d["c_dfkT"] = np.ascontiguousarray(dk.reshape(DEPTH, 2, 2, 2, 512, 64).transpose(0, 3, 5, 1, 2, 4)).reshape(DEPTH, 128, 4, 512)
            dv = A(I["cache_diff_v"][b])
            d["c_dfv"] = np.ascontiguousarray(dv.reshape(DEPTH, 4, 4, 128, 128).transpose(0, 3, 2, 1, 4)).reshape(DEPTH, 128, 4, 512)
            for nm, key in (("s_rf0", "state_ret_fwd"), ("s_rb0", "state_ret_bwd")):
                s_ = A(I[key][b])
                d[nm] = np.ascontiguousarray(s_.reshape(DEPTH, 4, 2, 64, 64).transpose(0, 2, 3, 1, 4)).reshape(DEPTH, 128, 256)
            d["h_l0"] = np.ascontiguousarray(np.stack([_fm(A(I["state_lru_fwd"][b])), _fm(A(I["state_lru_bwd"][b]))], axis=2))
        else:
            j = i - 4
            x = A(I["x_prompt"][8 * j:8 * j + 8]).reshape(T, D)
            cond = A(I["c_ctx"])
            d["flags"] = np.tile(np.array([[1.0, 0.0]], f32), (128, 1))
            d["rope"] = rope_p
            r = np.ones((128, 2, 16), f32)
            r[:, 0, 0::2] = 0.0
            r[:, 1, 1::2] = 0.0
            d["rfl"] = r
            for nm in ("c_nakT", "c_nav", "c_dfkT", "c_dfv"):
                d[nm] = np.zeros((DEPTH, 128, 4, 512), f32)
            d["s_rf0"] = np.zeros((DEPTH, 128, 256), f32)
            d["s_rb0"] = np.zeros((DEPTH, 128, 256), f32)
            d["h_l0"] = np.zeros((128, DEPTH, 2, 4), f32)
        d["xin"] = np.ascontiguousarray(x.reshape(NTB, TB, 8, 128).transpose(0, 3, 2, 1))
        d["cond"] = np.ascontiguousarray(cond.reshape(8, 128).T)
        maps.append(d)
    return maps


def _assemble(res):
    f32 = np.float32

    def yfull(r):
        return np.ascontiguousarray(np.asarray(r["yres"], f32).transpose(0, 3, 2, 1)).reshape(T, D)

    y_s = np.stack([yfull(res[i]) for i in range(4)], axis=0)
    y_p = np.concatenate([yfull(res[i]).reshape(8, 256, D) for i in range(4, 8)], axis=0)
    nak, nav, dfk, dfv, srf, srb, hlf, hlb = [], [], [], [], [], [], [], []
    for i in range(4, 8):
        r = res[i]
        a = np.asarray(r["o_nakT"], f32).reshape(DEPTH, 2, 64, 4, 8, 256)
        nak.append(a.transpose(4, 0, 3, 1, 5, 2).reshape(8, DEPTH, 8, 256, 64))
        a = np.asarray(r["o_nav"], f32).reshape(DEPTH, 8, 256, 8, 64)
        nav.append(a.transpose(1, 0, 3, 2, 4))
        a = np.asarray(r["o_dfkT"], f32).reshape(DEPTH, 2, 64, 2, 2, 8, 256)
        dfk.append(a.transpose(5, 0, 3, 4, 1, 6, 2).reshape(8, DEPTH, 2, 4, 256, 64))
        a = np.asarray(r["o_dfv"], f32).reshape(DEPTH, 8, 256, 4, 128)
        dfv.append(a.transpose(1, 0, 3, 2, 4))
        for lst, key in ((srf, "o_srf"), (srb, "o_srb")):
            a = np.asarray(r[key], f32).reshape(DEPTH, 8, 2, 64, 4, 64)
            lst.append(a.transpose(1, 0, 4, 2, 3, 5).reshape(8, DEPTH, 8, 64, 64))
        a = np.asarray(r["o_hl"], f32)
        hlf.append(a[:, :, 0].transpose(3, 0, 2, 1).reshape(8, DEPTH, 512))
        hlb.append(a[:, :, 1].transpose(3, 0, 2, 1).reshape(8, DEPTH, 512))
    cat = lambda x: np.ascontiguousarray(np.concatenate(x, axis=0), dtype=f32)
    return (np.ascontiguousarray(y_p, dtype=f32), np.ascontiguousarray(y_s, dtype=f32), cat(nak), cat(nav), cat(dfk), cat(dfv),
            cat(srf), cat(srb), cat(hlf), cat(hlb))


_PROG = {}


def kernel(**inputs):
    if "p" not in _PROG:
        _PROG["p"] = build()
    S = _PROG["p"]
    maps = _prep(inputs)
    res = run_bass_kernel_spmd(S.nc, maps, core_ids=list(range(8)))
    return _assemble(res.results)
```

```python
import math
import os
import numpy as np
import concourse.bass as bass
import concourse.mybir as mybir
from concourse.bass_utils import run_bass_kernel_spmd
from contextlib import ExitStack

F32 = mybir.dt.float32
BF16 = mybir.dt.bfloat16
AF = mybir.ActivationFunctionType
ALU = mybir.AluOpType
_DT_SIZE = {F32: 4, BF16: 2}
ENGS = ("pe", "act", "dve", "pool", "sp")
N_DMA_SEMS = 8

DEPTH = 4
D = 1024
T = 2048
NCH = 8
TB = 512
NTB = 4
NTT = 16
EPS = 1e-6
KIB = 256


def _box(ap, tracked_dram):
    if not hasattr(ap, "tensor"):
        ap = ap[:]
    t = ap.tensor
    pat = ap.ap
    off = ap.offset
    sz = _DT_SIZE[ap.dtype]
    if str(ap.space) == "DRAM":
        if t.name not in tracked_dram:
            return None
        lo = hi = off
        for st, cn in pat:
            if st < 0:
                lo += st * (cn - 1)
            else:
                hi += st * (cn - 1)
        return (t.name, 0, 1, lo * sz, (hi + 1) * sz)
    if str(ap.space) == "PSUM":
        return (t.name, 0, 128, 0, 2048)
    pstep, pcnt = pat[0]
    if pstep <= 0:
        pstep = 1 << 40
    p0 = off // pstep
    f = off % pstep
    lo = hi = f
    for st, cn in pat[1:]:
        if st < 0:
            lo += st * (cn - 1)
        else:
            hi += st * (cn - 1)
    return (t.name, p0, p0 + pcnt, lo * sz, (hi + 1) * sz)


def _overlap(a, b):
    return a[1] < b[2] and b[1] < a[2] and a[3] < b[4] and b[3] < a[4]


def _covers(a, b):
    return a[1] <= b[1] and a[2] >= b[2] and a[3] <= b[3] and a[4] >= b[4]


class Op:
    __slots__ = ("eng", "fn", "deps", "idx", "lidx", "is_dma", "sig", "sigval", "dsem", "dval")

    def __init__(self, eng, fn, is_dma):
        self.eng = eng
        self.fn = fn
        self.is_dma = is_dma
        self.deps = set()
        self.sig = False
        self.sigval = 0
        self.dsem = None
        self.dval = 0


class Sched:
    def __init__(self):
        self.nc = bass.Bass("TRN2", target_bir_lowering=False)
        self.ops = []
        self.streams = {e: [] for e in ENGS}
        self.recs = {}
        self.frozen = set()
        self.tracked_dram = set()
        self.stack = ExitStack()
        self.dma_count = {e: 0 for e in ENGS}
        self.dma_last = {}

    def dram_in(self, name, shape, dtype=F32):
        return self.nc.dram_tensor(name, list(shape), dtype, kind="ExternalInput").ap()

    def dram_out(self, name, shape, dtype=F32):
        return self.nc.dram_tensor(name, list(shape), dtype, kind="ExternalOutput").ap()

    def sb(self, name, shape, dtype=F32):
        return self.stack.enter_context(self.nc.sbuf_tensor(name, list(shape), dtype))

    def ps(self, name, shape, dtype=F32):
        return self.stack.enter_context(self.nc.psum_tensor(name, list(shape), dtype))

    def add(self, eng, fn, reads=(), writes=(), dma=False):
        op = Op(eng, fn, dma)
        op.idx = len(self.ops)
        op.lidx = len(self.streams[eng])
        self.ops.append(op)
        self.streams[eng].append(op)
        td = self.tracked_dram
        rboxes = [b for b in (_box(ap, td) for ap in reads) if b is not None]
        wboxes = [b for b in (_box(ap, td) for ap in writes) if b is not None]
        for b in rboxes:
            is_psum = b[0].startswith("ps") and b[4] - b[3] == 2048 and b[2] - b[1] == 128
            for r in self.recs.get(b[0], ()):
                if r[2]:
                    if _overlap(r[0], b):
                        op.deps.add(r[1])
                elif is_psum and r[3] != eng:
                    op.deps.add(r[1])
        for b in wboxes:
            for r in self.recs.get(b[0], ()):
                if _overlap(r[0], b):
                    op.deps.add(r[1])
        for b in wboxes:
            lst = self.recs.setdefault(b[0], [])
            keep = [r for r in lst if not (_overlap(r[0], b) and _covers(b, r[0]))]
            keep.append([b, op.idx, True, eng])
            self.recs[b[0]] = keep
        for b in rboxes:
            if b[0] in self.frozen:
                continue
            lst = self.recs.setdefault(b[0], [])
            if not dma:
                for r in lst:
                    if (not r[2]) and r[3] == eng and r[0] == b and not self.ops[r[1]].is_dma:
                        r[1] = op.idx
                        break
                else:
                    lst.append([b, op.idx, False, eng])
            else:
                lst.append([b, op.idx, False, eng])
        if dma:
            n = self.dma_count[eng]
            self.dma_count[eng] = n + 1
            slot = n % N_DMA_SEMS
            op.dsem = (eng, slot)
            op.dval = 16 * (n // N_DMA_SEMS + 1)
            prev = self.dma_last.get((eng, slot))
            if prev is not None:
                op.deps.add(prev.idx)
            self.dma_last[(eng, slot)] = op
        return op

    def emit(self):
        nc = self.nc
        ops = self.ops
        for op in ops:
            for d in list(op.deps):
                dop = ops[d]
                if dop.is_dma:
                    continue
                if dop.eng == op.eng and not op.is_dma:
                    if dop.eng == "pe":
                        op.deps.discard(d)
                        continue
                dop.sig = True
        for e in ENGS:
            c = 0
            for op in self.streams[e]:
                if (not op.is_dma) and op.sig:
                    c += 1
                    op.sigval = c
        st = self.stack
        esem = {e: st.enter_context(nc.semaphore("s_" + e)) for e in ENGS}
        dsem = {}
        for e in ENGS:
            if self.dma_count[e]:
                for s in range(N_DMA_SEMS):
                    dsem[(e, s)] = st.enter_context(nc.semaphore("d_%s%d" % (e, s)))
        block = st.enter_context(nc.Block())
        handles = {"pe": block.tensor, "act": block.scalar, "dve": block.vector, "pool": block.gpsimd, "sp": block.sync}
        self.n_waits = 0

        def make(e):
            stream = self.streams[e]

            def body(eng):
                waited = {}
                for op in stream:
                    need = {}
                    for d in op.deps:
                        dop = ops[d]
                        if dop.is_dma:
                            key = ("d",) + dop.dsem
                            val = dop.dval
                            sem = dsem[dop.dsem]
                        else:
                            key = ("e", dop.eng)
                            val = dop.sigval
                            sem = esem[dop.eng]
                        if val > need.get(key, (0, None))[0]:
                            need[key] = (val, sem)
                    for key, (val, sem) in need.items():
                        if waited.get(key, 0) >= val:
                            continue
                        waited[key] = val
                        eng.wait_ge(sem, val)
                        self.n_waits += 1
                    ins = op.fn(eng)
                    if op.is_dma:
                        ins.then_inc(dsem[op.dsem], 16)
                    elif op.sig:
                        ins.then_inc(esem[e], 1)
                for (ee, s), last in self.dma_last.items():
                    if ee == e:
                        eng.wait_ge(dsem[(ee, s)], last.dval)
            return body

        for e in ENGS:
            if self.streams[e]:
                handles[e](make(e))
        st.close()
        return nc


LAM_INIT = [0.8 - 0.6 * math.exp(-0.3 * l) for l in range(DEPTH)]
CF_RELF, CF_RELB, CF_POS1, CF_POSB, CF_SPERM, CF_MPOSF, CF_MPOSB, CF_N = 0, 128, 256, 384, 512, 640, 648, 656
CB_UM, CB_LM, CB_BD, CB_COL, CB_ONES, CB_N = 0, 128, 256, 384, 448, 576
VON = 8192
CVON = 2048
NSTG = 6


def build(depth=DEPTH, dbg=False, parts="ABCD", stop=None):
    S = Sched()
    nc = S.nc
    di, do = S.dram_in, S.dram_out
    xin = di("xin", [NTB, 128, 8, TB])
    yres = do("yres", [NTB, 128, 8, TB])
    S.tracked_dram.add("yres")
    cond = di("cond", [128, 8])
    flags_d = di("flags", [128, 2])
    rope_d = di("rope", [128, NTB, 2, TB])
    rfl_d = di("rfl", [128, 2, 16])
    ada_w = di("ada_w", [DEPTH, 12, 128, 8, 512])
    w_in = di("w_in", [DEPTH, 20, 128, 8, 512])
    w_br = di("w_br", [DEPTH, 4, 2, 128, 4, 512])
    w_out = di("w_out", [DEPTH, 2, 128, 8, 512])
    w1 = di("w1", [DEPTH, 8, 128, 8, 512])
    w2 = di("w2", [DEPTH, 4, 2, 128, 8, 512])
    adab_d = di("adab", [128, DEPTH, 48])
    gains_d = di("gains", [128, DEPTH, 4, 8])
    rpbp = di("rpbp", [DEPTH, 8, 16, 128])
    dlam_d = di("dlam", [128, DEPTH, 4, 64])
    dnorm_d = di("dnorm", [128, DEPTH])
    rtheta_d = di("rtheta", [128, DEPTH, 2, 8])
    rnorm_d = di("rnorm", [128, DEPTH, 4])
    convw_d = di("convw", [128, DEPTH, 4, 4])
    convb_d = di("convb", [128, DEPTH, 4])
    lruw_d = di("lruw", [DEPTH, 128, 4, 4, 128])
    lrub_d = di("lrub", [128, DEPTH, 6, 4])
    c_nakT = di("c_nakT", [DEPTH, 128, 4, 512])
    c_nav = di("c_nav", [DEPTH, 128, 4, 512])
    c_dfkT = di("c_dfkT", [DEPTH, 128, 4, 512])
    c_dfv = di("c_dfv", [DEPTH, 128, 4, 512])
    s_rf0 = di("s_rf0", [DEPTH, 128, 256])
    s_rb0 = di("s_rb0", [DEPTH, 128, 256])
    h_l0_d = di("h_l0", [128, DEPTH, 2, 4])
    cf_d = di("cf", [128, CF_N])
    cb_d = di("cb", [128, CB_N])

    o_nakT = do("o_nakT", [DEPTH, 128, 4, T])
    o_nav = do("o_nav", [DEPTH, NTT, 128, 512])
    o_dfkT = do("o_dfkT", [DEPTH, 128, 4, T])
    o_dfv = do("o_dfv", [DEPTH, NTT, 128, 512])
    o_srf = do("o_srf", [DEPTH, 8, 128, 256])
    o_srb = do("o_srb", [DEPTH, 8, 128, 256])
    o_hl = do("o_hl", [DEPTH, 128, 2, 4, 8])
    if dbg:
        dbg_y = do("dbg_y", [4, 128, 4, T])
        dbg_h = do("dbg_h", [128, 8, T])
        dbg_mod = do("dbg_mod", [128, 96])
        dbg_mod2 = do("dbg_mod2", [128, 96])

    ARENA_K = 160
    arena = S.sb("arena", [128, ARENA_K * KIB], F32)

    def av(off_k, size_k, dtype, pattern=None, **kw):
        a = arena[:, int(off_k * KIB):int((off_k + size_k) * KIB)]
        if dtype == BF16:
            a = a.bitcast(BF16)
        if pattern:
            a = a.rearrange(pattern, **kw)
        return a

    hT = av(0, 32, BF16, "p (c t) -> p c t", c=8)
    WR = [av(32 + 8 * i, 8, BF16, "p (k n) -> p k n", k=8) for i in range(3)]
    YM = [av(56 + 16 * m, 16, BF16, "p (c t) -> p c t", c=4) for m in range(4)]
    ymlp = av(56, 64, F32, "p (c t) -> p c t", c=8)
    vbuf_t = S.sb("vbuf", [128, VON + 64], BF16)
    cvbuf_t = S.sb("cvbuf", [128, CVON + 64], BF16)
    vbuf = vbuf_t[:, :]
    cvbuf = cvbuf_t[:, :]
    stg = S.sb("stg", [128, NSTG, 512], F32)
    cf = S.sb("cf_sb", [128, CF_N], F32)
    cb = S.sb("cb_sb", [128, CB_N], BF16)
    cst2 = S.sb("cst2", [128, 4, 256], F32)
    sm = S.sb("sm", [128, 512], F32)
    modraw = S.sb("modraw", [128, DEPTH, 48], F32)
    modc = S.sb("modc", [128, DEPTH + 1, 6, 8], F32)
    adab = S.sb("adab_sb", [128, DEPTH, 48], F32)
    gains = S.sb("gains_sb", [128, DEPTH, 4, 8], F32)
    flags = S.sb("flags_sb", [128, 2], F32)
    rfl = S.sb("rfl_sb", [128, 2, 16], F32)
    dnorm = S.sb("dnorm_sb", [128, DEPTH], F32)
    rtheta = S.sb("rtheta_sb", [128, DEPTH, 2, 8], F32)
    rnorm = S.sb("rnorm_sb", [128, DEPTH, 4], F32)
    convw = S.sb("convw_sb", [128, DEPTH, 4, 4], F32)
    convb = S.sb("convb_sb", [128, DEPTH, 4], F32)
    lrub = S.sb("lrub_sb", [128, DEPTH, 6, 4], F32)
    h_l0 = S.sb("h_l0_sb", [128, DEPTH, 2, 4], F32)
    hl_sb = S.sb("hl_sb", [128, 2, 4, 8], F32)
    scond = S.sb("scond", [128, 8], BF16)
    condsb = S.sb("condsb", [128, 8], F32)
    PS = [S.ps("ps%d" % i, [128, 512], F32) for i in range(8)]

    st = {"ps": 0, "stg": 0, "wr": 0, "stgL": 0}

    def stageL():
        i_ = st["stgL"] % 2
        s_ = cst2[:, 2 * i_:2 * i_ + 2, :].rearrange("p a b -> p (a b)")
        st["stgL"] += 1
        return s_

    def psb():
        b = PS[st["ps"] % 6][:, :]
        st["ps"] += 1
        return b

    def stage():
        s = stg[:, st["stg"] % NSTG, :]
        st["stg"] += 1
        return s

    def stage_bf():
        return stage().bitcast(BF16)[:, 0:512]

    def mm(out, lhsT, rhs, start=True, stop=True):
        S.add("pe", lambda e: e.matmul(out, lhsT=lhsT, rhs=rhs, start=start, stop=stop), reads=[lhsT, rhs], writes=[out])

    def act(out, in_, func, bias=None, scale=None):
        kw = {}
        rd = [in_]
        if bias is not None:
            kw["bias"] = bias
            if not isinstance(bias, (int, float)):
                rd.append(bias)
        if scale is not None:
            kw["scale"] = scale
            if not isinstance(scale, (int, float)):
                rd.append(scale)
        S.add("act", lambda e: e.activation(out=out, in_=in_, func=func, **kw), reads=rd, writes=[out])

    def tt(out, in0, in1, op, eng="dve"):
        S.add(eng, lambda e: e.tensor_tensor(out=out, in0=in0, in1=in1, op=op), reads=[in0, in1], writes=[out])

    def ts(out, in0, s1, s2, op0, op1=None, eng="dve"):
        rd = [in0]
        for s in (s1, s2):
            if s is not None and not isinstance(s, (int, float)):
                rd.append(s)
        if op1 is None:
            S.add(eng, lambda e: e.tensor_scalar(out=out, in0=in0, scalar1=s1, scalar2=None, op0=op0), reads=rd, writes=[out])
        else:
            S.add(eng, lambda e: e.tensor_scalar(out=out, in0=in0, scalar1=s1, scalar2=s2, op0=op0, op1=op1), reads=rd, writes=[out])

    def stt(out, in0, scalar, in1, op0, op1):
        rd = [in0, in1]
        if not isinstance(scalar, (int, float)):
            rd.append(scalar)
        S.add("dve", lambda e: e.scalar_tensor_tensor(out=out, in0=in0, scalar=scalar, in1=in1, op0=op0, op1=op1), reads=rd, writes=[out])

    def cp(out, in_, eng="dve"):
        if eng == "act":
            act(out, in_, AF.Copy)
        else:
            S.add(eng, lambda e: e.tensor_copy(out=out, in_=in_), reads=[in_], writes=[out])

    def recip(out, in_):
        S.add("dve", lambda e: e.reciprocal(out=out, in_=in_), reads=[in_], writes=[out])

    def memset(ap, val, eng="dve"):
        S.add(eng, lambda e: e.memset(ap, val), writes=[ap])

    def dma(eng, out, in_):
        S.add(eng, lambda e: e.dma_start(out=out, in_=in_), reads=[in_], writes=[out], dma=True)

    def load_w(src2d, nk, ncols):
        slot = WR[st["wr"] % 3]
        st["wr"] += 1
        dst = slot[:, 0:nk, 0:ncols]
        dma("pool", dst, src2d)
        return dst

    def proj_fm(wt, cc, tb, nk=8, src=None, n0=None, n=TB):
        src = hT if src is None else src
        n0 = tb * TB if n0 is None else n0
        ps = psb()
        for k in range(nk):
            mm(ps[:, 0:n], wt[:, k, cc * 128:(cc + 1) * 128], src[:, k, n0:n0 + n], k == 0, k == nk - 1)
        return ps

    def proj_tm(wt, ti):
        ps = psb()
        for k in range(8):
            mm(ps, hT[:, k, ti * 128:(ti + 1) * 128], wt[:, k, :], k == 0, k == 7)
        return ps

    ones_bf = cb[:, CB_ONES:CB_ONES + 128]
    bd_bf = cb[:, CB_BD:CB_BD + 128]
    epsc = sm[:, 0:1]
    onec = sm[:, 1:2]

    def vh(tile_i, h):
        off = tile_i * 512 + h * 64
        return vbuf[:, off:off + 64]

    def cvh(kc, h):
        off = kc * 512 + h * 64
        return cvbuf[:, off:off + 64]

    dma("sp", cf[:], cf_d)
    dma("pool", cb[:], cb_d)
    for dst, srcd in ((flags, flags_d), (rfl, rfl_d), (condsb, cond), (adab, adab_d), (gains, gains_d), (dnorm, dnorm_d),
                      (rtheta, rtheta_d), (rnorm, rnorm_d), (convw, convw_d), (convb, convb_d), (lrub, lrub_d), (h_l0, h_l0_d)):
        dma("sp", dst[:], srcd)
    memset(sm[:, 0:1], EPS)
    memset(sm[:, 1:2], 1.0)
    memset(vbuf[:, VON:VON + 64], 1.0)
    memset(cvbuf[:, CVON:CVON + 64], 1.0)
    act(scond[:], condsb[:], AF.Silu)

    def modulation(l):
        for j in range(12):
            wt = load_w(ada_w[l, j], 8, 512)
            ps = psb()
            for k in range(8):
                mm(ps[0:1, :], scond[:, k:k + 1], wt[:, k, :], k == 0, k == 7)
            row = stage()
            act(row[0:1, :], ps[0:1, :], AF.Copy)
            ps2 = psb()
            for q in range(4):
                mm(ps2[:, q:q + 1], row[0:1, q * 128:(q + 1) * 128], onec[0:1, 0:1])
            tt(modraw[:, l, j * 4:(j + 1) * 4], ps2[:, 0:4], adab[:, l, j * 4:(j + 1) * 4], ALU.add)
        mr = modraw[:, l, :]
        stt(modc[:, l, 0, :], mr[:, 8:16], 1.0, gains[:, l, 0, :], ALU.add, ALU.mult)
        cp(modc[:, l, 1, :], mr[:, 0:8])
        tt(modc[:, l, 2, :], mr[:, 16:24], gains[:, l, 1, :], ALU.mult)
        stt(modc[:, l, 3, :], mr[:, 32:40], 1.0, gains[:, l, 2, :], ALU.add, ALU.mult)
        cp(modc[:, l, 4, :], mr[:, 24:32])
        tt(modc[:, l, 5, :], mr[:, 40:48], gains[:, l, 3, :], ALU.mult)

    modulation(0)

    def resid_norm(l, tb, src, ysrc, ares, xs, hspec, preloaded=False):
        if not preloaded:
            dma("sp", xs, src[tb])
        if ysrc is not None:
            ps = psb()
            for c in range(8):
                sq = stage_bf()
                act(sq, ysrc[:, c, :], AF.Square)
                mm(ps, ones_bf, sq, c == 0, c == 7)
            rs = stageL()
            act(rs, ps, AF.Ln, bias=epsc, scale=1.0 / D)
            act(rs, rs, AF.Exp, scale=-0.5)
            for c in range(8):
                tmp = stage()
                stt(tmp, ysrc[:, c, :], modc[:, l, ares, c:c + 1], rs, ALU.mult, ALU.mult)
                tt(xs[:, c, :], xs[:, c, :], tmp, ALU.add, eng=("pool" if c % 2 == 0 else "dve"))
            dma("sp", yres[tb], xs)
        if hspec is not None:
            ln, an, bn = hspec
            ps = psb()
            for c in range(8):
                sq = stage_bf()
                act(sq, xs[:, c, :], AF.Square)
                mm(ps, ones_bf, sq, c == 0, c == 7)
            rs = stageL()
            act(rs, ps, AF.Ln, bias=epsc, scale=1.0 / D)
            act(rs, rs, AF.Exp, scale=-0.5)
            for c in range(8):
                tmp = stage()
                stt(tmp, xs[:, c, :], modc[:, ln, an, c:c + 1], rs, ALU.mult, ALU.mult)
                act(hT[:, c, tb * TB:(tb + 1) * TB], tmp, AF.Identity, bias=modc[:, ln, bn, c:c + 1])

    def run_pipeline(items, look):
        toks = {}
        n = len(items)
        for i in range(min(look, n)):
            toks[i] = items[i][0]()
        for i in range(n):
            if i + look < n:
                toks[i + look] = items[i + look][0]()
            items[i][1](toks.pop(i))
            pipe_now[0] = i
            while deferred and deferred[0][0] <= i - 2:
                deferred.pop(0)[1]()
        while deferred:
            deferred.pop(0)[1]()

    deferred = []
    pipe_now = [0]

    def defer(fn):
        deferred.append((pipe_now[0] + 1, fn))

    def mixerA(l):
        kT = av(120, 16, BF16, "p (c t) -> p c t", c=4)
        qT = av(136, 16, BF16, "p (c t) -> p c t", c=4)
        EB2 = av(152, 7.5, BF16, "p (h r q) -> p h r q", h=4, r=15)
        EBraw = av(72, 15, F32, "p (h r q) -> p h r q", h=4, r=15)
        ckT = av(87, 4, BF16, "p (c t) -> p c t", c=4)
        Eloc = [av(91 + 5 * i, 5, BF16, "p (j h q) -> p j h q", j=5, h=4) for i in range(2)]
        Ectx = [av(101 + 4 * i, 4, BF16, "p (h k) -> p h k", h=4) for i in range(2)]
        yap = av(109, 2, F32, "p (c q) -> p c q", c=2)
        yas = av(111, 2, F32, "p (c q) -> p c q", c=2)
        Ep = [av(113 + i, 1, BF16) for i in range(4)]
        epi = [0]
        ya = YM[0]
        wk = load_w(w_in[l, 1], 8, 512)
        for c in range(4):
            for tb in range(NTB):
                ps = proj_fm(wk, c, tb)
                sg = stage()
                cp(sg, ps)
                act(kT[:, c, tb * TB:(tb + 1) * TB], sg, AF.Copy)
                dma("sp", o_nakT[l][:, c, tb * TB:(tb + 1) * TB], sg)
        wv = load_w(w_in[l, 2], 8, 512)
        for ti in range(NTT):
            ps = proj_tm(wv, ti)
            sg = stage()
            cp(sg, ps)
            act(vbuf[:, ti * 512:(ti + 1) * 512], sg, AF.Copy)
            dma("sp", o_nav[l][ti], sg)
        wq = load_w(w_in[l, 0], 8, 512)
        for c in range(4):
            for tb in range(NTB):
                ps = proj_fm(wq, c, tb)
                act(qT[:, c, tb * TB:(tb + 1) * TB], ps, AF.Copy)
        dma("pool", ckT, c_nakT[l])
        dma("pool", cvbuf[:, 0:CVON].rearrange("p (k n) -> p k n", k=4), c_nav[l])
        colm = bass.AP(cb.tensor if hasattr(cb, "tensor") else cb, CB_COL, [[CB_N, 128], [0, 60], [1, 64]])
        for hg in range(2):
            for a in range(2):
                for hh in range(4):
                    srcap = bass.AP(rpbp.tensor, rpbp[l, 4 * hg + hh, a, 0:1].offset, [[1, 64], [128, 15], [1, 64]])
                    if os.environ.get("A_NOEB"):
                        memset(EBraw[a * 64:(a + 1) * 64, hh, :, :], 0.0)
                    else:
                        dma("sp", EBraw[a * 64:(a + 1) * 64, hh, :, :], srcap)
            if not os.environ.get("A_NOEB2"):
                act(EB2[:, :, :, :], EBraw[:, :, :, :], AF.Exp)
                e2f = EB2.rearrange("p h r q -> p (h r) q")
                if not os.environ.get("A_NOCOLM"):
                    tt(e2f, e2f, colm, ALU.mult)
            items = []
            for s in range(8):
                for cl in range(2):
                    def f1(s=s, cl=cl):
                        c = 2 * hg + cl
                        Es = []
                        for hs in range(2):
                            hp = hs * 64
                            psS = psb()
                            for kc in range(2):
                                mm(psS[:, kc * 256:(kc + 1) * 256], kT[hp:hp + 64, c, (2 * s + kc) * 128:(2 * s + kc + 1) * 128],
                                   qT[hp:hp + 64, c, s * 256:(s + 1) * 256])
                            E = Ep[epi[0] % 4]
                            epi[0] += 1
                            act(E, psS, AF.Exp, scale=0.125)
                            Es.append(E)
                        return Es

                    def f2(Es, s=s, cl=cl):
                        c = 2 * hg + cl
                        psO = psb()
                        for hs in range(2):
                            h = 2 * c + hs
                            E = Es[hs]
                            for kc in range(2):
                                mm(psO[0:64, hs * 256:(hs + 1) * 256], vh(2 * s + kc, h), E[:, kc * 256:(kc + 1) * 256], kc == 0, kc == 1)
                            for kc in range(2):
                                mm(psO[64:128, hs * 256:(hs + 1) * 256], ones_bf[:, 0:64], E[:, kc * 256:(kc + 1) * 256], kc == 0, kc == 1)
                        rd = stage()
                        recip(rd[64:128, :], psO[64:128, :])
                        for hs in range(2):
                            tt(yap[hs * 64:(hs + 1) * 64, cl, :], psO[0:64, hs * 256:(hs + 1) * 256], rd[64:128, hs * 256:(hs + 1) * 256], ALU.mult)
                    items.append((f1, f2))
                for i in (2 * s, 2 * s + 1):
                    if i < 2:
                        js = [0, 1, 2, 3]
                    elif i > 13:
                        js = [12, 13, 14, 15]
                    else:
                        js = [i - 2, i - 1, i, i + 1, i + 2]

                    def f1(i=i, js=js):
                        nj = len(js)
                        El = Eloc[i % 2]
                        Ec = Ectx[i % 2]
                        for hh in range(4):
                            h = 4 * hg + hh
                            c = h // 2
                            hp = (h % 2) * 64
                            q_ = qT[hp:hp + 64, c, i * 128:(i + 1) * 128]
                            psL = psb()
                            psL2 = psb() if nj == 5 else None
                            psC = psb()
                            for idx, j in enumerate(js):
                                tgt = psL[:, idx * 128:(idx + 1) * 128] if idx < 4 else psL2[:, 0:128]
                                mm(tgt, kT[hp:hp + 64, c, j * 128:(j + 1) * 128], q_)
                            for kc in range(4):
                                mm(psC[:, kc * 128:(kc + 1) * 128], ckT[hp:hp + 64, c, kc * 128:(kc + 1) * 128], q_)
                            act(El[:, 0:4, hh, :], psL.rearrange("p (j q) -> p j q", j=4), AF.Exp, scale=0.125)
                            if nj == 5:
                                act(El[:, 4, hh, :], psL2[:, 0:128], AF.Exp, scale=0.125)
                            act(Ec[:, hh, :], psC, AF.Exp, scale=0.125)
                        for idx, j in enumerate(js):
                            for b in range(2):
                                dr0 = 2 * (j - i) - b + 7
                                assert 0 <= dr0 <= 14
                                tt(El[:, idx, :, b * 64:(b + 1) * 64], El[:, idx, :, b * 64:(b + 1) * 64], EB2[:, :, dr0, ::-1], ALU.mult)
                        if nj == 5:
                            memset(El[0:64, 0, :, 64:128], 0.0)
                            memset(El[0:64, 4, :, 0:64], 0.0)
                            memset(El[64:128, 4, :, :], 0.0)
                        return (El, Ec)

                    def f2(tok, i=i, js=js, s=s):
                        El, Ec = tok
                        psO = psb()
                        for hh in range(4):
                            h = 4 * hg + hh
                            for idx, j in enumerate(js):
                                mm(psO[0:64, hh * 128:(hh + 1) * 128], vh(j, h), El[:, idx, hh, :], idx == 0, False)
                            for kc in range(4):
                                mm(psO[0:64, hh * 128:(hh + 1) * 128], cvh(kc, h), Ec[:, hh, kc * 128:(kc + 1) * 128], False, kc == 3)
                            for idx, j in enumerate(js):
                                mm(psO[64:128, hh * 128:(hh + 1) * 128], ones_bf[:, 0:64], El[:, idx, hh, :], idx == 0, False)
                            for kc in range(4):
                                mm(psO[64:128, hh * 128:(hh + 1) * 128], ones_bf[:, 0:64], Ec[:, hh, kc * 128:(kc + 1) * 128], False, kc == 3)
                        rd = stage()
                        recip(rd[64:128, :], psO[64:128, :])
                        for hh in range(4):
                            tt(yas[(hh % 2) * 64:(hh % 2) * 64 + 64, hh // 2, (i % 2) * 128:(i % 2) * 128 + 128],
                               psO[0:64, hh * 128:(hh + 1) * 128], rd[64:128, hh * 128:(hh + 1) * 128], ALU.mult)
                        if i % 2 == 1:
                            t_ = stage().rearrange("p (c q) -> p c q", c=2)
                            ts(t_, yas, flags[:, 1:2], None, ALU.mult)
                            stt(ya[:, 2 * hg:2 * hg + 2, s * 256:(s + 1) * 256], yap, flags[:, 0:1], t_, ALU.mult, ALU.add)
                    items.append((f1, f2))
            run_pipeline(items, 1)

    def mixerB(l):
        kT = av(120, 16, BF16, "p (c t) -> p c t", c=4)
        qT = av(136, 16, BF16, "p (c t) -> p c t", c=4)
        ckT = av(152, 4, BF16, "p (c t) -> p c t", c=4)
        ropes = av(88, 16, F32, "p (b s t) -> p b s t", b=4, s=2)
        Er = [av(104 + i, 1, BF16) for i in range(4)] + [av(116 + i, 1, BF16) for i in range(2)]
        Eacc = av(118, 2, F32)
        onesf = av(152 + 4, 0.5, F32)
        ybp = av(108, 2, F32)
        ybs = av(110, 2, F32)
        o1n = av(112, 2, F32)
        ybh = av(114, 2, F32)
        yb = YM[1]
        sperm = cf[:, CF_SPERM:CF_SPERM + 128]
        ones128 = ones_bf
        dma("sp", ropes, rope_d)
        dl = stage().rearrange("p (a d) -> p a d", a=8)
        dma("sp", dl[:, 0:4, :], dlam_d[:, l, :, :])
        tt(dl[:, 4, :], dl[:, 0, :], dl[:, 1, :], ALU.mult)
        tt(dl[:, 5, :], dl[:, 2, :], dl[:, 3, :], ALU.mult)
        S.add("dve", lambda e: e.tensor_reduce(out=sm[:, 8:10], in_=dl[:, 4:6, :], axis=mybir.AxisListType.X, op=ALU.add),
              reads=[dl[:, 4:6, :]], writes=[sm[:, 8:10]])
        act(sm[:, 10:12], sm[:, 8:10], AF.Exp)
        tt(sm[:, 12:13], sm[:, 11:12], sm[:, 10:11], ALU.subtract)
        ts(sm[:, 12:13], sm[:, 12:13], -LAM_INIT[l], None, ALU.add)
        neglam = sm[:, 12:13]
        ts(sm[:, 13:14], dnorm[:, l:l + 1], 1.0 - LAM_INIT[l], None, ALU.mult)
        dnc = sm[:, 13:14]

        def rot_proj(wt, dst, outd):
            for c in range(4):
                for tb in range(NTB):
                    ps = proj_fm(wt, c, tb)
                    zs = stage()
                    act(zs, ps, AF.Copy)
                    ps2 = psb()
                    mm(ps2, sperm, zs)
                    t1 = stage()
                    tt(t1, zs, ropes[:, tb, 0, :], ALU.mult)
                    t2 = stage()
                    tt(t2, ps2, ropes[:, tb, 1, :], ALU.mult)
                    tt(t1, t1, t2, ALU.add, eng="pool")
                    act(dst[:, c, tb * TB:(tb + 1) * TB], t1, AF.Copy)
                    if outd is not None:
                        dma("sp", outd[:, c, tb * TB:(tb + 1) * TB], t1)

        wk = load_w(w_in[l, 4], 8, 512)
        rot_proj(wk, kT, o_dfkT[l])
        wv = load_w(w_in[l, 5], 8, 512)
        for ti in range(NTT):
            ps = proj_tm(wv, ti)
            sg = stage()
            cp(sg, ps)
            act(vbuf[:, ti * 512:(ti + 1) * 512], sg, AF.Copy)
            dma("sp", o_dfv[l][ti], sg)
        wq = load_w(w_in[l, 3], 8, 512)
        rot_proj(wq, qT, None)
        dma("pool", ckT, c_dfkT[l])
        dma("pool", cvbuf[:, 0:CVON].rearrange("p (k n) -> p k n", k=4), c_dfv[l])
        psOa, psDa = PS[6][:, :], PS[7][:, :]
        memset(onesf, 1.0)

        ybhs = [ybh, av(118, 2, F32)]
        sqs = [av(156.5 + i, 1, BF16) for i in range(2)]
        fin_i = [0]

        def finish(dst, w, after=None):
            k = fin_i[0] % 2
            yb_ = ybhs[k]
            sq = sqs[k]
            act(sq[:, 0:w], yb_[:, 0:w], AF.Square)

            def tail():
                psN = psb()
                mm(psN[:, 0:w], ones128, sq[:, 0:w])
                rs = stage()
                act(rs[:, 0:w], psN[:, 0:w], AF.Ln, bias=epsc, scale=1.0 / 128)
                act(rs[:, 0:w], rs[:, 0:w], AF.Exp, scale=-0.5)
                stt(dst[:, 0:w], yb_[:, 0:w], dnc, rs[:, 0:w], ALU.mult, ALU.mult)
                if after is not None:
                    after()
            defer(tail)
            fin_i[0] += 1

        def cur_ybh():
            return ybhs[fin_i[0] % 2]

        ei = [0]

        def nextE():
            e_ = Er[ei[0] % 6]
            ei[0] += 1
            return e_

        qz = [av(88 + 8 * i, 8, BF16, "p (c s t) -> p c s t", c=4, s=2) for i in range(2)]
        for i in range(2):
            memset(qz[i], 0.0, eng="pool")
        built = set()

        def build_qz(tb):
            if tb in built:
                return
            built.add(tb)
            for cq in range(4):
                for hs in range(2):
                    cp(qz[tb % 2][hs * 64:(hs + 1) * 64, cq, hs, :], qT[hs * 64:(hs + 1) * 64, cq, tb * TB:(tb + 1) * TB],
                       eng=("pool" if hs == 0 else "act"))

        items = []
        for tb in range(NTB):
            for h in range(4):
                hp = (h % 2) * 64
                for s in (2 * tb, 2 * tb + 1):
                    for comp in range(2):
                        def f1(s=s, comp=comp, h=h, hp=hp, tb=tb):
                            build_qz(tb)
                            cq = comp * 2 + h // 2
                            psS = psb()
                            for kc in range(2):
                                mm(psS[:, kc * 256:(kc + 1) * 256], kT[:, cq, (2 * s + kc) * 128:(2 * s + kc + 1) * 128],
                                   qz[tb % 2][:, cq, h % 2, (s % 2) * 256:(s % 2) * 256 + 256])
                            E = nextE()
                            act(E, psS, AF.Exp, scale=0.125)
                            return E

                        def f2(E, s=s, comp=comp, h=h):
                            psO = psb()
                            for kc in range(2):
                                mm(psO[:, 0:256], vbuf[:, (2 * s + kc) * 512 + h * 128:(2 * s + kc) * 512 + (h + 1) * 128],
                                   E[:, kc * 256:(kc + 1) * 256], kc == 0, kc == 1)
                            for kc in range(2):
                                mm(psO[:, 256:512], ones128, E[:, kc * 256:(kc + 1) * 256], kc == 0, kc == 1)
                            rd = stage()
                            recip(rd[:, 0:256], psO[:, 256:512])
                            if comp == 0:
                                tt(o1n[:, 0:256], psO[:, 0:256], rd[:, 0:256], ALU.mult)
                            else:
                                t_ = stage()
                                tt(t_[:, 0:256], psO[:, 0:256], rd[:, 0:256], ALU.mult)
                                stt(cur_ybh()[:, 0:256], t_[:, 0:256], neglam, o1n[:, 0:256], ALU.mult, ALU.add)
                                finish(ybp[:, (s % 2) * 256:(s % 2) * 256 + 256], 256)
                        items.append((f1, f2))
                for comp in range(2):
                    for kk in range(20):
                        def f1(comp=comp, kk=kk, h=h, hp=hp, tb=tb):
                            build_qz(tb)
                            cq = comp * 2 + h // 2
                            q_ = qz[tb % 2][:, cq, h % 2, :]
                            if kk < 16:
                                k_ = kT[:, cq, kk * 128:(kk + 1) * 128]
                            else:
                                k_ = ckT[:, cq, (kk - 16) * 128:(kk - 15) * 128]
                            psS = psb()
                            mm(psS, k_, q_)
                            E = nextE()
                            act(E, psS, AF.Exp, scale=0.125)
                            return E

                        def f2(E, comp=comp, kk=kk, h=h, tb=tb):
                            if kk < 16:
                                v_ = vbuf[:, kk * 512 + h * 128:kk * 512 + (h + 1) * 128]
                            else:
                                v_ = cvbuf[:, (kk - 16) * 512 + h * 128:(kk - 16) * 512 + (h + 1) * 128]
                            mm(psOa, v_, E, kk == 0, kk == 19)
                            mm(psDa, ones128, E, kk == 0, kk == 19)
                            if kk == 19:
                                rd = stage()
                                recip(rd, psDa)
                                if comp == 0:
                                    tt(o1n, psOa, rd, ALU.mult)
                                else:
                                    t_ = stage()
                                    tt(t_, psOa, rd, ALU.mult)
                                    stt(cur_ybh(), t_, neglam, o1n, ALU.mult, ALU.add)

                                    def blend(h=h, tb=tb):
                                        t2_ = stage()
                                        ts(t2_, ybs, flags[:, 1:2], None, ALU.mult)
                                        stt(yb[:, h, tb * TB:(tb + 1) * TB], ybp, flags[:, 0:1], t2_, ALU.mult, ALU.add)
                                    finish(ybs, 512, after=blend)
                        items.append((f1, f2))
        run_pipeline(items, 3)

    def mixerC(l):
        ktok = av(120, 16, BF16, "p (t n) -> p t n", t=16)
        qT = av(120, 16, BF16, "p (c t) -> p c t", c=4)
        kT = av(136, 16, BF16, "p (c t) -> p c t", c=4)
        Dc = av(152, 2, BF16, "p (h n) -> p h n", h=8)
        XIf = av(154, 1, BF16, "p (c n) -> p c n", c=4)
        XIb = av(155, 1, BF16, "p (c n) -> p c n", c=4)
        kz = [av(156 + i, 1, BF16) for i in range(2)]
        innerD = av(158, 2, BF16, "p (h n) -> p h n", h=8)
        SA = [av(104 + 8 * d, 8, BF16, "p (c n) -> p c n", c=16) for d in range(2)]
        yc = YM[2]
        Srun = [cst2[:, 0, :], cst2[:, 1, :]]
        Gt = [cst2[:, 2, :], cst2[:, 3, :]]
        relf = cf[:, CF_RELF:CF_RELF + 128]
        relb = cf[:, CF_RELB:CF_RELB + 128]
        pos1 = cf[:, CF_POS1:CF_POS1 + 128]
        posb = cf[:, CF_POSB:CF_POSB + 128]
        um = cb[:, CB_UM:CB_UM + 128]
        lmk = cb[:, CB_LM:CB_LM + 128]
        lg = sm[:, 16:32].rearrange("p (d h) -> p d h", d=2)
        lgp = sm[:, 32:40].rearrange("p (d c) -> p d c", d=2)
        act(lg, rtheta[:, l, :, :], AF.Exp)
        ts(lg, lg, -1.0, 1.0, ALU.mult, ALU.add)
        act(lg, lg, AF.Ln)
        for d in range(2):
            cp(lgp[0:64, d, :], lg[0:64, d, 0:8:2])
            cp(lgp[64:128, d, :], lg[64:128, d, 1:8:2])
        for c in range(4):
            act(XIf[:, c, :], pos1, AF.Exp, scale=lgp[:, 0, c:c + 1])
            act(XIb[:, c, :], posb, AF.Exp, scale=lgp[:, 1, c:c + 1])
        for h in range(8):
            a1 = stage()
            act(a1[:, 0:128], relf, AF.Exp, scale=lg[:, 0, h:h + 1])
            tt(a1[:, 0:128], a1[:, 0:128], um, ALU.mult)
            act(a1[:, 128:256], relb, AF.Exp, scale=lg[:, 1, h:h + 1])
            tt(a1[:, 128:256], a1[:, 128:256], lmk, ALU.mult)
            tt(Dc[:, h, :], a1[:, 0:128], a1[:, 128:256], ALU.add)
        zf = sm[:, 40:48]
        zb = sm[:, 48:56]
        tt(zf, cf[:, CF_MPOSF:CF_MPOSF + 8], lg[:, 0, :], ALU.mult)
        act(zf, zf, AF.Exp)
        ts(zf, zf, 0.125, None, ALU.mult)
        tt(zb, cf[:, CF_MPOSB:CF_MPOSB + 8], lg[:, 1, :], ALU.mult)
        act(zb, zb, AF.Exp)
        ts(zb, zb, 0.125, None, ALU.mult)
        g4 = sm[:, 56:64].rearrange("p (d c) -> p d c", d=2)
        act(g4, lgp, AF.Exp, scale=128.0)
        for d in range(2):
            src = bass.AP(sm, 56 + 4 * d, [[512, 128], [1, 4], [0, 64]])
            cp(Gt[d].rearrange("p (c e) -> p c e", c=4), src)
        wk = load_w(w_in[l, 7], 8, 512)
        for ti in range(NTT):
            ps = proj_tm(wk, ti)
            act(ktok[:, ti, :], ps, AF.Copy)
        wv = load_w(w_in[l, 8], 8, 512)
        for ti in range(NTT):
            ps = proj_tm(wv, ti)
            act(vbuf[:, ti * 512:(ti + 1) * 512], ps, AF.Copy)
        for d in range(2):
            z_ = zf if d == 0 else zb
            zbc = bass.AP(sm, 40 + 8 * d, [[512, 128], [1, 8], [0, 64]])
            order = list(range(16)) if d == 0 else list(range(15, -1, -1))
            s0 = s_rf0 if d == 0 else s_rb0
            od = o_srf if d == 0 else o_srb
            dma("sp", Srun[d], s0[l])
            cp(SA[d][:, order[0], :], Srun[d], eng="pool")
            for n_, c in enumerate(order):
                kzt = kz[n_ % 2]
                tt(kzt.rearrange("p (h e) -> p h e", h=8), ktok[:, c, :].rearrange("p (h e) -> p h e", h=8), zbc, ALU.mult)
                psP = psb()
                for h in range(8):
                    hp = (h % 2) * 64
                    mm(psP[hp:hp + 64, (h // 2) * 64:(h // 2) * 64 + 64], kzt[:, h * 64:(h + 1) * 64],
                       vbuf[:, c * 512 + h * 64:c * 512 + (h + 1) * 64])
                t_ = stage()
                tt(t_[:, 0:256], Srun[d], Gt[d], ALU.mult)
                se = stage()
                tt(se[:, 0:256], t_[:, 0:256], psP[:, 0:256], ALU.add)
                if (d == 0 and c % 2 == 1) or (d == 1 and c % 2 == 0):
                    dma("sp", od[l][c // 2], se[:, 0:256])
                if n_ < 15:
                    cn = order[n_ + 1]
                    ts(Srun[d], se[:, 0:256], rfl[:, d, cn:cn + 1], None, ALU.mult)
                    cp(SA[d][:, cn, :], Srun[d], eng="pool")
        wq = load_w(w_in[l, 6], 8, 512)
        for c in range(4):
            for tb in range(NTB):
                ps = proj_fm(wq, c, tb)
                act(qT[:, c, tb * TB:(tb + 1) * TB], ps, AF.Copy)
        wk2 = load_w(w_in[l, 7], 8, 512)
        for c in range(4):
            for tb in range(NTB):
                ps = proj_fm(wk2, c, tb)
                act(kT[:, c, tb * TB:(tb + 1) * TB], ps, AF.Copy)
        for c in range(16):
            cs = slice(c * 128, (c + 1) * 128)
            psI = [psb(), psb()]
            for h in range(8):
                hp = (h % 2) * 64
                mm(psI[h % 2][:, (h // 2) * 128:(h // 2 + 1) * 128], kT[hp:hp + 64, h // 2, cs], qT[hp:hp + 64, h // 2, cs])
            for g in range(2):
                tt(innerD[:, g:8:2, :], psI[g].rearrange("p (h n) -> p h n", h=4), Dc[:, g:8:2, :], ALU.mult)
            qx = stage_bf().rearrange("p (c n) -> p c n", c=4)
            tt(qx, qT[:, :, cs], XIf, ALU.mult, eng="pool")
            qy = stage_bf().rearrange("p (c n) -> p c n", c=4)
            tt(qy, qT[:, :, cs], XIb, ALU.mult, eng="pool")
            sbd = stage().bitcast(BF16).rearrange("p (d c n) -> p d c n", d=2, c=4)
            memset(sbd, 0.0, eng="pool")
            for d in range(2):
                for hs in range(2):
                    cp(sbd[hs * 64:(hs + 1) * 64, d, :, hs * 64:(hs + 1) * 64],
                       SA[d][hs * 64:(hs + 1) * 64, c, :].rearrange("p (c e) -> p c e", c=4), eng="pool")
            psO = psb()
            for ch in range(4):
                for hs in range(2):
                    h = 2 * ch + hs
                    mm(psO[hs * 64:(hs + 1) * 64, ch * 128:(ch + 1) * 128], vbuf[:, c * 512 + h * 64:c * 512 + (h + 1) * 64],
                       innerD[:, h, :], True, False)
                mm(psO[:, ch * 128:(ch + 1) * 128], sbd[:, 0, ch, :], qx[:, ch, :], False, False)
                mm(psO[:, ch * 128:(ch + 1) * 128], sbd[:, 1, ch, :], qy[:, ch, :], False, True)
            sq = stage_bf()
            act(sq, psO, AF.Square)
            psN = psb()
            mm(psN, bd_bf, sq)
            rs = stage()
            act(rs, psN, AF.Ln, bias=epsc, scale=1.0 / 64)
            act(rs, rs, AF.Exp, scale=-0.5)
            for ch in range(4):
                stt(yc[:, ch, cs], psO[:, ch * 128:(ch + 1) * 128], rnorm[:, l, ch:ch + 1], rs[:, ch * 128:(ch + 1) * 128], ALU.mult, ALU.mult)
        wg = load_w(w_in[l, 9], 8, 512)
        for ch in range(4):
            for tb in range(NTB):
                ps = proj_fm(wg, ch, tb)
                sg = stage()
                act(sg, ps, AF.Silu)
                tt(yc[:, ch, tb * TB:(tb + 1) * TB], yc[:, ch, tb * TB:(tb + 1) * TB], sg, ALU.mult)

    def mixerD(l):
        lrx = av(120, 8.25, F32)
        abuf = lrx
        xd = av(128.25, 8, F32)
        xdbf = av(136.25, 4, BF16)
        ubuf = av(140.25, 8, F32)
        hf = av(148.25, 8, F32)
        yd = YM[3]
        lw = cvbuf[:, 0:2048].rearrange("p (k c n) -> p k c n", k=4, c=4)
        dma("pool", lw, lruw_d[l])
        wx = load_w(w_in[l, 10], 8, 512)
        wg = load_w(w_in[l, 11], 8, 512)
        cl = sm[:, 64:72].rearrange("p (d c) -> p d c", d=2)
        cl2 = sm[:, 72:80].rearrange("p (d c) -> p d c", d=2)
        for d in range(2):
            act(cl[:, d, :], lrub[:, l, 2 + 3 * d, :], AF.Exp, scale=-1.0)
        act(cl, cl, AF.Ln, bias=onec)
        ts(cl2, cl, -16.0, None, ALU.mult)
        ts(cl, cl, -8.0, None, ALU.mult)
        nfw = sm[:, 80:96].rearrange("p (k c) -> p k c", k=4)
        ts(nfw, convw[:, l, :, :], flags[:, 0:1], -1.0, ALU.mult, ALU.mult)
        memset(lrx[:, 2049:2112], 0.0)
        for ch in range(4):
            memset(lrx[:, 0:1], 0.0)
            for tb in range(NTB):
                ps = proj_fm(wx, ch, tb)
                act(lrx[:, 1 + tb * TB:1 + (tb + 1) * TB], ps, AF.Copy)
            act(xd, lrx[:, 1:2049], AF.Identity, bias=convb[:, l, ch:ch + 1], scale=convw[:, l, 1, ch:ch + 1])
            stt(xd, lrx[:, 0:2048], convw[:, l, 0, ch:ch + 1], xd, ALU.mult, ALU.add)
            stt(xd, lrx[:, 2:2050], convw[:, l, 2, ch:ch + 1], xd, ALU.mult, ALU.add)
            stt(xd, lrx[:, 3:2051], convw[:, l, 3, ch:ch + 1], xd, ALU.mult, ALU.add)
            stt(xd[:, 256:2048:256], lrx[:, 256:2048:256], nfw[:, 0, ch:ch + 1], xd[:, 256:2048:256], ALU.mult, ALU.add)
            stt(xd[:, 255:1792:256], lrx[:, 257:1794:256], nfw[:, 2, ch:ch + 1], xd[:, 255:1792:256], ALU.mult, ALU.add)
            stt(xd[:, 255:1792:256], lrx[:, 258:1795:256], nfw[:, 3, ch:ch + 1], xd[:, 255:1792:256], ALU.mult, ALU.add)
            stt(xd[:, 254:1791:256], lrx[:, 257:1794:256], nfw[:, 3, ch:ch + 1], xd[:, 254:1791:256], ALU.mult, ALU.add)
            cp(xdbf, xd, eng="pool")
            for d in range(2):
                for tb in range(NTB):
                    ts_ = slice(tb * TB, (tb + 1) * TB)
                    psr = psb()
                    mm(psr, lw[:, 2 * d, ch, :], xdbf[:, ts_])
                    act(abuf[:, ts_], psr, AF.Sigmoid, bias=lrub[:, l, 3 * d, ch:ch + 1])
                    psi = psb()
                    mm(psi, lw[:, 2 * d + 1, ch, :], xdbf[:, ts_])
                    act(ubuf[:, ts_], psi, AF.Sigmoid, bias=lrub[:, l, 3 * d + 1, ch:ch + 1])
                act(abuf[:, 0:2048], abuf[:, 0:2048], AF.Exp, scale=cl[:, d, ch:ch + 1])
                tt(ubuf, ubuf, xd, ALU.mult, eng="pool")
                for tb in range(NTB):
                    ts_ = slice(tb * TB, (tb + 1) * TB)
                    e2 = stage()
                    act(e2, abuf[:, ts_], AF.Square)
                    ts(e2, e2, -1.0, 1.0, ALU.mult, ALU.add)
                    act(e2, e2, AF.Ln)
                    act(e2, e2, AF.Exp, scale=0.5)
                    tt(ubuf[:, ts_], ubuf[:, ts_], e2, ALU.mult)
                if d == 0:
                    ts(abuf[:, 256:2048:256], abuf[:, 256:2048:256], flags[:, 1:2], None, ALU.mult)
                    S.add("dve", lambda e, ch=ch: e.tensor_tensor_scan(out=hf, data0=abuf[:, 0:2048], data1=ubuf, initial=h_l0[:, l, 0, ch:ch + 1],
                                                                        op0=ALU.mult, op1=ALU.add),
                          reads=[abuf[:, 0:2048], ubuf, h_l0[:, l, 0, ch:ch + 1]], writes=[hf])
                    cp(hl_sb[:, 0, ch, :], hf[:, 255:2048:256])
                else:
                    ts(abuf[:, 255:1792:256], abuf[:, 255:1792:256], flags[:, 1:2], None, ALU.mult)
                    S.add("dve", lambda e, ch=ch: e.tensor_tensor_scan(out=ubuf[:, ::-1], data0=abuf[:, 2047::-1], data1=ubuf[:, ::-1],
                                                                        initial=h_l0[:, l, 1, ch:ch + 1], op0=ALU.mult, op1=ALU.add),
                          reads=[abuf[:, 0:2048], ubuf, h_l0[:, l, 1, ch:ch + 1]], writes=[ubuf])
                    cp(hl_sb[:, 1, ch, :], ubuf[:, 0:2048:256])
            tt(hf, hf, ubuf, ALU.add, eng="pool")
            for tb in range(NTB):
                ts_ = slice(tb * TB, (tb + 1) * TB)
                psg = proj_fm(wg, ch, tb)
                g = stage()
                act(g, psg, AF.Copy)
                t_ = stage()
                act(t_, psg, AF.Square)
                ts(t_, t_, 0.044715, 1.0, ALU.mult, ALU.add)
                tt(t_, t_, g, ALU.mult)
                act(t_, t_, AF.Sigmoid, scale=2.0 * math.sqrt(2.0 / math.pi))
                tt(t_, t_, g, ALU.mult)
                tt(yd[:, ch, ts_], t_, hf[:, ts_], ALU.mult)
        dma("sp", o_hl[l], hl_sb[:])

    def merge_out(l):
        acc = av(120, 32, F32, "p (c t) -> p c t", c=8)
        Y01 = av(56, 32, BF16, "p (c t) -> p c t", c=8)
        Y23 = av(88, 32, F32, "p (c t) -> p c t", c=8)
        src = xin if l == 0 else yres
        for th in range(2):
            merged = Y01[:, :, th * 1024:(th + 1) * 1024]
            xs = Y23[:, :, th * 512:(th + 1) * 512]
            for half in range(2):
                for m in range(4):
                    gt = load_w(w_in[l, 12 + 2 * m + half], 8, 512)
                    bt = load_w(w_br[l, m, half], 4, 512)
                    for cc in range(4):
                        c8 = half * 4 + cc
                        for t2 in range(2):
                            tb = 2 * th + t2
                            a_ = acc[:, c8, t2 * TB:(t2 + 1) * TB]
                            psG = proj_fm(gt, cc, tb)
                            G = stage()
                            act(G, psG, AF.Sigmoid)
                            psP = proj_fm(bt, cc, tb, nk=4, src=YM[m])
                            if m == 0:
                                tt(a_, psP, G, ALU.mult)
                            else:
                                t_ = stage()
                                tt(t_, psP, G, ALU.mult)
                                tt(a_, a_, t_, ALU.add, eng="pool")
            for c8 in range(8):
                act(merged[:, c8, :], acc[:, c8, :], AF.Copy)
            for half in range(2):
                wt = load_w(w_out[l, half], 8, 512)
                for cc in range(4):
                    for t2 in range(2):
                        ps = proj_fm(wt, cc, 0, src=merged, n0=t2 * TB)
                        cp(acc[:, half * 4 + cc, t2 * TB:(t2 + 1) * TB], ps)
            for t2 in range(2):
                resid_norm(l, 2 * th + t2, src, acc[:, :, t2 * TB:(t2 + 1) * TB], 2, xs, (l, 3, 4))

    def mlp(l, last):
        hid = av(120, 32, BF16, "p (c t) -> p c t", c=8)
        xs = av(120, 16, F32, "p (c t) -> p c t", c=8)
        for jg in range(4):
            if jg == 2 and not last:
                modulation(l + 1)
            for jh in range(2):
                wt = load_w(w1[l, 2 * jg + jh], 8, 512)
                for jj in range(4):
                    for tb in range(NTB):
                        ps = proj_fm(wt, jj, tb)
                        r = stage()
                        act(r, ps, AF.Relu)
                        tt(hid[:, jh * 4 + jj, tb * TB:(tb + 1) * TB], r, r, ALU.mult, eng="pool")
            for chh in range(2):
                wt2 = load_w(w2[l, jg, chh], 8, 512)
                for cc in range(4):
                    for tb in range(NTB):
                        ps = proj_fm(wt2, cc, tb, src=hid)
                        dst = ymlp[:, chh * 4 + cc, tb * TB:(tb + 1) * TB]
                        if jg == 0:
                            act(dst, ps, AF.Copy)
                        else:
                            tt(dst, dst, ps, ALU.add)
        xsb = [xs, av(136, 16, F32, "p (c t) -> p c t", c=8)]
        dma("sp", xsb[0], yres[0])
        dma("sp", xsb[1], yres[1])
        for tb in range(NTB):
            resid_norm(l, tb, yres, ymlp[:, :, tb * TB:(tb + 1) * TB], 5, xsb[tb % 2], None if last else (l + 1, 0, 1), preloaded=True)
            if tb + 2 < NTB:
                dma("sp", xsb[tb % 2], yres[tb + 2])

    xs0 = av(120, 16, F32, "p (c t) -> p c t", c=8)
    for tb in range(NTB):
        resid_norm(0, tb, xin, None, 0, xs0, (0, 0, 1))
    for l in range(depth):
        if dbg and l == 0:
            dma("pool", dbg_h, hT)
        if stop == "mod":

            if os.environ.get("DBGMOD", "1") == "1":
                dma("sp", dbg_mod, modraw[:, 0:2, :])
            break
        if dbg and l == 0:
            pass
        S.marks = getattr(S, "marks", [])
        for nm_, fn_, m_ in (("A", mixerA, 0), ("B", mixerB, 1), ("C", mixerC, 2), ("D", mixerD, 3)):
            S.marks.append(("L%d %s" % (l, nm_), len(S.streams["pe"])))
            if nm_ in parts:
                fn_(l)
            else:
                memset(YM[m_], 0.0)
        S.marks.append(("L%d merge" % l, len(S.streams["pe"])))
        if dbg and l == 0:
            for m_ in range(4):
                dma("pool", dbg_y[m_], YM[m_])
        merge_out(l)
        S.marks.append(("L%d mlp" % l, len(S.streams["pe"])))
        mlp(l, l == depth - 1)
        S.marks.append(("L%d end" % l, len(S.streams["pe"])))
    S.emit()
    return S


def _consts():
    m = np.arange(128)[:, None].astype(np.float32)
    n = np.arange(128)[None, :].astype(np.float32)
    cf = np.zeros((128, CF_N), np.float32)
    cf[:, CF_RELF:CF_RELF + 128] = np.maximum(n - m, 0)
    cf[:, CF_RELB:CF_RELB + 128] = np.maximum(m - n, 0)
    cf[:, CF_POS1:CF_POS1 + 128] = n + 1
    cf[:, CF_POSB:CF_POSB + 128] = 128 - n
    sp = np.zeros((128, 128), np.float32)
    for d in range(128):
        dd = d % 64
        base = d - dd
        if dd % 32 < 16:
            sp[base + dd + 16, d] = -1.0
        else:
            sp[base + dd - 16, d] = 1.0
    cf[:, CF_SPERM:CF_SPERM + 128] = sp
    cf[:, CF_MPOSF:CF_MPOSF + 8] = 127 - m
    cf[:, CF_MPOSB:CF_MPOSB + 8] = m
    cb = np.zeros((128, CB_N), np.float32)
    cb[:, CB_UM:CB_UM + 128] = 0.125 * (n >= m)
    cb[:, CB_LM:CB_LM + 128] = 0.125 * (m > n)
    cb[:, CB_BD:CB_BD + 128] = ((np.arange(128)[:, None] // 64) == (np.arange(128)[None, :] // 64))
    kc = np.arange(128)[:, None] % 64
    qc = 63 - np.arange(64)[None, :]
    cs = np.clip(qc - 8, 0, 48)
    cb[:, CB_COL:CB_COL + 64] = (kc >= cs) & (kc < cs + 16)
    cb[:, CB_ONES:CB_ONES + 128] = 1.0
    return cf, cb


def _rope_tables():
    t = np.arange(T)
    inv = (np.float32(10000.0) ** (-np.arange(16, dtype=np.float32) / np.float32(16))).astype(np.float32)
    ang_r = (t // 64).astype(np.float32)[:, None] * inv
    ang_c = (t % 64).astype(np.float32)[:, None] * inv
    ang = np.zeros((128, T), np.float32)
    for p in range(128):
        d = p % 64
        ang[p] = (ang_r if d < 32 else ang_c)[:, d % 16]
    rope = np.zeros((128, NTB, 2, TB), np.float32)
    rope[:, :, 0, :] = np.cos(ang).astype(np.float32).reshape(128, NTB, TB)
    rope[:, :, 1, :] = np.sin(ang).astype(np.float32).reshape(128, NTB, TB)
    return rope


def _tile(w):
    L, K, N = w.shape
    return np.ascontiguousarray(w.reshape(L, K // 128, 128, N // 512, 512).transpose(0, 3, 2, 1, 4))


def _fm(v):
    k = v.shape[-1] // 128
    r = v.reshape(v.shape[:-1] + (k, 128))
    return np.ascontiguousarray(np.moveaxis(r, -1, 0))


def _prep(I):
    f32 = np.float32
    A = lambda x: np.ascontiguousarray(x, dtype=f32)
    cf, cb = _consts()
    shared = {
        "ada_w": _tile(A(I["ada_w"])), "w_in": _tile(A(I["w_in"])),
        "w_br": np.ascontiguousarray(_tile(A(I["w_branch"]).reshape(DEPTH * 4, 512, D)).reshape(DEPTH, 4, 2, 128, 4, 512)),
        "w_out": _tile(A(I["w_out"])), "w1": _tile(A(I["mlp_w1"])),
        "w2": np.ascontiguousarray(_tile(A(I["mlp_w2"]).reshape(DEPTH * 4, 1024, D)).reshape(DEPTH, 4, 2, 128, 8, 512)),
        "adab": _fm(A(I["ada_b"]).reshape(DEPTH, 6 * D)).reshape(128, DEPTH, 48),
        "gains": np.ascontiguousarray(np.stack([_fm(A(I[k])) for k in ("norm_mix_pre", "norm_mix_post", "norm_ffn_pre", "norm_ffn_post")], axis=2)),
        "dlam": np.ascontiguousarray(np.broadcast_to(np.stack([A(I[k]) for k in ("diff_lq1", "diff_lk1", "diff_lq2", "diff_lk2")], axis=1)[None], (128, DEPTH, 4, 64))),
        "dnorm": np.ascontiguousarray(A(I["diff_norm"]).T),
        "rtheta": np.ascontiguousarray(np.broadcast_to(np.stack([A(I["ret_theta_fwd"]), A(I["ret_theta_bwd"])], axis=1)[None], (128, DEPTH, 2, 8))),
        "rnorm": _fm(A(I["ret_norm"])),
        "convw": _fm(A(I["lru_conv_w"])),
        "convb": _fm(A(I["lru_conv_b"])),
        "lrub": np.ascontiguousarray(np.stack([_fm(A(I[k])) for k in ("lru_ba_fwd", "lru_bx_fwd", "lru_lam_fwd", "lru_ba_bwd", "lru_bx_bwd", "lru_lam_bwd")], axis=2)),
        "cf": cf, "cb": cb,
    }
    rp = np.zeros((DEPTH, 8, 16, 128), f32)
    rp[:, :, 0:15, 48:79] = A(I["na_rpb"])
    shared["rpbp"] = rp
    lw = np.zeros((DEPTH, 128, 4, 4, 128), f32)
    for ki, k in enumerate(("lru_wa_fwd", "lru_wx_fwd", "lru_wa_bwd", "lru_wx_bwd")):
        W = A(I[k])
        for ch in range(4):
            for nb in range(2):
                lw[:, nb * 64:(nb + 1) * 64, ki, ch, nb * 64:(nb + 1) * 64] = W[:, 2 * ch + nb]
    shared["lruw"] = lw
    rope_s = _rope_tables()
    rope_p = np.zeros_like(rope_s)
    rope_p[:, :, 0, :] = 1.0
    maps = []
    for i in range(8):
        d = dict(shared)
        if i < 4:
            b = i
            x = A(I["x_sample"][b])
            cond = A(I["c"][b])
            d["flags"] = np.tile(np.array([[0.0, 1.0]], f32), (128, 1))
            d["rope"] = rope_s
            d["rfl"] = np.ones((128, 2, 16), f32)
            ck = A(I["cache_na_k"][b])
            d["c_nakT"] = np.ascontiguousarray(ck.reshape(DEPTH, 4, 2, 512, 64).transpose(0, 2, 4, 1, 3)).reshape(DEPTH, 128, 4, 512)
            cv = A(I["cache_na_v"][b])
            d["c_nav"] = np.ascontiguousarray(cv.reshape(DEPTH, 8, 4, 128, 64).transpose(0, 3, 2, 1, 4)).reshape(DEPTH, 128, 4, 512)
            dk = A(I["cache_diff_k"][b])
            d["c_dfkT"] = np.ascontiguousarray(dk.reshape(DEPTH, 2, 2, 2, 512, 64).transpose(0, 3, 5, 1, 2, 4)).reshape(DEPTH, 128, 4, 512)
            dv = A(I["cache_diff_v"][b])
            d["c_dfv"] = np.ascontiguousarray(dv.reshape(DEPTH, 4, 4, 128, 128).transpose(0, 3, 2, 1, 4)).reshape(DEPTH, 128, 4, 512)
            for nm, key in (("s_rf0", "state_ret_fwd"), ("s_rb0", "state_ret_bwd")):
                s_ = A(I[key][b])
                d[nm] = np.ascontiguousarray(s_.reshape(DEPTH, 4, 2, 64, 64).transpose(0, 2, 3, 1, 4)).reshape(DEPTH, 128, 256)
            d["h_l0"] = np.ascontiguousarray(np.stack([_fm(A(I["state_lru_fwd"][b])), _fm(A(I["state_lru_bwd"][b]))], axis=2))
        else:
            j = i - 4
            x = A(I["x_prompt"][8 * j:8 * j + 8]).reshape(T, D)
            cond = A(I["c_ctx"])
            d["flags"] = np.tile(np.array([[1.0, 0.0]], f32), (128, 1))
            d["rope"] = rope_p
            r = np.ones((128, 2, 16), f32)
            r[:, 0, 0::2] = 0.0
            r[:, 1, 1::2] = 0.0
            d["rfl"] = r
            for nm in ("c_nakT", "c_nav", "c_dfkT", "c_dfv"):
                d[nm] = np.zeros((DEPTH, 128, 4, 512), f32)
            d["s_rf0"] = np.zeros((DEPTH, 128, 256), f32)
            d["s_rb0"] = np.zeros((DEPTH, 128, 256), f32)
            d["h_l0"] = np.zeros((128, DEPTH, 2, 4), f32)
        d["xin"] = np.ascontiguousarray(x.reshape(NTB, TB, 8, 128).transpose(0, 3, 2, 1))
        d["cond"] = np.ascontiguousarray(cond.reshape(8, 128).T)
        maps.append(d)
    return maps


def _assemble(res):
    f32 = np.float32

    def yfull(r):
        return np.ascontiguousarray(np.asarray(r["yres"], f32).transpose(0, 3, 2, 1)).reshape(T, D)

    y_s = np.stack([yfull(res[i]) for i in range(4)], axis=0)
    y_p = np.concatenate([yfull(res[i]).reshape(8, 256, D) for i in range(4, 8)], axis=0)
    nak, nav, dfk, dfv, srf, srb, hlf, hlb = [], [], [], [], [], [], [], []
    for i in range(4, 8):
        r = res[i]
        a = np.asarray(r["o_nakT"], f32).reshape(DEPTH, 2, 64, 4, 8, 256)
        nak.append(a.transpose(4, 0, 3, 1, 5, 2).reshape(8, DEPTH, 8, 256, 64))
        a = np.asarray(r["o_nav"], f32).reshape(DEPTH, 8, 256, 8, 64)
        nav.append(a.transpose(1, 0, 3, 2, 4))
        a = np.asarray(r["o_dfkT"], f32).reshape(DEPTH, 2, 64, 2, 2, 8, 256)
        dfk.append(a.transpose(5, 0, 3, 4, 1, 6, 2).reshape(8, DEPTH, 2, 4, 256, 64))
        a = np.asarray(r["o_dfv"], f32).reshape(DEPTH, 8, 256, 4, 128)
        dfv.append(a.transpose(1, 0, 3, 2, 4))
        for lst, key in ((srf, "o_srf"), (srb, "o_srb")):
            a = np.asarray(r[key], f32).reshape(DEPTH, 8, 2, 64, 4, 64)
            lst.append(a.transpose(1, 0, 4, 2, 3, 5).reshape(8, DEPTH, 8, 64, 64))
        a = np.asarray(r["o_hl"], f32)
        hlf.append(a[:, :, 0].transpose(3, 0, 2, 1).reshape(8, DEPTH, 512))
        hlb.append(a[:, :, 1].transpose(3, 0, 2, 1).reshape(8, DEPTH, 512))
    cat = lambda x: np.ascontiguousarray(np.concatenate(x, axis=0), dtype=f32)
    return (np.ascontiguousarray(y_p, dtype=f32), np.ascontiguousarray(y_s, dtype=f32), cat(nak), cat(nav), cat(dfk), cat(dfv),
            cat(srf), cat(srb), cat(hlf), cat(hlb))


_PROG = {}


def kernel(**inputs):
    if "p" not in _PROG:
        _PROG["p"] = build()
    S = _PROG["p"]
    maps = _prep(inputs)
    res = run_bass_kernel_spmd(S.nc, maps, core_ids=list(range(8)))
    return _assemble(res.results)
```

```python
import math
import os
import numpy as np
import concourse.bass as bass
import concourse.mybir as mybir
from concourse.bass_utils import run_bass_kernel_spmd
from contextlib import ExitStack

F32 = mybir.dt.float32
BF16 = mybir.dt.bfloat16
AF = mybir.ActivationFunctionType
ALU = mybir.AluOpType
_DT_SIZE = {F32: 4, BF16: 2}
ENGS = ("pe", "act", "dve", "pool", "sp")
N_DMA_SEMS = 8

DEPTH = 4
D = 1024
T = 2048
NCH = 8
TB = 512
NTB = 4
NTT = 16
EPS = 1e-6
KIB = 256


def _box(ap, tracked_dram):
    if not hasattr(ap, "tensor"):
        ap = ap[:]
    t = ap.tensor
    pat = ap.ap
    off = ap.offset
    sz = _DT_SIZE[ap.dtype]
    if str(ap.space) == "DRAM":
        if t.name not in tracked_dram:
            return None
        lo = hi = off
        for st, cn in pat:
            if st < 0:
                lo += st * (cn - 1)
            else:
                hi += st * (cn - 1)
        return (t.name, 0, 1, lo * sz, (hi + 1) * sz)
    if str(ap.space) == "PSUM":
        return (t.name, 0, 128, 0, 2048)
    pstep, pcnt = pat[0]
    if pstep <= 0:
        pstep = 1 << 40
    p0 = off // pstep
    f = off % pstep
    lo = hi = f
    for st, cn in pat[1:]:
        if st < 0:
            lo += st * (cn - 1)
        else:
            hi += st * (cn - 1)
    return (t.name, p0, p0 + pcnt, lo * sz, (hi + 1) * sz)


def _overlap(a, b):
    return a[1] < b[2] and b[1] < a[2] and a[3] < b[4] and b[3] < a[4]


def _covers(a, b):
    return a[1] <= b[1] and a[2] >= b[2] and a[3] <= b[3] and a[4] >= b[4]


class Op:
    __slots__ = ("eng", "fn", "deps", "idx", "lidx", "is_dma", "sig", "sigval", "dsem", "dval")

    def __init__(self, eng, fn, is_dma):
        self.eng = eng
        self.fn = fn
        self.is_dma = is_dma
        self.deps = set()
        self.sig = False
        self.sigval = 0
        self.dsem = None
        self.dval = 0


class Sched:
    def __init__(self):
        self.nc = bass.Bass("TRN2", target_bir_lowering=False)
        self.ops = []
        self.streams = {e: [] for e in ENGS}
        self.recs = {}
        self.frozen = set()
        self.tracked_dram = set()
        self.stack = ExitStack()
        self.dma_count = {e: 0 for e in ENGS}
        self.dma_last = {}

    def dram_in(self, name, shape, dtype=F32):
        return self.nc.dram_tensor(name, list(shape), dtype, kind="ExternalInput").ap()

    def dram_out(self, name, shape, dtype=F32):
        return self.nc.dram_tensor(name, list(shape), dtype, kind="ExternalOutput").ap()

    def sb(self, name, shape, dtype=F32):
        return self.stack.enter_context(self.nc.sbuf_tensor(name, list(shape), dtype))

    def ps(self, name, shape, dtype=F32):
        return self.stack.enter_context(self.nc.psum_tensor(name, list(shape), dtype))

    def add(self, eng, fn, reads=(), writes=(), dma=False):
        op = Op(eng, fn, dma)
        op.idx = len(self.ops)
        op.lidx = len(self.streams[eng])
        self.ops.append(op)
        self.streams[eng].append(op)
        td = self.tracked_dram
        rboxes = [b for b in (_box(ap, td) for ap in reads) if b is not None]
        wboxes = [b for b in (_box(ap, td) for ap in writes) if b is not None]
        for b in rboxes:
            is_psum = b[0].startswith("ps") and b[4] - b[3] == 2048 and b[2] - b[1] == 128
            for r in self.recs.get(b[0], ()):
                if r[2]:
                    if _overlap(r[0], b):
                        op.deps.add(r[1])
                elif is_psum and r[3] != eng:
                    op.deps.add(r[1])
        for b in wboxes:
            for r in self.recs.get(b[0], ()):
                if _overlap(r[0], b):
                    op.deps.add(r[1])
        for b in wboxes:
            lst = self.recs.setdefault(b[0], [])
            keep = [r for r in lst if not (_overlap(r[0], b) and _covers(b, r[0]))]
            keep.append([b, op.idx, True, eng])
            self.recs[b[0]] = keep
        for b in rboxes:
            if b[0] in self.frozen:
                continue
            lst = self.recs.setdefault(b[0], [])
            if not dma:
                for r in lst:
                    if (not r[2]) and r[3] == eng and r[0] == b and not self.ops[r[1]].is_dma:
                        r[1] = op.idx
                        break
                else:
                    lst.append([b, op.idx, False, eng])
            else:
                lst.append([b, op.idx, False, eng])
        if dma:
            n = self.dma_count[eng]
            self.dma_count[eng] = n + 1
            slot = n % N_DMA_SEMS
            op.dsem = (eng, slot)
            op.dval = 16 * (n // N_DMA_SEMS + 1)
            prev = self.dma_last.get((eng, slot))
            if prev is not None:
                op.deps.add(prev.idx)
            self.dma_last[(eng, slot)] = op
        return op

    def emit(self):
        nc = self.nc
        ops = self.ops
        for op in ops:
            for d in list(op.deps):
                dop = ops[d]
                if dop.is_dma:
                    continue
                if dop.eng == op.eng and not op.is_dma:
                    if dop.eng == "pe":
                        op.deps.discard(d)
                        continue
                dop.sig = True
        for e in ENGS:
            c = 0
            for op in self.streams[e]:
                if (not op.is_dma) and op.sig:
                    c += 1
                    op.sigval = c
        st = self.stack
        esem = {e: st.enter_context(nc.semaphore("s_" + e)) for e in ENGS}
        dsem = {}
        for e in ENGS:
            if self.dma_count[e]:
                for s in range(N_DMA_SEMS):
                    dsem[(e, s)] = st.enter_context(nc.semaphore("d_%s%d" % (e, s)))
        block = st.enter_context(nc.Block())
        handles = {"pe": block.tensor, "act": block.scalar, "dve": block.vector, "pool": block.gpsimd, "sp": block.sync}
        self.n_waits = 0

        def make(e):
            stream = self.streams[e]

            def body(eng):
                waited = {}
                for op in stream:
                    need = {}
                    for d in op.deps:
                        dop = ops[d]
                        if dop.is_dma:
                            key = ("d",) + dop.dsem
                            val = dop.dval
                            sem = dsem[dop.dsem]
                        else:
                            key = ("e", dop.eng)
                            val = dop.sigval
                            sem = esem[dop.eng]
                        if val > need.get(key, (0, None))[0]:
                            need[key] = (val, sem)
                    for key, (val, sem) in need.items():
                        if waited.get(key, 0) >= val:
                            continue
                        waited[key] = val
                        eng.wait_ge(sem, val)
                        self.n_waits += 1
                    ins = op.fn(eng)
                    if op.is_dma:
                        ins.then_inc(dsem[op.dsem], 16)
                    elif op.sig:
                        ins.then_inc(esem[e], 1)
                for (ee, s), last in self.dma_last.items():
                    if ee == e:
                        eng.wait_ge(dsem[(ee, s)], last.dval)
            return body

        for e in ENGS:
            if self.streams[e]:
                handles[e](make(e))
        st.close()
        return nc


LAM_INIT = [0.8 - 0.6 * math.exp(-0.3 * l) for l in range(DEPTH)]
CF_RELF, CF_RELB, CF_POS1, CF_POSB, CF_SPERM, CF_MPOSF, CF_MPOSB, CF_N = 0, 128, 256, 384, 512, 640, 648, 656
CB_UM, CB_LM, CB_BD, CB_COL, CB_ONES, CB_N = 0, 128, 256, 384, 448, 576
VON = 8192
CVON = 2048
NSTG = 6


def build(depth=DEPTH, dbg=False, parts="ABCD", stop=None):
    S = Sched()
    nc = S.nc
    di, do = S.dram_in, S.dram_out
    xin = di("xin", [NTB, 128, 8, TB])
    yres = do("yres", [NTB, 128, 8, TB])
    S.tracked_dram.add("yres")
    cond = di("cond", [128, 8])
    flags_d = di("flags", [128, 2])
    rope_d = di("rope", [128, NTB, 2, TB])
    rfl_d = di("rfl", [128, 2, 16])
    ada_w = di("ada_w", [DEPTH, 12, 128, 8, 512])
    w_in = di("w_in", [DEPTH, 20, 128, 8, 512])
    w_br = di("w_br", [DEPTH, 4, 2, 128, 4, 512])
    w_out = di("w_out", [DEPTH, 2, 128, 8, 512])
    w1 = di("w1", [DEPTH, 8, 128, 8, 512])
    w2 = di("w2", [DEPTH, 4, 2, 128, 8, 512])
    adab_d = di("adab", [128, DEPTH, 48])
    gains_d = di("gains", [128, DEPTH, 4, 8])
    rpbp = di("rpbp", [DEPTH, 8, 16, 128])
    dlam_d = di("dlam", [128, DEPTH, 4, 64])
    dnorm_d = di("dnorm", [128, DEPTH])
    rtheta_d = di("rtheta", [128, DEPTH, 2, 8])
    rnorm_d = di("rnorm", [128, DEPTH, 4])
    convw_d = di("convw", [128, DEPTH, 4, 4])
    convb_d = di("convb", [128, DEPTH, 4])
    lruw_d = di("lruw", [DEPTH, 128, 4, 4, 128])
    lrub_d = di("lrub", [128, DEPTH, 6, 4])
    c_nakT = di("c_nakT", [DEPTH, 128, 4, 512])
    c_nav = di("c_nav", [DEPTH, 128, 4, 512])
    c_dfkT = di("c_dfkT", [DEPTH, 128, 4, 512])
    c_dfv = di("c_dfv", [DEPTH, 128, 4, 512])
    s_rf0 = di("s_rf0", [DEPTH, 128, 256])
    s_rb0 = di("s_rb0", [DEPTH, 128, 256])
    h_l0_d = di("h_l0", [128, DEPTH, 2, 4])
    cf_d = di("cf", [128, CF_N])
    cb_d = di("cb", [128, CB_N])

    o_nakT = do("o_nakT", [DEPTH, 128, 4, T])
    o_nav = do("o_nav", [DEPTH, NTT, 128, 512])
    o_dfkT = do("o_dfkT", [DEPTH, 128, 4, T])
    o_dfv = do("o_dfv", [DEPTH, NTT, 128, 512])
    o_srf = do("o_srf", [DEPTH, 8, 128, 256])
    o_srb = do("o_srb", [DEPTH, 8, 128, 256])
    o_hl = do("o_hl", [DEPTH, 128, 2, 4, 8])
    if dbg:
        dbg_y = do("dbg_y", [4, 128, 4, T])
        dbg_h = do("dbg_h", [128, 8, T])
        dbg_mod = do("dbg_mod", [128, 96])
        dbg_mod2 = do("dbg_mod2", [128, 96])

    ARENA_K = 160
    arena = S.sb("arena", [128, ARENA_K * KIB], F32)

    def av(off_k, size_k, dtype, pattern=None, **kw):
        a = arena[:, int(off_k * KIB):int((off_k + size_k) * KIB)]
        if dtype == BF16:
            a = a.bitcast(BF16)
        if pattern:
            a = a.rearrange(pattern, **kw)
        return a

    hT = av(0, 32, BF16, "p (c t) -> p c t", c=8)
    WR = [av(32 + 8 * i, 8, BF16, "p (k n) -> p k n", k=8) for i in range(3)]
    YM = [av(56 + 16 * m, 16, BF16, "p (c t) -> p c t", c=4) for m in range(4)]
    ymlp = av(56, 64, F32, "p (c t) -> p c t", c=8)
    vbuf_t = S.sb("vbuf", [128, VON + 64], BF16)
    cvbuf_t = S.sb("cvbuf", [128, CVON + 64], BF16)
    vbuf = vbuf_t[:, :]
    cvbuf = cvbuf_t[:, :]
    stg = S.sb("stg", [128, NSTG, 512], F32)
    cf = S.sb("cf_sb", [128, CF_N], F32)
    cb = S.sb("cb_sb", [128, CB_N], BF16)
    cst2 = S.sb("cst2", [128, 4, 256], F32)
    sm = S.sb("sm", [128, 512], F32)
    modraw = S.sb("modraw", [128, DEPTH, 48], F32)
    modc = S.sb("modc", [128, DEPTH + 1, 6, 8], F32)
    adab = S.sb("adab_sb", [128, DEPTH, 48], F32)
    gains = S.sb("gains_sb", [128, DEPTH, 4, 8], F32)
    flags = S.sb("flags_sb", [128, 2], F32)
    rfl = S.sb("rfl_sb", [128, 2, 16], F32)
    dnorm = S.sb("dnorm_sb", [128, DEPTH], F32)
    rtheta = S.sb("rtheta_sb", [128, DEPTH, 2, 8], F32)
    rnorm = S.sb("rnorm_sb", [128, DEPTH, 4], F32)
    convw = S.sb("convw_sb", [128, DEPTH, 4, 4], F32)
    convb = S.sb("convb_sb", [128, DEPTH, 4], F32)
    lrub = S.sb("lrub_sb", [128, DEPTH, 6, 4], F32)
    h_l0 = S.sb("h_l0_sb", [128, DEPTH, 2, 4], F32)
    hl_sb = S.sb("hl_sb", [128, 2, 4, 8], F32)
    scond = S.sb("scond", [128, 8], BF16)
    condsb = S.sb("condsb", [128, 8], F32)
    PS = [S.ps("ps%d" % i, [128, 512], F32) for i in range(8)]

    st = {"ps": 0, "stg": 0, "wr": 0, "stgL": 0, "nps": 8}

    def stageL():
        i_ = st["stgL"] % 2
        s_ = cst2[:, 2 * i_:2 * i_ + 2, :].rearrange("p a b -> p (a b)")
        st["stgL"] += 1
        return s_

    def psb():
        b = PS[st["ps"] % st["nps"]][:, :]
        st["ps"] += 1
        return b

    def stage():
        s = stg[:, st["stg"] % NSTG, :]
        st["stg"] += 1
        return s

    def stage_bf():
        return stage().bitcast(BF16)[:, 0:512]

    def mm(out, lhsT, rhs, start=True, stop=True):
        S.add("pe", lambda e: e.matmul(out, lhsT=lhsT, rhs=rhs, start=start, stop=stop), reads=[lhsT, rhs], writes=[out])

    def act(out, in_, func, bias=None, scale=None):
        kw = {}
        rd = [in_]
        if bias is not None:
            kw["bias"] = bias
            if not isinstance(bias, (int, float)):
                rd.append(bias)
        if scale is not None:
            kw["scale"] = scale
            if not isinstance(scale, (int, float)):
                rd.append(scale)
        S.add("act", lambda e: e.activation(out=out, in_=in_, func=func, **kw), reads=rd, writes=[out])

    def tt(out, in0, in1, op, eng="dve"):
        S.add(eng, lambda e: e.tensor_tensor(out=out, in0=in0, in1=in1, op=op), reads=[in0, in1], writes=[out])

    def ts(out, in0, s1, s2, op0, op1=None, eng="dve"):
        rd = [in0]
        for s in (s1, s2):
            if s is not None and not isinstance(s, (int, float)):
                rd.append(s)
        if op1 is None:
            S.add(eng, lambda e: e.tensor_scalar(out=out, in0=in0, scalar1=s1, scalar2=None, op0=op0), reads=rd, writes=[out])
        else:
            S.add(eng, lambda e: e.tensor_scalar(out=out, in0=in0, scalar1=s1, scalar2=s2, op0=op0, op1=op1), reads=rd, writes=[out])

    def stt(out, in0, scalar, in1, op0, op1):
        rd = [in0, in1]
        if not isinstance(scalar, (int, float)):
            rd.append(scalar)
        S.add("dve", lambda e: e.scalar_tensor_tensor(out=out, in0=in0, scalar=scalar, in1=in1, op0=op0, op1=op1), reads=rd, writes=[out])

    def cp(out, in_, eng="dve"):
        if eng == "act":
            act(out, in_, AF.Copy)
        else:
            S.add(eng, lambda e: e.tensor_copy(out=out, in_=in_), reads=[in_], writes=[out])

    def recip(out, in_):
        S.add("dve", lambda e: e.reciprocal(out=out, in_=in_), reads=[in_], writes=[out])

    def memset(ap, val, eng="dve"):
        S.add(eng, lambda e: e.memset(ap, val), writes=[ap])

    def dma(eng, out, in_):
        S.add(eng, lambda e: e.dma_start(out=out, in_=in_), reads=[in_], writes=[out], dma=True)

    def load_w(src2d, nk, ncols):
        slot = WR[st["wr"] % 3]
        st["wr"] += 1
        dst = slot[:, 0:nk, 0:ncols]
        dma("pool", dst, src2d)
        return dst

    def proj_fm(wt, cc, tb, nk=8, src=None, n0=None, n=TB):
        src = hT if src is None else src
        n0 = tb * TB if n0 is None else n0
        ps = psb()
        for k in range(nk):
            mm(ps[:, 0:n], wt[:, k, cc * 128:(cc + 1) * 128], src[:, k, n0:n0 + n], k == 0, k == nk - 1)
        return ps

    def proj_tm(wt, ti):
        ps = psb()
        for k in range(8):
            mm(ps, hT[:, k, ti * 128:(ti + 1) * 128], wt[:, k, :], k == 0, k == 7)
        return ps

    ones_bf = cb[:, CB_ONES:CB_ONES + 128]
    bd_bf = cb[:, CB_BD:CB_BD + 128]
    epsc = sm[:, 0:1]
    onec = sm[:, 1:2]

    def vh(tile_i, h):
        off = tile_i * 512 + h * 64
        return vbuf[:, off:off + 64]

    def cvh(kc, h):
        off = kc * 512 + h * 64
        return cvbuf[:, off:off + 64]

    dma("sp", cf[:], cf_d)
    dma("pool", cb[:], cb_d)
    for dst, srcd in ((flags, flags_d), (rfl, rfl_d), (condsb, cond), (adab, adab_d), (gains, gains_d), (dnorm, dnorm_d),
                      (rtheta, rtheta_d), (rnorm, rnorm_d), (convw, convw_d), (convb, convb_d), (lrub, lrub_d), (h_l0, h_l0_d)):
        dma("sp", dst[:], srcd)
    memset(sm[:, 0:1], EPS)
    memset(sm[:, 1:2], 1.0)
    memset(vbuf[:, VON:VON + 64], 1.0)
    memset(cvbuf[:, CVON:CVON + 64], 1.0)
    act(scond[:], condsb[:], AF.Silu)

    def modulation(l):
        for j in range(12):
            wt = load_w(ada_w[l, j], 8, 512)
            ps = psb()
            for k in range(8):
                mm(ps[0:1, :], scond[:, k:k + 1], wt[:, k, :], k == 0, k == 7)
            row = stage()
            act(row[0:1, :], ps[0:1, :], AF.Copy)
            ps2 = psb()
            for q in range(4):
                mm(ps2[:, q:q + 1], row[0:1, q * 128:(q + 1) * 128], onec[0:1, 0:1])
            tt(modraw[:, l, j * 4:(j + 1) * 4], ps2[:, 0:4], adab[:, l, j * 4:(j + 1) * 4], ALU.add)
        mr = modraw[:, l, :]
        stt(modc[:, l, 0, :], mr[:, 8:16], 1.0, gains[:, l, 0, :], ALU.add, ALU.mult)
        cp(modc[:, l, 1, :], mr[:, 0:8])
        tt(modc[:, l, 2, :], mr[:, 16:24], gains[:, l, 1, :], ALU.mult)
        stt(modc[:, l, 3, :], mr[:, 32:40], 1.0, gains[:, l, 2, :], ALU.add, ALU.mult)
        cp(modc[:, l, 4, :], mr[:, 24:32])
        tt(modc[:, l, 5, :], mr[:, 40:48], gains[:, l, 3, :], ALU.mult)

    modulation(0)

    def resid_norm(l, tb, src, ysrc, ares, xs, hspec, preloaded=False):
        if not preloaded:
            dma("sp", xs, src[tb])
        if ysrc is not None:
            ps = psb()
            for c in range(8):
                sq = stage_bf()
                act(sq, ysrc[:, c, :], AF.Square)
                mm(ps, ones_bf, sq, c == 0, c == 7)
            rs = stageL()
            act(rs, ps, AF.Ln, bias=epsc, scale=1.0 / D)
            act(rs, rs, AF.Exp, scale=-0.5)
            for c in range(8):
                tmp = stage()
                stt(tmp, ysrc[:, c, :], modc[:, l, ares, c:c + 1], rs, ALU.mult, ALU.mult)
                tt(xs[:, c, :], xs[:, c, :], tmp, ALU.add, eng=("pool" if c % 2 == 0 else "dve"))
            dma("sp", yres[tb], xs)
        if hspec is not None:
            ln, an, bn = hspec
            ps = psb()
            for c in range(8):
                sq = stage_bf()
                act(sq, xs[:, c, :], AF.Square)
                mm(ps, ones_bf, sq, c == 0, c == 7)
            rs = stageL()
            act(rs, ps, AF.Ln, bias=epsc, scale=1.0 / D)
            act(rs, rs, AF.Exp, scale=-0.5)
            for c in range(8):
                tmp = stage()
                stt(tmp, xs[:, c, :], modc[:, ln, an, c:c + 1], rs, ALU.mult, ALU.mult)
                act(hT[:, c, tb * TB:(tb + 1) * TB], tmp, AF.Identity, bias=modc[:, ln, bn, c:c + 1])

    def run_pipeline(items, look):
        toks = {}
        n = len(items)
        for i in range(min(look, n)):
            toks[i] = items[i][0]()
        for i in range(n):
            if i + look < n:
                toks[i + look] = items[i + look][0]()
            items[i][1](toks.pop(i))
            pipe_now[0] = i
            while deferred and deferred[0][0] <= i - 2:
                deferred.pop(0)[1]()
        while deferred:
            deferred.pop(0)[1]()

    deferred = []
    pipe_now = [0]

    def defer(fn):
        deferred.append((pipe_now[0] + 1, fn))

    def mixerA(l):
        kT = av(120, 16, BF16, "p (c t) -> p c t", c=4)
        qT = av(136, 16, BF16, "p (c t) -> p c t", c=4)
        EB2 = av(152, 7.5, BF16, "p (h r q) -> p h r q", h=4, r=15)
        EBraw = av(72, 15, F32, "p (h r q) -> p h r q", h=4, r=15)
        ckT = av(87, 4, BF16, "p (c t) -> p c t", c=4)
        Eloc = [av(91 + 5 * i, 5, BF16, "p (j h q) -> p j h q", j=5, h=4) for i in range(2)]
        Ectx = [av(101 + 4 * i, 4, BF16, "p (h k) -> p h k", h=4) for i in range(2)]
        yap = av(109, 2, F32, "p (c q) -> p c q", c=2)
        yas = av(111, 2, F32, "p (c q) -> p c q", c=2)
        Ep = [av(113 + i, 1, BF16) for i in range(4)]
        epi = [0]
        ya = YM[0]
        wk = load_w(w_in[l, 1], 8, 512)
        for c in range(4):
            for tb in range(NTB):
                ps = proj_fm(wk, c, tb)
                sg = stage()
                cp(sg, ps)
                act(kT[:, c, tb * TB:(tb + 1) * TB], sg, AF.Copy)
                dma("sp", o_nakT[l][:, c, tb * TB:(tb + 1) * TB], sg)
        wv = load_w(w_in[l, 2], 8, 512)
        for ti in range(NTT):
            ps = proj_tm(wv, ti)
            sg = stage()
            cp(sg, ps)
            act(vbuf[:, ti * 512:(ti + 1) * 512], sg, AF.Copy)
            dma("sp", o_nav[l][ti], sg)
        wq = load_w(w_in[l, 0], 8, 512)
        for c in range(4):
            for tb in range(NTB):
                ps = proj_fm(wq, c, tb)
                act(qT[:, c, tb * TB:(tb + 1) * TB], ps, AF.Copy)
        dma("pool", ckT, c_nakT[l])
        dma("pool", cvbuf[:, 0:CVON].rearrange("p (k n) -> p k n", k=4), c_nav[l])
        colm = bass.AP(cb.tensor if hasattr(cb, "tensor") else cb, CB_COL, [[CB_N, 128], [0, 60], [1, 64]])
        for hg in range(2):
            for a in range(2):
                for hh in range(4):
                    srcap = bass.AP(rpbp.tensor, rpbp[l, 4 * hg + hh, a, 0:1].offset, [[1, 64], [128, 15], [1, 64]])
                    if os.environ.get("A_NOEB"):
                        memset(EBraw[a * 64:(a + 1) * 64, hh, :, :], 0.0)
                    else:
                        dma("sp", EBraw[a * 64:(a + 1) * 64, hh, :, :], srcap)
            if not os.environ.get("A_NOEB2"):
                act(EB2[:, :, :, :], EBraw[:, :, :, :], AF.Exp)
                e2f = EB2.rearrange("p h r q -> p (h r) q")
                if not os.environ.get("A_NOCOLM"):
                    tt(e2f, e2f, colm, ALU.mult)
            items = []
            for s in range(8):
                for cl in range(2):
                    def f1(s=s, cl=cl):
                        c = 2 * hg + cl
                        Es = []
                        for hs in range(2):
                            hp = hs * 64
                            psS = psb()
                            for kc in range(2):
                                mm(psS[:, kc * 256:(kc + 1) * 256], kT[hp:hp + 64, c, (2 * s + kc) * 128:(2 * s + kc + 1) * 128],
                                   qT[hp:hp + 64, c, s * 256:(s + 1) * 256])
                            E = Ep[epi[0] % 4]
                            epi[0] += 1
                            act(E, psS, AF.Exp, scale=0.125)
                            Es.append(E)
                        return Es

                    def f2(Es, s=s, cl=cl):
                        c = 2 * hg + cl
                        psO = psb()
                        for hs in range(2):
                            h = 2 * c + hs
                            E = Es[hs]
                            for kc in range(2):
                                mm(psO[0:64, hs * 256:(hs + 1) * 256], vh(2 * s + kc, h), E[:, kc * 256:(kc + 1) * 256], kc == 0, kc == 1)
                            for kc in range(2):
                                mm(psO[64:128, hs * 256:(hs + 1) * 256], ones_bf[:, 0:64], E[:, kc * 256:(kc + 1) * 256], kc == 0, kc == 1)
                        rd = stage()
                        recip(rd[64:128, :], psO[64:128, :])
                        for hs in range(2):
                            tt(yap[hs * 64:(hs + 1) * 64, cl, :], psO[0:64, hs * 256:(hs + 1) * 256], rd[64:128, hs * 256:(hs + 1) * 256], ALU.mult)
                    items.append((f1, f2))
                for i in (2 * s, 2 * s + 1):
                    if i < 2:
                        js = [0, 1, 2, 3]
                    elif i > 13:
                        js = [12, 13, 14, 15]
                    else:
                        js = [i - 2, i - 1, i, i + 1, i + 2]

                    def f1(i=i, js=js):
                        nj = len(js)
                        El = Eloc[i % 2]
                        Ec = Ectx[i % 2]
                        for hh in range(4):
                            h = 4 * hg + hh
                            c = h // 2
                            hp = (h % 2) * 64
                            q_ = qT[hp:hp + 64, c, i * 128:(i + 1) * 128]
                            psL = psb()
                            psL2 = psb() if nj == 5 else None
                            psC = psb()
                            for idx, j in enumerate(js):
                                tgt = psL[:, idx * 128:(idx + 1) * 128] if idx < 4 else psL2[:, 0:128]
                                mm(tgt, kT[hp:hp + 64, c, j * 128:(j + 1) * 128], q_)
                            for kc in range(4):
                                mm(psC[:, kc * 128:(kc + 1) * 128], ckT[hp:hp + 64, c, kc * 128:(kc + 1) * 128], q_)
                            act(El[:, 0:4, hh, :], psL.rearrange("p (j q) -> p j q", j=4), AF.Exp, scale=0.125)
                            if nj == 5:
                                act(El[:, 4, hh, :], psL2[:, 0:128], AF.Exp, scale=0.125)
                            act(Ec[:, hh, :], psC, AF.Exp, scale=0.125)
                        for idx, j in enumerate(js):
                            for b in range(2):
                                dr0 = 2 * (j - i) - b + 7
                                assert 0 <= dr0 <= 14
                                tt(El[:, idx, :, b * 64:(b + 1) * 64], El[:, idx, :, b * 64:(b + 1) * 64], EB2[:, :, dr0, ::-1], ALU.mult)
                        if nj == 5:
                            memset(El[0:64, 0, :, 64:128], 0.0)
                            memset(El[0:64, 4, :, 0:64], 0.0)
                            memset(El[64:128, 4, :, :], 0.0)
                        return (El, Ec)

                    def f2(tok, i=i, js=js, s=s):
                        El, Ec = tok
                        psO = psb()
                        for hh in range(4):
                            h = 4 * hg + hh
                            for idx, j in enumerate(js):
                                mm(psO[0:64, hh * 128:(hh + 1) * 128], vh(j, h), El[:, idx, hh, :], idx == 0, False)
                            for kc in range(4):
                                mm(psO[0:64, hh * 128:(hh + 1) * 128], cvh(kc, h), Ec[:, hh, kc * 128:(kc + 1) * 128], False, kc == 3)
                            for idx, j in enumerate(js):
                                mm(psO[64:128, hh * 128:(hh + 1) * 128], ones_bf[:, 0:64], El[:, idx, hh, :], idx == 0, False)
                            for kc in range(4):
                                mm(psO[64:128, hh * 128:(hh + 1) * 128], ones_bf[:, 0:64], Ec[:, hh, kc * 128:(kc + 1) * 128], False, kc == 3)
                        rd = stage()
                        recip(rd[64:128, :], psO[64:128, :])
                        for hh in range(4):
                            tt(yas[(hh % 2) * 64:(hh % 2) * 64 + 64, hh // 2, (i % 2) * 128:(i % 2) * 128 + 128],
                               psO[0:64, hh * 128:(hh + 1) * 128], rd[64:128, hh * 128:(hh + 1) * 128], ALU.mult)
                        if i % 2 == 1:
                            t_ = stage().rearrange("p (c q) -> p c q", c=2)
                            ts(t_, yas, flags[:, 1:2], None, ALU.mult)
                            stt(ya[:, 2 * hg:2 * hg + 2, s * 256:(s + 1) * 256], yap, flags[:, 0:1], t_, ALU.mult, ALU.add)
                    items.append((f1, f2))
            run_pipeline(items, 1)

    def mixerB(l):
        kT = av(120, 16, BF16, "p (c t) -> p c t", c=4)
        qT = av(136, 16, BF16, "p (c t) -> p c t", c=4)
        ckT = av(152, 4, BF16, "p (c t) -> p c t", c=4)
        ropes = av(88, 16, F32, "p (b s t) -> p b s t", b=4, s=2)
        Er = [av(104 + i, 1, BF16) for i in range(4)] + [av(116 + i, 1, BF16) for i in range(2)]
        Eacc = av(118, 2, F32)
        onesf = av(152 + 4, 0.5, F32)
        ybp = av(108, 2, F32)
        ybs = av(110, 2, F32)
        o1n = av(112, 2, F32)
        ybh = av(114, 2, F32)
        yb = YM[1]
        sperm = cf[:, CF_SPERM:CF_SPERM + 128]
        ones128 = ones_bf
        dma("sp", ropes, rope_d)
        dl = stage().rearrange("p (a d) -> p a d", a=8)
        dma("sp", dl[:, 0:4, :], dlam_d[:, l, :, :])
        tt(dl[:, 4, :], dl[:, 0, :], dl[:, 1, :], ALU.mult)
        tt(dl[:, 5, :], dl[:, 2, :], dl[:, 3, :], ALU.mult)
        S.add("dve", lambda e: e.tensor_reduce(out=sm[:, 8:10], in_=dl[:, 4:6, :], axis=mybir.AxisListType.X, op=ALU.add),
              reads=[dl[:, 4:6, :]], writes=[sm[:, 8:10]])
        act(sm[:, 10:12], sm[:, 8:10], AF.Exp)
        tt(sm[:, 12:13], sm[:, 11:12], sm[:, 10:11], ALU.subtract)
        ts(sm[:, 12:13], sm[:, 12:13], -LAM_INIT[l], None, ALU.add)
        neglam = sm[:, 12:13]
        ts(sm[:, 13:14], dnorm[:, l:l + 1], 1.0 - LAM_INIT[l], None, ALU.mult)
        dnc = sm[:, 13:14]

        def rot_proj(wt, dst, outd):
            for c in range(4):
                for tb in range(NTB):
                    ps = proj_fm(wt, c, tb)
                    zs = stage()
                    act(zs, ps, AF.Copy)
                    ps2 = psb()
                    mm(ps2, sperm, zs)
                    t1 = stage()
                    tt(t1, zs, ropes[:, tb, 0, :], ALU.mult)
                    t2 = stage()
                    tt(t2, ps2, ropes[:, tb, 1, :], ALU.mult)
                    tt(t1, t1, t2, ALU.add, eng="pool")
                    act(dst[:, c, tb * TB:(tb + 1) * TB], t1, AF.Copy)
                    if outd is not None:
                        dma("sp", outd[:, c, tb * TB:(tb + 1) * TB], t1)

        wk = load_w(w_in[l, 4], 8, 512)
        rot_proj(wk, kT, o_dfkT[l])
        wv = load_w(w_in[l, 5], 8, 512)
        for ti in range(NTT):
            ps = proj_tm(wv, ti)
            sg = stage()
            cp(sg, ps)
            act(vbuf[:, ti * 512:(ti + 1) * 512], sg, AF.Copy)
            dma("sp", o_dfv[l][ti], sg)
        wq = load_w(w_in[l, 3], 8, 512)
        rot_proj(wq, qT, None)
        dma("pool", ckT, c_dfkT[l])
        dma("pool", cvbuf[:, 0:CVON].rearrange("p (k n) -> p k n", k=4), c_dfv[l])
        psOa, psDa = PS[6][:, :], PS[7][:, :]
        memset(onesf, 1.0)

        ybhs = [ybh, av(118, 2, F32)]
        sqs = [av(156.5 + i, 1, BF16) for i in range(2)]
        fin_i = [0]

        def finish(dst, w, after=None):
            k = fin_i[0] % 2
            yb_ = ybhs[k]
            sq = sqs[k]
            act(sq[:, 0:w], yb_[:, 0:w], AF.Square)

            def tail():
                psN = psb()
                mm(psN[:, 0:w], ones128, sq[:, 0:w])
                rs = stage()
                act(rs[:, 0:w], psN[:, 0:w], AF.Ln, bias=epsc, scale=1.0 / 128)
                act(rs[:, 0:w], rs[:, 0:w], AF.Exp, scale=-0.5)
                stt(dst[:, 0:w], yb_[:, 0:w], dnc, rs[:, 0:w], ALU.mult, ALU.mult)
                if after is not None:
                    after()
            defer(tail)
            fin_i[0] += 1

        def cur_ybh():
            return ybhs[fin_i[0] % 2]

        ei = [0]

        def nextE():
            e_ = Er[ei[0] % 6]
            ei[0] += 1
            return e_

        qz = [av(88 + 8 * i, 8, BF16, "p (c s t) -> p c s t", c=4, s=2) for i in range(2)]
        for i in range(2):
            memset(qz[i], 0.0, eng="pool")
        built = set()

        def build_qz(tb):
            if tb in built:
                return
            built.add(tb)
            for cq in range(4):
                for hs in range(2):
                    cp(qz[tb % 2][hs * 64:(hs + 1) * 64, cq, hs, :], qT[hs * 64:(hs + 1) * 64, cq, tb * TB:(tb + 1) * TB],
                       eng=("pool" if hs == 0 else "act"))

        items = []
        for tb in range(NTB):
            for h in range(4):
                hp = (h % 2) * 64
                for s in (2 * tb, 2 * tb + 1):
                    for comp in range(2):
                        def f1(s=s, comp=comp, h=h, hp=hp, tb=tb):
                            build_qz(tb)
                            cq = comp * 2 + h // 2
                            psS = psb()
                            for kc in range(2):
                                mm(psS[:, kc * 256:(kc + 1) * 256], kT[:, cq, (2 * s + kc) * 128:(2 * s + kc + 1) * 128],
                                   qz[tb % 2][:, cq, h % 2, (s % 2) * 256:(s % 2) * 256 + 256])
                            E = nextE()
                            act(E, psS, AF.Exp, scale=0.125)
                            return E

                        def f2(E, s=s, comp=comp, h=h):
                            psO = psb()
                            for kc in range(2):
                                mm(psO[:, 0:256], vbuf[:, (2 * s + kc) * 512 + h * 128:(2 * s + kc) * 512 + (h + 1) * 128],
                                   E[:, kc * 256:(kc + 1) * 256], kc == 0, kc == 1)
                            for kc in range(2):
                                mm(psO[:, 256:512], ones128, E[:, kc * 256:(kc + 1) * 256], kc == 0, kc == 1)
                            rd = stage()
                            recip(rd[:, 0:256], psO[:, 256:512])
                            if comp == 0:
                                tt(o1n[:, 0:256], psO[:, 0:256], rd[:, 0:256], ALU.mult)
                            else:
                                t_ = stage()
                                tt(t_[:, 0:256], psO[:, 0:256], rd[:, 0:256], ALU.mult)
                                stt(cur_ybh()[:, 0:256], t_[:, 0:256], neglam, o1n[:, 0:256], ALU.mult, ALU.add)
                                finish(ybp[:, (s % 2) * 256:(s % 2) * 256 + 256], 256)
                        items.append((f1, f2))
                for comp in range(2):
                    for kk in range(20):
                        def f1(comp=comp, kk=kk, h=h, hp=hp, tb=tb):
                            build_qz(tb)
                            cq = comp * 2 + h // 2
                            q_ = qz[tb % 2][:, cq, h % 2, :]
                            if kk < 16:
                                k_ = kT[:, cq, kk * 128:(kk + 1) * 128]
                            else:
                                k_ = ckT[:, cq, (kk - 16) * 128:(kk - 15) * 128]
                            psS = psb()
                            mm(psS, k_, q_)
                            E = nextE()
                            act(E, psS, AF.Exp, scale=0.125)
                            return E

                        def f2(E, comp=comp, kk=kk, h=h, tb=tb):
                            if kk < 16:
                                v_ = vbuf[:, kk * 512 + h * 128:kk * 512 + (h + 1) * 128]
                            else:
                                v_ = cvbuf[:, (kk - 16) * 512 + h * 128:(kk - 16) * 512 + (h + 1) * 128]
                            mm(psOa, v_, E, kk == 0, kk == 19)
                            mm(psDa, ones128, E, kk == 0, kk == 19)
                            if kk == 19:
                                rd = stage()
                                recip(rd, psDa)
                                if comp == 0:
                                    tt(o1n, psOa, rd, ALU.mult)
                                else:
                                    t_ = stage()
                                    tt(t_, psOa, rd, ALU.mult)
                                    stt(cur_ybh(), t_, neglam, o1n, ALU.mult, ALU.add)

                                    def blend(h=h, tb=tb):
                                        t2_ = stage()
                                        ts(t2_, ybs, flags[:, 1:2], None, ALU.mult)
                                        stt(yb[:, h, tb * TB:(tb + 1) * TB], ybp, flags[:, 0:1], t2_, ALU.mult, ALU.add)
                                    finish(ybs, 512, after=blend)
                        items.append((f1, f2))
        run_pipeline(items, 3)

    def mixerC(l):
        ktok = av(120, 16, BF16, "p (t n) -> p t n", t=16)
        qT = av(120, 16, BF16, "p (c t) -> p c t", c=4)
        kT = av(136, 16, BF16, "p (c t) -> p c t", c=4)
        Dc = av(152, 2, BF16, "p (h n) -> p h n", h=8)
        XIf = av(154, 1, BF16, "p (c n) -> p c n", c=4)
        XIb = av(155, 1, BF16, "p (c n) -> p c n", c=4)
        kz = [av(156 + i, 1, BF16) for i in range(2)]
        innerD = av(158, 2, BF16, "p (h n) -> p h n", h=8)
        SA = [av(104 + 8 * d, 8, BF16, "p (c n) -> p c n", c=16) for d in range(2)]
        yc = YM[2]
        Srun = [cst2[:, 0, :], cst2[:, 1, :]]
        Gt = [cst2[:, 2, :], cst2[:, 3, :]]
        relf = cf[:, CF_RELF:CF_RELF + 128]
        relb = cf[:, CF_RELB:CF_RELB + 128]
        pos1 = cf[:, CF_POS1:CF_POS1 + 128]
        posb = cf[:, CF_POSB:CF_POSB + 128]
        um = cb[:, CB_UM:CB_UM + 128]
        lmk = cb[:, CB_LM:CB_LM + 128]
        lg = sm[:, 16:32].rearrange("p (d h) -> p d h", d=2)
        lgp = sm[:, 32:40].rearrange("p (d c) -> p d c", d=2)
        act(lg, rtheta[:, l, :, :], AF.Exp)
        ts(lg, lg, -1.0, 1.0, ALU.mult, ALU.add)
        act(lg, lg, AF.Ln)
        for d in range(2):
            cp(lgp[0:64, d, :], lg[0:64, d, 0:8:2])
            cp(lgp[64:128, d, :], lg[64:128, d, 1:8:2])
        for c in range(4):
            act(XIf[:, c, :], pos1, AF.Exp, scale=lgp[:, 0, c:c + 1])
            act(XIb[:, c, :], posb, AF.Exp, scale=lgp[:, 1, c:c + 1])
        for h in range(8):
            a1 = stage()
            act(a1[:, 0:128], relf, AF.Exp, scale=lg[:, 0, h:h + 1])
            tt(a1[:, 0:128], a1[:, 0:128], um, ALU.mult)
            act(a1[:, 128:256], relb, AF.Exp, scale=lg[:, 1, h:h + 1])
            tt(a1[:, 128:256], a1[:, 128:256], lmk, ALU.mult)
            tt(Dc[:, h, :], a1[:, 0:128], a1[:, 128:256], ALU.add)
        zf = sm[:, 40:48]
        zb = sm[:, 48:56]
        tt(zf, cf[:, CF_MPOSF:CF_MPOSF + 8], lg[:, 0, :], ALU.mult)
        act(zf, zf, AF.Exp)
        ts(zf, zf, 0.125, None, ALU.mult)
        tt(zb, cf[:, CF_MPOSB:CF_MPOSB + 8], lg[:, 1, :], ALU.mult)
        act(zb, zb, AF.Exp)
        ts(zb, zb, 0.125, None, ALU.mult)
        g4 = sm[:, 56:64].rearrange("p (d c) -> p d c", d=2)
        act(g4, lgp, AF.Exp, scale=128.0)
        for d in range(2):
            src = bass.AP(sm, 56 + 4 * d, [[512, 128], [1, 4], [0, 64]])
            cp(Gt[d].rearrange("p (c e) -> p c e", c=4), src)
        wk = load_w(w_in[l, 7], 8, 512)
        for ti in range(NTT):
            ps = proj_tm(wk, ti)
            act(ktok[:, ti, :], ps, AF.Copy)
        wv = load_w(w_in[l, 8], 8, 512)
        for ti in range(NTT):
            ps = proj_tm(wv, ti)
            act(vbuf[:, ti * 512:(ti + 1) * 512], ps, AF.Copy)
        for d in range(2):
            z_ = zf if d == 0 else zb
            zbc = bass.AP(sm, 40 + 8 * d, [[512, 128], [1, 8], [0, 64]])
            order = list(range(16)) if d == 0 else list(range(15, -1, -1))
            s0 = s_rf0 if d == 0 else s_rb0
            od = o_srf if d == 0 else o_srb
            dma("sp", Srun[d], s0[l])
            cp(SA[d][:, order[0], :], Srun[d], eng="pool")
            for n_, c in enumerate(order):
                kzt = kz[n_ % 2]
                tt(kzt.rearrange("p (h e) -> p h e", h=8), ktok[:, c, :].rearrange("p (h e) -> p h e", h=8), zbc, ALU.mult)
                psP = psb()
                for h in range(8):
                    hp = (h % 2) * 64
                    mm(psP[hp:hp + 64, (h // 2) * 64:(h // 2) * 64 + 64], kzt[:, h * 64:(h + 1) * 64],
                       vbuf[:, c * 512 + h * 64:c * 512 + (h + 1) * 64])
                t_ = stage()
                tt(t_[:, 0:256], Srun[d], Gt[d], ALU.mult)
                se = stage()
                tt(se[:, 0:256], t_[:, 0:256], psP[:, 0:256], ALU.add)
                if (d == 0 and c % 2 == 1) or (d == 1 and c % 2 == 0):
                    dma("sp", od[l][c // 2], se[:, 0:256])
                if n_ < 15:
                    cn = order[n_ + 1]
                    ts(Srun[d], se[:, 0:256], rfl[:, d, cn:cn + 1], None, ALU.mult)
                    cp(SA[d][:, cn, :], Srun[d], eng="pool")
        wq = load_w(w_in[l, 6], 8, 512)
        for c in range(4):
            for tb in range(NTB):
                ps = proj_fm(wq, c, tb)
                act(qT[:, c, tb * TB:(tb + 1) * TB], ps, AF.Copy)
        wk2 = load_w(w_in[l, 7], 8, 512)
        for c in range(4):
            for tb in range(NTB):
                ps = proj_fm(wk2, c, tb)
                act(kT[:, c, tb * TB:(tb + 1) * TB], ps, AF.Copy)
        for c in range(16):
            cs = slice(c * 128, (c + 1) * 128)
            psI = [psb(), psb()]
            for h in range(8):
                hp = (h % 2) * 64
                mm(psI[h % 2][:, (h // 2) * 128:(h // 2 + 1) * 128], kT[hp:hp + 64, h // 2, cs], qT[hp:hp + 64, h // 2, cs])
            for g in range(2):
                tt(innerD[:, g:8:2, :], psI[g].rearrange("p (h n) -> p h n", h=4), Dc[:, g:8:2, :], ALU.mult)
            qx = stage_bf().rearrange("p (c n) -> p c n", c=4)
            tt(qx, qT[:, :, cs], XIf, ALU.mult, eng="pool")
            qy = stage_bf().rearrange("p (c n) -> p c n", c=4)
            tt(qy, qT[:, :, cs], XIb, ALU.mult, eng="pool")
            sbd = stage().bitcast(BF16).rearrange("p (d c n) -> p d c n", d=2, c=4)
            memset(sbd, 0.0, eng="pool")
            for d in range(2):
                for hs in range(2):
                    cp(sbd[hs * 64:(hs + 1) * 64, d, :, hs * 64:(hs + 1) * 64],
                       SA[d][hs * 64:(hs + 1) * 64, c, :].rearrange("p (c e) -> p c e", c=4), eng="pool")
            psO = psb()
            for ch in range(4):
                for hs in range(2):
                    h = 2 * ch + hs
                    mm(psO[hs * 64:(hs + 1) * 64, ch * 128:(ch + 1) * 128], vbuf[:, c * 512 + h * 64:c * 512 + (h + 1) * 64],
                       innerD[:, h, :], True, False)
                mm(psO[:, ch * 128:(ch + 1) * 128], sbd[:, 0, ch, :], qx[:, ch, :], False, False)
                mm(psO[:, ch * 128:(ch + 1) * 128], sbd[:, 1, ch, :], qy[:, ch, :], False, True)
            sq = stage_bf()
            act(sq, psO, AF.Square)
            psN = psb()
            mm(psN, bd_bf, sq)
            rs = stage()
            act(rs, psN, AF.Ln, bias=epsc, scale=1.0 / 64)
            act(rs, rs, AF.Exp, scale=-0.5)
            for ch in range(4):
                stt(yc[:, ch, cs], psO[:, ch * 128:(ch + 1) * 128], rnorm[:, l, ch:ch + 1], rs[:, ch * 128:(ch + 1) * 128], ALU.mult, ALU.mult)
        wg = load_w(w_in[l, 9], 8, 512)
        for ch in range(4):
            for tb in range(NTB):
                ps = proj_fm(wg, ch, tb)
                sg = stage()
                act(sg, ps, AF.Silu)
                tt(yc[:, ch, tb * TB:(tb + 1) * TB], yc[:, ch, tb * TB:(tb + 1) * TB], sg, ALU.mult)

    def mixerD(l):
        lrx = av(120, 8.25, F32)
        abuf = lrx
        xd = av(128.25, 8, F32)
        xdbf = av(136.25, 4, BF16)
        ubuf = av(140.25, 8, F32)
        hf = av(148.25, 8, F32)
        yd = YM[3]
        lw = cvbuf[:, 0:2048].rearrange("p (k c n) -> p k c n", k=4, c=4)
        dma("pool", lw, lruw_d[l])
        wx = load_w(w_in[l, 10], 8, 512)
        wg = load_w(w_in[l, 11], 8, 512)
        cl = sm[:, 64:72].rearrange("p (d c) -> p d c", d=2)
        cl2 = sm[:, 72:80].rearrange("p (d c) -> p d c", d=2)
        for d in range(2):
            act(cl[:, d, :], lrub[:, l, 2 + 3 * d, :], AF.Exp, scale=-1.0)
        act(cl, cl, AF.Ln, bias=onec)
        ts(cl2, cl, -16.0, None, ALU.mult)
        ts(cl, cl, -8.0, None, ALU.mult)
        nfw = sm[:, 80:96].rearrange("p (k c) -> p k c", k=4)
        ts(nfw, convw[:, l, :, :], flags[:, 0:1], -1.0, ALU.mult, ALU.mult)
        memset(lrx[:, 2049:2112], 0.0)
        for ch in range(4):
            memset(lrx[:, 0:1], 0.0)
            for tb in range(NTB):
                ps = proj_fm(wx, ch, tb)
                act(lrx[:, 1 + tb * TB:1 + (tb + 1) * TB], ps, AF.Copy)
            act(xd, lrx[:, 1:2049], AF.Identity, bias=convb[:, l, ch:ch + 1], scale=convw[:, l, 1, ch:ch + 1])
            stt(xd, lrx[:, 0:2048], convw[:, l, 0, ch:ch + 1], xd, ALU.mult, ALU.add)
            stt(xd, lrx[:, 2:2050], convw[:, l, 2, ch:ch + 1], xd, ALU.mult, ALU.add)
            stt(xd, lrx[:, 3:2051], convw[:, l, 3, ch:ch + 1], xd, ALU.mult, ALU.add)
            stt(xd[:, 256:2048:256], lrx[:, 256:2048:256], nfw[:, 0, ch:ch + 1], xd[:, 256:2048:256], ALU.mult, ALU.add)
            stt(xd[:, 255:1792:256], lrx[:, 257:1794:256], nfw[:, 2, ch:ch + 1], xd[:, 255:1792:256], ALU.mult, ALU.add)
            stt(xd[:, 255:1792:256], lrx[:, 258:1795:256], nfw[:, 3, ch:ch + 1], xd[:, 255:1792:256], ALU.mult, ALU.add)
            stt(xd[:, 254:1791:256], lrx[:, 257:1794:256], nfw[:, 3, ch:ch + 1], xd[:, 254:1791:256], ALU.mult, ALU.add)
            cp(xdbf, xd, eng="pool")
            for d in range(2):
                for tb in range(NTB):
                    ts_ = slice(tb * TB, (tb + 1) * TB)
                    psr = psb()
                    mm(psr, lw[:, 2 * d, ch, :], xdbf[:, ts_])
                    act(abuf[:, ts_], psr, AF.Sigmoid, bias=lrub[:, l, 3 * d, ch:ch + 1])
                    psi = psb()
                    mm(psi, lw[:, 2 * d + 1, ch, :], xdbf[:, ts_])
                    act(ubuf[:, ts_], psi, AF.Sigmoid, bias=lrub[:, l, 3 * d + 1, ch:ch + 1])
                act(abuf[:, 0:2048], abuf[:, 0:2048], AF.Exp, scale=cl[:, d, ch:ch + 1])
                tt(ubuf, ubuf, xd, ALU.mult, eng="pool")
                for tb in range(NTB):
                    ts_ = slice(tb * TB, (tb + 1) * TB)
                    e2 = stage()
                    act(e2, abuf[:, ts_], AF.Square)
                    ts(e2, e2, -1.0, 1.0, ALU.mult, ALU.add)
                    act(e2, e2, AF.Ln)
                    act(e2, e2, AF.Exp, scale=0.5)
                    tt(ubuf[:, ts_], ubuf[:, ts_], e2, ALU.mult)
                if d == 0:
                    ts(abuf[:, 256:2048:256], abuf[:, 256:2048:256], flags[:, 1:2], None, ALU.mult)
                    S.add("dve", lambda e, ch=ch: e.tensor_tensor_scan(out=hf, data0=abuf[:, 0:2048], data1=ubuf, initial=h_l0[:, l, 0, ch:ch + 1],
                                                                        op0=ALU.mult, op1=ALU.add),
                          reads=[abuf[:, 0:2048], ubuf, h_l0[:, l, 0, ch:ch + 1]], writes=[hf])
                    cp(hl_sb[:, 0, ch, :], hf[:, 255:2048:256])
                else:
                    ts(abuf[:, 255:1792:256], abuf[:, 255:1792:256], flags[:, 1:2], None, ALU.mult)
                    S.add("dve", lambda e, ch=ch: e.tensor_tensor_scan(out=ubuf[:, ::-1], data0=abuf[:, 2047::-1], data1=ubuf[:, ::-1],
                                                                        initial=h_l0[:, l, 1, ch:ch + 1], op0=ALU.mult, op1=ALU.add),
                          reads=[abuf[:, 0:2048], ubuf, h_l0[:, l, 1, ch:ch + 1]], writes=[ubuf])
                    cp(hl_sb[:, 1, ch, :], ubuf[:, 0:2048:256])
            tt(hf, hf, ubuf, ALU.add, eng="pool")
            for tb in range(NTB):
                ts_ = slice(tb * TB, (tb + 1) * TB)
                psg = proj_fm(wg, ch, tb)
                g = stage()
                act(g, psg, AF.Copy)
                t_ = stage()
                act(t_, psg, AF.Square)
                ts(t_, t_, 0.044715, 1.0, ALU.mult, ALU.add)
                tt(t_, t_, g, ALU.mult)
                act(t_, t_, AF.Sigmoid, scale=2.0 * math.sqrt(2.0 / math.pi))
                tt(t_, t_, g, ALU.mult)
                tt(yd[:, ch, ts_], t_, hf[:, ts_], ALU.mult)
        dma("sp", o_hl[l], hl_sb[:])

    def merge_out(l):
        acc = av(120, 32, F32, "p (c t) -> p c t", c=8)
        Y01 = av(56, 32, BF16, "p (c t) -> p c t", c=8)
        Y23 = av(88, 32, F32, "p (c t) -> p c t", c=8)
        src = xin if l == 0 else yres
        for th in range(2):
            merged = Y01[:, :, th * 1024:(th + 1) * 1024]
            xs = Y23[:, :, th * 512:(th + 1) * 512]
            for half in range(2):
                for m in range(4):
                    gt = load_w(w_in[l, 12 + 2 * m + half], 8, 512)
                    bt = load_w(w_br[l, m, half], 4, 512)
                    for cc in range(4):
                        c8 = half * 4 + cc
                        for t2 in range(2):
                            tb = 2 * th + t2
                            a_ = acc[:, c8, t2 * TB:(t2 + 1) * TB]
                            psG = proj_fm(gt, cc, tb)
                            G = stage()
                            act(G, psG, AF.Sigmoid)
                            psP = proj_fm(bt, cc, tb, nk=4, src=YM[m])
                            if m == 0:
                                tt(a_, psP, G, ALU.mult)
                            else:
                                t_ = stage()
                                tt(t_, psP, G, ALU.mult)
                                tt(a_, a_, t_, ALU.add, eng="pool")
            for c8 in range(8):
                act(merged[:, c8, :], acc[:, c8, :], AF.Copy)
            for half in range(2):
                wt = load_w(w_out[l, half], 8, 512)
                for cc in range(4):
                    for t2 in range(2):
                        ps = proj_fm(wt, cc, 0, src=merged, n0=t2 * TB)
                        cp(acc[:, half * 4 + cc, t2 * TB:(t2 + 1) * TB], ps)
            for t2 in range(2):
                resid_norm(l, 2 * th + t2, src, acc[:, :, t2 * TB:(t2 + 1) * TB], 2, xs, (l, 3, 4))

    def mlp(l, last):
        hid = av(120, 32, BF16, "p (c t) -> p c t", c=8)
        xs = av(120, 16, F32, "p (c t) -> p c t", c=8)
        for jg in range(4):
            if jg == 2 and not last:
                modulation(l + 1)
            for jh in range(2):
                wt = load_w(w1[l, 2 * jg + jh], 8, 512)
                for jj in range(4):
                    for tb in range(NTB):
                        ps = proj_fm(wt, jj, tb)
                        r = stage()
                        act(r, ps, AF.Relu)
                        tt(hid[:, jh * 4 + jj, tb * TB:(tb + 1) * TB], r, r, ALU.mult, eng="pool")
            for chh in range(2):
                wt2 = load_w(w2[l, jg, chh], 8, 512)
                for cc in range(4):
                    for tb in range(NTB):
                        ps = proj_fm(wt2, cc, tb, src=hid)
                        dst = ymlp[:, chh * 4 + cc, tb * TB:(tb + 1) * TB]
                        if jg == 0:
                            act(dst, ps, AF.Copy)
                        else:
                            tt(dst, dst, ps, ALU.add)
        xsb = [xs, av(136, 16, F32, "p (c t) -> p c t", c=8)]
        dma("sp", xsb[0], yres[0])
        dma("sp", xsb[1], yres[1])
        for tb in range(NTB):
            resid_norm(l, tb, yres, ymlp[:, :, tb * TB:(tb + 1) * TB], 5, xsb[tb % 2], None if last else (l + 1, 0, 1), preloaded=True)
            if tb + 2 < NTB:
                dma("sp", xsb[tb % 2], yres[tb + 2])

    xs0 = av(120, 16, F32, "p (c t) -> p c t", c=8)
    for tb in range(NTB):
        resid_norm(0, tb, xin, None, 0, xs0, (0, 0, 1))
    for l in range(depth):
        if dbg and l == 0:
            dma("pool", dbg_h, hT)
        if stop == "mod":

            if os.environ.get("DBGMOD", "1") == "1":
                dma("sp", dbg_mod, modraw[:, 0:2, :])
            break
        if dbg and l == 0:
            pass
        S.marks = getattr(S, "marks", [])
        for nm_, fn_, m_ in (("A", mixerA, 0), ("B", mixerB, 1), ("C", mixerC, 2), ("D", mixerD, 3)):
            S.marks.append(("L%d %s" % (l, nm_), len(S.streams["pe"])))
            st["nps"] = 6 if nm_ == "B" else 8
            if nm_ in parts:
                fn_(l)
            else:
                memset(YM[m_], 0.0)
        S.marks.append(("L%d merge" % l, len(S.streams["pe"])))
        if dbg and l == 0:
            for m_ in range(4):
                dma("pool", dbg_y[m_], YM[m_])
        st["nps"] = 8
        merge_out(l)
        S.marks.append(("L%d mlp" % l, len(S.streams["pe"])))
        mlp(l, l == depth - 1)
        S.marks.append(("L%d end" % l, len(S.streams["pe"])))
    S.emit()
    return S


def _consts():
    m = np.arange(128)[:, None].astype(np.float32)
    n = np.arange(128)[None, :].astype(np.float32)
    cf = np.zeros((128, CF_N), np.float32)
    cf[:, CF_RELF:CF_RELF + 128] = np.maximum(n - m, 0)
    cf[:, CF_RELB:CF_RELB + 128] = np.maximum(m - n, 0)
    cf[:, CF_POS1:CF_POS1 + 128] = n + 1
    cf[:, CF_POSB:CF_POSB + 128] = 128 - n
    sp = np.zeros((128, 128), np.float32)
    for d in range(128):
        dd = d % 64
        base = d - dd
        if dd % 32 < 16:
            sp[base + dd + 16, d] = -1.0
        else:
            sp[base + dd - 16, d] = 1.0
    cf[:, CF_SPERM:CF_SPERM + 128] = sp
    cf[:, CF_MPOSF:CF_MPOSF + 8] = 127 - m
    cf[:, CF_MPOSB:CF_MPOSB + 8] = m
    cb = np.zeros((128, CB_N), np.float32)
    cb[:, CB_UM:CB_UM + 128] = 0.125 * (n >= m)
    cb[:, CB_LM:CB_LM + 128] = 0.125 * (m > n)
    cb[:, CB_BD:CB_BD + 128] = ((np.arange(128)[:, None] // 64) == (np.arange(128)[None, :] // 64))
    kc = np.arange(128)[:, None] % 64
    qc = 63 - np.arange(64)[None, :]
    cs = np.clip(qc - 8, 0, 48)
    cb[:, CB_COL:CB_COL + 64] = (kc >= cs) & (kc < cs + 16)
    cb[:, CB_ONES:CB_ONES + 128] = 1.0
    return cf, cb


def _rope_tables():
    t = np.arange(T)
    inv = (np.float32(10000.0) ** (-np.arange(16, dtype=np.float32) / np.float32(16))).astype(np.float32)
    ang_r = (t // 64).astype(np.float32)[:, None] * inv
    ang_c = (t % 64).astype(np.float32)[:, None] * inv
    ang = np.zeros((128, T), np.float32)
    for p in range(128):
        d = p % 64
        ang[p] = (ang_r if d < 32 else ang_c)[:, d % 16]
    rope = np.zeros((128, NTB, 2, TB), np.float32)
    rope[:, :, 0, :] = np.cos(ang).astype(np.float32).reshape(128, NTB, TB)
    rope[:, :, 1, :] = np.sin(ang).astype(np.float32).reshape(128, NTB, TB)
    return rope


def _tile(w):
    L, K, N = w.shape
    return np.ascontiguousarray(w.reshape(L, K // 128, 128, N // 512, 512).transpose(0, 3, 2, 1, 4))


def _fm(v):
    k = v.shape[-1] // 128
    r = v.reshape(v.shape[:-1] + (k, 128))
    return np.ascontiguousarray(np.moveaxis(r, -1, 0))


def _prep(I):
    f32 = np.float32
    A = lambda x: np.ascontiguousarray(x, dtype=f32)
    cf, cb = _consts()
    shared = {
        "ada_w": _tile(A(I["ada_w"])), "w_in": _tile(A(I["w_in"])),
        "w_br": np.ascontiguousarray(_tile(A(I["w_branch"]).reshape(DEPTH * 4, 512, D)).reshape(DEPTH, 4, 2, 128, 4, 512)),
        "w_out": _tile(A(I["w_out"])), "w1": _tile(A(I["mlp_w1"])),
        "w2": np.ascontiguousarray(_tile(A(I["mlp_w2"]).reshape(DEPTH * 4, 1024, D)).reshape(DEPTH, 4, 2, 128, 8, 512)),
        "adab": _fm(A(I["ada_b"]).reshape(DEPTH, 6 * D)).reshape(128, DEPTH, 48),
        "gains": np.ascontiguousarray(np.stack([_fm(A(I[k])) for k in ("norm_mix_pre", "norm_mix_post", "norm_ffn_pre", "norm_ffn_post")], axis=2)),
        "dlam": np.ascontiguousarray(np.broadcast_to(np.stack([A(I[k]) for k in ("diff_lq1", "diff_lk1", "diff_lq2", "diff_lk2")], axis=1)[None], (128, DEPTH, 4, 64))),
        "dnorm": np.ascontiguousarray(A(I["diff_norm"]).T),
        "rtheta": np.ascontiguousarray(np.broadcast_to(np.stack([A(I["ret_theta_fwd"]), A(I["ret_theta_bwd"])], axis=1)[None], (128, DEPTH, 2, 8))),
        "rnorm": _fm(A(I["ret_norm"])),
        "convw": _fm(A(I["lru_conv_w"])),
        "convb": _fm(A(I["lru_conv_b"])),
        "lrub": np.ascontiguousarray(np.stack([_fm(A(I[k])) for k in ("lru_ba_fwd", "lru_bx_fwd", "lru_lam_fwd", "lru_ba_bwd", "lru_bx_bwd", "lru_lam_bwd")], axis=2)),
        "cf": cf, "cb": cb,
    }
    rp = np.zeros((DEPTH, 8, 16, 128), f32)
    rp[:, :, 0:15, 48:79] = A(I["na_rpb"])
    shared["rpbp"] = rp
    lw = np.zeros((DEPTH, 128, 4, 4, 128), f32)
    for ki, k in enumerate(("lru_wa_fwd", "lru_wx_fwd", "lru_wa_bwd", "lru_wx_bwd")):
        W = A(I[k])
        for ch in range(4):
            for nb in range(2):
                lw[:, nb * 64:(nb + 1) * 64, ki, ch, nb * 64:(nb + 1) * 64] = W[:, 2 * ch + nb]
    shared["lruw"] = lw
    rope_s = _rope_tables()
    rope_p = np.zeros_like(rope_s)
    rope_p[:, :, 0, :] = 1.0
    maps = []
    for i in range(8):
        d = dict(shared)
        if i < 4:
            b = i
            x = A(I["x_sample"][b])
            cond = A(I["c"][b])
            d["flags"] = np.tile(np.array([[0.0, 1.0]], f32), (128, 1))
            d["rope"] = rope_s
            d["rfl"] = np.ones((128, 2, 16), f32)
            ck = A(I["cache_na_k"][b])
            d["c_nakT"] = np.ascontiguousarray(ck.reshape(DEPTH, 4, 2, 512, 64).transpose(0, 2, 4, 1, 3)).reshape(DEPTH, 128, 4, 512)
            cv = A(I["cache_na_v"][b])
            d["c_nav"] = np.ascontiguousarray(cv.reshape(DEPTH, 8, 4, 128, 64).transpose(0, 3, 2, 1, 4)).reshape(DEPTH, 128, 4, 512)
            dk = A(I["cache_diff_k"][b])
            d["c_dfkT"] = np.ascontiguousarray(dk.reshape(DEPTH, 2, 2, 2, 512, 64).transpose(0, 3, 5, 1, 2, 4)).reshape(DEPTH, 128, 4, 512)
            dv = A(I["cache_diff_v"][b])
            d["c_dfv"] = np.ascontiguousarray(dv.reshape(DEPTH, 4, 4, 128, 128).transpose(0, 3, 2, 1, 4)).reshape(DEPTH, 128, 4, 512)
            for nm, key in (("s_rf0", "state_ret_fwd"), ("s_rb0", "state_ret_bwd")):
                s_ = A(I[key][b])
                d[nm] = np.ascontiguousarray(s_.reshape(DEPTH, 4, 2, 64, 64).transpose(0, 2, 3, 1, 4)).reshape(DEPTH, 128, 256)
            d["h_l0"] = np.ascontiguousarray(np.stack([_fm(A(I["state_lru_fwd"][b])), _fm(A(I["state_lru_bwd"][b]))], axis=2))
        else:
            j = i - 4
            x = A(I["x_prompt"][8 * j:8 * j + 8]).reshape(T, D)
            cond = A(I["c_ctx"])
            d["flags"] = np.tile(np.array([[1.0, 0.0]], f32), (128, 1))
            d["rope"] = rope_p
            r = np.ones((128, 2, 16), f32)
            r[:, 0, 0::2] = 0.0
            r[:, 1, 1::2] = 0.0
            d["rfl"] = r
            for nm in ("c_nakT", "c_nav", "c_dfkT", "c_dfv"):
                d[nm] = np.zeros((DEPTH, 128, 4, 512), f32)
            d["s_rf0"] = np.zeros((DEPTH, 128, 256), f32)
            d["s_rb0"] = np.zeros((DEPTH, 128, 256), f32)
            d["h_l0"] = np.zeros((128, DEPTH, 2, 4), f32)
        d["xin"] = np.ascontiguousarray(x.reshape(NTB, TB, 8, 128).transpose(0, 3, 2, 1))
        d["cond"] = np.ascontiguousarray(cond.reshape(8, 128).T)
        maps.append(d)
    return maps


def _assemble(res):
    f32 = np.float32

    def yfull(r):
        return np.ascontiguousarray(np.asarray(r["yres"], f32).transpose(0, 3, 2, 1)).reshape(T, D)

    y_s = np.stack([yfull(res[i]) for i in range(4)], axis=0)
    y_p = np.concatenate([yfull(res[i]).reshape(8, 256, D) for i in range(4, 8)], axis=0)
    nak, nav, dfk, dfv, srf, srb, hlf, hlb = [], [], [], [], [], [], [], []
    for i in range(4, 8):
        r = res[i]
        a = np.asarray(r["o_nakT"], f32).reshape(DEPTH, 2, 64, 4, 8, 256)
        nak.append(a.transpose(4, 0, 3, 1, 5, 2).reshape(8, DEPTH, 8, 256, 64))
        a = np.asarray(r["o_nav"], f32).reshape(DEPTH, 8, 256, 8, 64)
        nav.append(a.transpose(1, 0, 3, 2, 4))
        a = np.asarray(r["o_dfkT"], f32).reshape(DEPTH, 2, 64, 2, 2, 8, 256)
        dfk.append(a.transpose(5, 0, 3, 4, 1, 6, 2).reshape(8, DEPTH, 2, 4, 256, 64))
        a = np.asarray(r["o_dfv"], f32).reshape(DEPTH, 8, 256, 4, 128)
        dfv.append(a.transpose(1, 0, 3, 2, 4))
        for lst, key in ((srf, "o_srf"), (srb, "o_srb")):
            a = np.asarray(r[key], f32).reshape(DEPTH, 8, 2, 64, 4, 64)
            lst.append(a.transpose(1, 0, 4, 2, 3, 5).reshape(8, DEPTH, 8, 64, 64))
        a = np.asarray(r["o_hl"], f32)
        hlf.append(a[:, :, 0].transpose(3, 0, 2, 1).reshape(8, DEPTH, 512))
        hlb.append(a[:, :, 1].transpose(3, 0, 2, 1).reshape(8, DEPTH, 512))
    cat = lambda x: np.ascontiguousarray(np.concatenate(x, axis=0), dtype=f32)
    return (np.ascontiguousarray(y_p, dtype=f32), np.ascontiguousarray(y_s, dtype=f32), cat(nak), cat(nav), cat(dfk), cat(dfv),
            cat(srf), cat(srb), cat(hlf), cat(hlb))


_PROG = {}


def kernel(**inputs):
    if "p" not in _PROG:
        _PROG["p"] = build()
    S = _PROG["p"]
    maps = _prep(inputs)
    res = run_bass_kernel_spmd(S.nc, maps, core_ids=list(range(8)))
    return _assemble(res.results)
```
